# Optimizing a Trainium2 kernel written in Bass

```python
import jax, jax.numpy as jnp
from jax import lax
import numpy as np


D_MODEL = 1024
BATCH = 16
SEQ = 4096
DEPTH = 4

CTX_LEN = 256
GRID_W = 64
W_CONV = 512
W_FNO = 512
N_FGROUPS = 4
N_HEADS = 8
HEAD_DIM = 64
W_ATTN = N_HEADS * HEAD_DIM
WIN_H = 8
WIN_W = 16
Q_BLOCK_W = 16
K_BLOCK_W = 2 * WIN_W
N_BRANCH = 3
EPS = 1e-6
NEG_INF = -1e30
D_IN = 4 * W_CONV + 2 * W_FNO + 4 * W_ATTN + N_BRANCH * D_MODEL
KV_LO = 4 * W_CONV + 2 * W_FNO + W_ATTN
KV_HI = KV_LO + 2 * W_ATTN

kernel_name = 'hybrid_conv_fourier_natten_dit'


def rmsnorm(x, g):
    xf = x.astype(jnp.float32)
    y = xf * lax.rsqrt(jnp.mean(xf * xf, axis=-1, keepdims=True) + EPS)
    return (y * g.astype(jnp.float32)).astype(x.dtype)


def split_proj(p):
    sizes = (W_CONV,) * 4 + (W_FNO,) * 2 + (W_ATTN,) * 4 + (D_MODEL,) * N_BRANCH
    points = np.cumsum(sizes)[:-1].tolist()
    return jnp.split(p, points, axis=-1)


def heads(t):
    return t.reshape(t.shape[0], t.shape[1], N_HEADS, HEAD_DIM)


def short_conv(u, w_conv, b_conv):
    up = jnp.pad(u, ((0, 0), (1, 1), (0, 0)))
    return up[:, :-2] * w_conv[0] + up[:, 1:-1] * w_conv[1] + up[:, 2:] * w_conv[2] + b_conv


def fourier_mix(u):
    b, l, w = u.shape
    ug = u.reshape(b, l, N_FGROUPS, w // N_FGROUPS).astype(jnp.float32)
    y = jnp.fft.fft2(ug, axes=(1, 3), norm='ortho').real
    return y.reshape(b, l, w).astype(u.dtype)


def context_attention(q, k, v):
    s = jnp.einsum('bqhd,bkhd->bhqk', q, k).astype(jnp.float32) * (HEAD_DIM ** -0.5)
    p = jax.nn.softmax(s, axis=-1).astype(v.dtype)
    o = jnp.einsum('bhqk,bkhd->bqhd', p, v)
    return o.reshape(o.shape[0], o.shape[1], W_ATTN)


def neighbourhood_attention(q, k, v, k_ctx, v_ctx, rpb):
    b, s, h, dh = q.shape
    rows = s // GRID_W
    kh = min(WIN_H, rows)
    n_cb = GRID_W // Q_BLOCK_W
    scale = HEAD_DIM ** -0.5
    q = q.reshape(b, rows, n_cb, Q_BLOCK_W, h, dh)
    k = k.reshape(b, rows, GRID_W, h, dh)
    v = v.reshape(b, rows, GRID_W, h, dh)
    qcol = np.arange(GRID_W).reshape(n_cb, Q_BLOCK_W)
    kcs = np.clip(np.arange(n_cb) * Q_BLOCK_W - WIN_W // 2, 0, GRID_W - K_BLOCK_W)
    kcol = kcs[:, None] + np.arange(K_BLOCK_W)[None, :]
    cs = np.clip(qcol - WIN_W // 2, 0, GRID_W - WIN_W)
    col_ok = (kcol[:, None, :] >= cs[..., None]) & (kcol[:, None, :] < cs[..., None] + WIN_W)
    dc_idx = np.clip(kcol[:, None, :] - qcol[..., None] + WIN_W - 1, 0, 2 * WIN_W - 2)
    rpb_col = rpb[:, :, dc_idx]
    mask = col_ok[:, :, None, :]

    def row_block(r):
        rs = jnp.clip(r - kh // 2, 0, rows - kh)
        qr = lax.dynamic_index_in_dim(q, r, axis=1, keepdims=False)
        kb = lax.dynamic_slice_in_dim(k, rs, kh, axis=1)[:, :, kcol]
        vb = lax.dynamic_slice_in_dim(v, rs, kh, axis=1)[:, :, kcol]
        dr_idx = rs + jnp.arange(kh) - r + (WIN_H - 1)
        bias = jnp.take(rpb_col, dr_idx, axis=1).transpose(0, 2, 3, 1, 4).astype(jnp.float32)
        s_loc = jnp.einsum('bnqhd,binchd->bhnqic', qr, kb).astype(jnp.float32) * scale + bias
        s_loc = jnp.where(mask, s_loc, NEG_INF)
        s_ctx = jnp.einsum('bnqhd,bkhd->bhnqk', qr, k_ctx).astype(jnp.float32) * scale
        n_loc = kh * K_BLOCK_W
        sc = jnp.concatenate([s_loc.reshape(b, h, n_cb, Q_BLOCK_W, n_loc), s_ctx], axis=-1)
        p = jax.nn.softmax(sc, axis=-1).astype(v.dtype)
        p_loc = p[..., :n_loc].reshape(b, h, n_cb, Q_BLOCK_W, kh, K_BLOCK_W)
        p_ctx = p[..., n_loc:]
        return (jnp.einsum('bhnqic,binchd->bnqhd', p_loc, vb)
                + jnp.einsum('bhnqk,bkhd->bnqhd', p_ctx, v_ctx))

    o = lax.map(row_block, jnp.arange(rows))
    return o.transpose(1, 0, 2, 3, 4, 5).reshape(b, s, W_ATTN)


def merge_branches(pieces, attn_o, w_conv, b_conv, w_br_a, w_br_f, w_br_c, w_out):
    a_x, a_b, a_c, a_z, f_u, f_z, _q, _k, _v, c_z, g_a, g_f, g_c = pieces
    silu, sig = jax.nn.silu, jax.nn.sigmoid
    y_a = (a_b * short_conv(a_c * a_x, w_conv, b_conv) * silu(a_z)) @ w_br_a
    y_f = (fourier_mix(f_u) * silu(f_z)) @ w_br_f
    y_c = (attn_o * silu(c_z)) @ w_br_c
    return (sig(g_a) * y_a + sig(g_f) * y_f + sig(g_c) * y_c) @ w_out


def hybrid_layer(x, ctx, c, c_ctx, norm_g, w_ada, b_ada, w_in, b_in, w_conv, b_conv,
                 rpb, w_br_a, w_br_f, w_br_c, w_out, update_ctx):
    mod_l = jax.nn.silu(c) @ w_ada + b_ada
    mod_c = jax.nn.silu(c_ctx) @ w_ada + b_ada
    sh_l, sc_l, gt_l = jnp.split(mod_l[:, None, :], 3, axis=-1)
    sh_c, sc_c, gt_c = jnp.split(mod_c, 3, axis=-1)
    h_lat = rmsnorm(x, norm_g) * (1.0 + sc_l) + sh_l
    h_ctx = rmsnorm(ctx, norm_g) * (1.0 + sc_c) + sh_c

    pl = split_proj(h_lat @ w_in + b_in)
    if update_ctx:
        pc = split_proj(h_ctx @ w_in + b_in)
        k_c, v_c = pc[7], pc[8]
    else:
        k_c, v_c = jnp.split(h_ctx @ w_in[:, KV_LO:KV_HI] + b_in[KV_LO:KV_HI], 2, axis=-1)
    k_c, v_c = heads(k_c), heads(v_c)

    o_lat = neighbourhood_attention(heads(pl[6]), heads(pl[7]), heads(pl[8]), k_c, v_c, rpb)
    x_new = x + gt_l * merge_branches(pl, o_lat, w_conv, b_conv, w_br_a, w_br_f, w_br_c, w_out)
    if update_ctx:
        o_ctx = context_attention(heads(pc[6]), k_c, v_c)
        ctx = ctx + gt_c * merge_branches(pc, o_ctx, w_conv, b_conv, w_br_a, w_br_f, w_br_c, w_out)
    return x_new, ctx


def setup_inputs(seed: int = 0) -> dict:
    key = jax.random.key(seed)
    ks = jax.random.split(key, 17)
    f32 = jnp.float32
    nrm = lambda k, shape, s: jax.random.normal(k, shape, f32) * s
    return {
        'x': nrm(ks[0], (BATCH, SEQ, D_MODEL), 1.0),
        'c': nrm(ks[1], (BATCH, D_MODEL), 1.0),
        'ctx': nrm(ks[2], (BATCH, CTX_LEN, D_MODEL), 1.0),
        'c_ctx': nrm(ks[3], (D_MODEL,), 1.0),
        'norm_g': 1.0 + nrm(ks[4], (DEPTH, D_MODEL), 0.05),
        'w_ada': nrm(ks[5], (DEPTH, D_MODEL, 3 * D_MODEL), 0.5 * D_MODEL ** -0.5),
        'b_ada': nrm(ks[6], (DEPTH, 3 * D_MODEL), 0.01),
        'w_in': nrm(ks[7], (DEPTH, D_MODEL, D_IN), D_MODEL ** -0.5),
        'b_in': nrm(ks[8], (DEPTH, D_IN), 0.01),
        'w_conv': nrm(ks[9], (DEPTH, 3, W_CONV), 3 ** -0.5),
        'b_conv': nrm(ks[10], (DEPTH, W_CONV), 0.01),
        'rpb': nrm(ks[11], (DEPTH, N_HEADS, 2 * WIN_H - 1, 2 * WIN_W - 1), 0.1),
        'w_br_a': nrm(ks[12], (DEPTH, W_CONV, D_MODEL), W_CONV ** -0.5),
        'w_br_f': nrm(ks[13], (DEPTH, W_FNO, D_MODEL), W_FNO ** -0.5),
        'w_br_c': nrm(ks[14], (DEPTH, W_ATTN, D_MODEL), W_ATTN ** -0.5),
        'w_out': nrm(ks[15], (DEPTH, D_MODEL, D_MODEL), D_MODEL ** -0.5),
        'final_g': 1.0 + nrm(ks[16], (D_MODEL,), 0.05),
    }


def reference(x, c, ctx, c_ctx, norm_g, w_ada, b_ada, w_in, b_in, w_conv, b_conv,
              rpb, w_br_a, w_br_f, w_br_c, w_out, final_g):
    for l in range(DEPTH):
        x, ctx = hybrid_layer(x, ctx, c, c_ctx, norm_g[l], w_ada[l], b_ada[l], w_in[l], b_in[l],
                              w_conv[l], b_conv[l], rpb[l], w_br_a[l], w_br_f[l], w_br_c[l],
                              w_out[l], update_ctx=(l < DEPTH - 1))
    return rmsnorm(x, final_g)
```

```python
import contextlib
import numpy as np
import ml_dtypes
import concourse.bass as bass
import concourse.mybir as mybir
from concourse.bass_utils import run_bass_kernel_spmd

F32 = mybir.dt.float32
BF16 = mybir.dt.bfloat16
AF = mybir.ActivationFunctionType
ALU = mybir.AluOpType
NPBF = ml_dtypes.bfloat16

D = 1024
SEQ = 4096
DEPTH = 4
NCORES = 8
NB = 2
CTX = 256
GW = 64
NH = 8
HD = 64
WIN_H = 8
WIN_W = 16
DIN = 8192
EPS = 1e-6
NEG = -30000.0
C_AX, C_AB, C_AC, C_AZ, C_FU, C_FZ, C_Q, C_K, C_V, C_CZ, C_GA, C_GF, C_GC = (
    0, 512, 1024, 1536, 2048, 2560, 3072, 3584, 4096, 4608, 5120, 6144, 7168)
V_NG, V_BADA, V_BIN, V_WC, V_BC, NVL = 0, 8, 32, 96, 108, 112


class Res:
    __slots__ = ("name", "lw", "rd")

    def __init__(self, name):
        self.name = name
        self.lw = {}
        self.rd = {}


class Sched:
    def __init__(self, nc, es):
        self.nc = nc
        self.es = es
        self.eng = {"pe": nc.tensor, "act": nc.scalar, "dve": nc.vector, "pool": nc.gpsimd, "sp": nc.sync}
        self.sems = {}
        self.cnt = {}
        self.isdma = {}
        self.waited = {e: {} for e in self.eng}
        for e in ("pe", "act", "dve", "pool"):
            self._sem(e, False)
        self.ninst = 0
        self.nwaits = 0

    def _sem(self, key, isdma):
        if key not in self.sems:
            self.sems[key] = self.es.enter_context(self.nc.semaphore("s_" + str(key)))
            self.cnt[key] = 0
            self.isdma[key] = isdma
        return self.sems[key]

    def _wait(self, e, k, v):
        if self.isdma[k]:
            v = self.cnt[k]
        w = self.waited[e]
        if w.get(k, 0) >= v:
            return
        if e == "pe" and k == "pe":
            return
        self.eng[e].wait_ge(self.sems[k], v)
        w[k] = v
        self.nwaits += 1

    def deps(self, e, reads, writes, partial):
        for r in reads:
            for k, v in r.lw.items():
                self._wait(e, k, v)
        for w_ in writes:
            for k, v in w_.lw.items():
                self._wait(e, k, v)
            for k, v in w_.rd.items():
                self._wait(e, k, v)
        for w_ in partial:
            for k, v in w_.rd.items():
                self._wait(e, k, v)

    def mark(self, ev, reads, writes, partial):
        k, v = ev
        for r in reads:
            if r.rd.get(k, 0) < v:
                r.rd[k] = v
        for w_ in writes:
            w_.lw = {k: v}
            w_.rd = {}
        for w_ in partial:
            if w_.lw.get(k, 0) < v:
                w_.lw[k] = v

    def op(self, e, fn, reads=(), writes=(), partial=(), inc=True):
        self.deps(e, reads, writes, partial)
        ins = fn(self.eng[e])
        self.ninst += 1
        if inc:
            self.cnt[e] += 1
            ins.then_inc(self.sems[e], 1)
            ev = (e, self.cnt[e])
        else:
            ev = (e, self.cnt[e] + 1)
        self.mark(ev, reads, writes, partial)
        return ins

    def dma(self, out, in_, semkey, reads=(), writes=(), partial=(), e="sp"):
        self._sem(semkey, True)
        self.deps(e, reads, writes, partial)
        ins = self.eng[e].dma_start(out=out, in_=in_)
        self.ninst += 1
        self.cnt[semkey] += 16
        ins.then_inc(self.sems[semkey], 16)
        self.mark((semkey, self.cnt[semkey]), reads, writes, partial)
        return ins

    def barrier(self, engines=("pe", "act", "dve", "pool", "sp")):
        for e in engines:
            for k in list(self.sems.keys()):
                if self.cnt[k] > 0:
                    self._wait(e, k, self.cnt[k])


class Buf:
    def __init__(self, ap_fn, name):
        self.t = ap_fn
        self.r = Res(name)


def _dft_pos_table(L, TN, LG):
    m = np.arange(L, dtype=np.float64)
    cosv = np.cos(2 * np.pi * m / L) / np.sqrt(L)
    sinv = -np.sin(2 * np.pi * m / L) / np.sqrt(L)
    l = np.arange(L, dtype=np.int64)
    idx = (l[:, None] * l[None, :]) % L
    ntile = L // TN
    nlc = L // 128
    nlg = nlc // LG
    out = np.empty((ntile, nlg, 128, LG, 2, TN), dtype=NPBF)
    for cs, tab in ((0, cosv), (1, sinv)):
        full = tab[idx].astype(NPBF)
        full = full.reshape(nlg, LG, 128, ntile, TN)
        out[:, :, :, :, cs, :] = full.transpose(3, 0, 2, 1, 4)
    return out


def _chan_table():
    c = np.arange(128, dtype=np.int64)
    idx = (c[:, None] * c[None, :]) % 128
    ang = 2 * np.pi * idx / 128.0
    return np.concatenate([np.cos(ang), np.sin(ang)], axis=1) / np.sqrt(128.0)


def _attn_geometry(seq):
    R = seq // GW
    kh = min(WIN_H, R)

    def rs(r):
        return int(np.clip(r - kh // 2, 0, R - kh))
    subs = []
    pats = {}
    for j in range(R // 2):
        r0 = 2 * j
        a0, a1 = rs(r0) - r0, rs(r0 + 1) - r0
        lo, hi = rs(r0), rs(r0 + 1) + kh - 1
        chunks = list(range(lo // 2, hi // 2 + 1))
        key = (a0, a1)
        deltas = tuple(2 * c - r0 for c in chunks)
        if key not in pats:
            pats[key] = deltas
        assert pats[key] == deltas
        subs.append((key, chunks))
    cnts = {}
    for key, _ in subs:
        cnts[key] = cnts.get(key, 0) + 1
    order = sorted(pats.keys(), key=lambda k: -cnts[k])
    tab_base = {}
    n = 0
    for key in order:
        tab_base[key] = n
        n += len(pats[key])
    return dict(R=R, kh=kh, subs=subs, pats=pats, order=order, tab_base=tab_base, ntab=n)


def _bias_tables(rpb_l, geo):
    kh = geo["kh"]
    out = np.full((geo["ntab"], NH, 128, 128), NEG, dtype=np.float32)
    qc = np.arange(GW)
    kc = np.arange(GW)
    cs = np.clip(qc - WIN_W // 2, 0, GW - WIN_W)
    col_ok = (kc[None, :] >= cs[:, None]) & (kc[None, :] < cs[:, None] + WIN_W)
    dc = np.clip(kc[None, :] - qc[:, None] + WIN_W - 1, 0, 2 * WIN_W - 2)
    for key in geo["order"]:
        a0, a1 = key
        for ci, dl in enumerate(geo["pats"][key]):
            t = geo["tab_base"][key] + ci
            for qr in range(2):
                a = a0 if qr == 0 else a1
                for krl in range(2):
                    rel = dl + krl
                    if not (a <= rel < a + kh):
                        continue
                    dr = rel - qr + WIN_H - 1
                    blk = np.where(col_ok[None], rpb_l[:, dr][:, dc], np.float32(NEG))
                    out[t, :, qr * 64:(qr + 1) * 64, krl * 64:(krl + 1) * 64] = blk
    return out


def build_program(seq, depth):
    geo = _attn_geometry(seq)
    LT = CTX + seq
    nc = bass.Bass("TRN2", target_bir_lowering=False)
    NV = NVL * depth + 8
    x_in = nc.dram_tensor("x_in", [NB, D, LT], F32, kind="ExternalInput").ap()
    cT_in = nc.dram_tensor("cT", [128, 8, 3], F32, kind="ExternalInput").ap()
    vecs_in = nc.dram_tensor("vecs", [128, NV], F32, kind="ExternalInput").ap()
    w_ada = nc.dram_tensor("w_ada", [depth, D, 3 * D], F32, kind="ExternalInput").ap()
    w_in = nc.dram_tensor("w_in", [depth, D, DIN], F32, kind="ExternalInput").ap()
    w_br = nc.dram_tensor("w_br", [depth, 3, 512, D], F32, kind="ExternalInput").ap()
    w_out = nc.dram_tensor("w_out", [depth, D, D], F32, kind="ExternalInput").ap()
    bias_in = nc.dram_tensor("bias_tab", [depth, geo["ntab"] * NH, 128, 128], F32, kind="ExternalInput").ap()
    consts_in = nc.dram_tensor("consts", [128, 512], BF16, kind="ExternalInput").ap()
    LGL = 4
    dftl_in = nc.dram_tensor("dftl", [seq // 512, seq // 128 // LGL, 128, LGL, 2, 512], BF16, kind="ExternalInput").ap()
    dftc_in = nc.dram_tensor("dftc", [1, 1, 128, 2, 2, 256], BF16, kind="ExternalInput").ap()
    y_out = nc.dram_tensor("y", [NB, D, seq], F32, kind="ExternalOutput").ap()
    xs = nc.dram_tensor("xs", [NB, D, LT], F32, kind="Internal").ap()
    hT = nc.dram_tensor("hT", [D, LT], BF16, kind="Internal").ap()
    tT = nc.dram_tensor("tT", [3, 512, LT], BF16, kind="Internal").ap()

    es = contextlib.ExitStack()
    with es:
        S = Sched(nc, es)

        def sb(name, shape, dt):
            return es.enter_context(nc.sbuf_tensor("sb_" + name, shape, dt))

        vecs = sb("vecs", [128, NV], F32); r_vecs = Res("vecs")
        consts = sb("consts", [128, 512], BF16); r_consts = Res("consts")
        ident = consts[:, 0:128]
        ones = consts[:, 128:256]
        csc = consts[:, 256:512]
        epsc = sb("epsc", [128, 1], F32)
        cts = sb("cts", [128, 8, 3], F32); r_cts = Res("cts")
        mod = sb("mod", [128, 24, 3], F32); r_mod = Res("mod")
        gm = sb("gm", [128, 8, 3], F32); r_gm = Res("gm")
        ht = [sb(f"ht{i}", [128, 8, 512], BF16) for i in range(2)]; r_ht = [Res(f"ht{i}") for i in range(2)]
        wst = [sb(f"wst{i}", [128, 2048], F32) for i in range(2)]; r_wst = [Res(f"wst{i}") for i in range(2)]
        rstd = [sb(f"rstd{i}", [128, 512], F32) for i in range(2)]; r_rstd = [Res(f"rstd{i}") for i in range(2)]
        WORKN = 16384
        work = sb("work", [128, WORKN], BF16)
        ARN = 57344
        arena = sb("arena", [128, ARN], BF16)
        psf = [es.enter_context(nc.psum_tensor(f"psf{i}", [128, 512], F32)) for i in range(7)]
        r_psf = [Res(f"psf{i}") for i in range(7)]
        pst = es.enter_context(nc.psum_tensor("pst", [128, 1024], BF16)); r_pst = Res("pst")
        ps_pool = {"gen": list(range(7)), "acc": [3, 4, 5, 6], "o": [5, 6]}
        ps_rr = {"gen": 0, "acc": 0, "o": 0}

        def psum(pool="gen"):
            lst = ps_pool[pool]
            i = lst[ps_rr[pool] % len(lst)]
            ps_rr[pool] += 1
            return psf[i], r_psf[i]

        hbm_res = {}

        def hres(*key):
            if key not in hbm_res:
                hbm_res[key] = Res(str(key))
            return hbm_res[key]

        class Carver:
            def __init__(self, base, total):
                self.base = base
                self.total = total
                self.off = 0

            def take(self, n, name):
                assert self.off + n <= self.total, (name, self.off, n, self.total)
                v = self.base[:, self.off:self.off + n]
                self.off += n
                return v, Res(name)

        def take_f32(carver, n, name):
            v, r = carver.take(2 * n, name)
            return v.bitcast(F32), r

        def carve_xt(carver):
            xs_, rs_ = [], []
            for i in range(2):
                v, r = take_f32(carver, 8 * 512, f"xt{i}")
                xs_.append(v.rearrange("p (k n) -> p k n", k=8))
                rs_.append(r)
            return xs_, rs_

        def carve_f32a(carver, n):
            fs_, rs_ = [], []
            for i in range(n):
                v, r = take_f32(carver, 512, f"f32a{i}")
                fs_.append(v)
                rs_.append(r)
            return fs_, rs_

        rr = {"evac": 0, "wst": 0}

        def load_cast(src2d, kcn, ncols, dst3, r_dst, col0=0):
            src = src2d.rearrange("(kc p) n -> p kc n", p=128)
            piece = max(1, 2048 // kcn)
            piece = min(piece, ncols)
            c = 0
            while c < ncols:
                w = min(piece, ncols - c)
                i = rr["wst"] % 2
                rr["wst"] += 1
                stg = wst[i][:, 0:kcn * w].rearrange("p (k n) -> p k n", k=kcn)
                S.dma(stg, src[:, :, col0 + c:col0 + c + w], f"d_wst{i}", writes=[r_wst[i]])
                S.op("pool", lambda e: e.tensor_copy(out=dst3[:, :, c:c + w], in_=stg),
                     reads=[r_wst[i]], partial=[r_dst])
                c += w

        def inproj(htb, r_htb, N, wv, r_w, mchunk):
            ps, r_ps = psum()
            for kc in range(8):
                S.op("pe", lambda e: e.matmul(ps[:, 0:N], wv[:, kc, mchunk * 128:(mchunk + 1) * 128], htb[:, kc, 0:N],
                                              start=(kc == 0), stop=(kc == 7)),
                     reads=[r_w, r_htb], writes=[r_ps] if kc == 0 else (), partial=[r_ps] if kc else (), inc=(kc == 7))
            return ps, r_ps

        def vcol(l, off, j=0):
            c = l * NVL + off + j
            return vecs[:, c:c + 1]

        def load_h(b, seqoff, t0, N, slot):
            S.dma(ht[slot][:, :, 0:N],
                  hT.rearrange("(k p) n -> p k n", p=128)[:, :, seqoff + t0:seqoff + t0 + N],
                  f"d_ht{slot}", reads=[hres("h", seqoff + t0)], writes=[r_ht[slot]])

        seqs = [("ctx", 0, CTX, 256, 2), ("lat", CTX, seq, 512, 3)]

        S.dma(vecs[:, :], vecs_in[:, :], "d_misc", writes=[r_vecs])
        S.dma(consts[:, :], consts_in[:, :], "d_misc", writes=[r_consts])
        S.dma(cts[:, :, :], cT_in[:, :, :], "d_misc", writes=[r_cts])
        S.op("pool", lambda e: e.memset(epsc[:, :], EPS), partial=[r_consts])
        S.op("act", lambda e: e.activation(out=cts[:, :, :], in_=cts[:, :, :], func=AF.Silu), reads=[r_cts], writes=[r_cts])

        for l in range(depth):
            last = (l == depth - 1)
            xsrc = x_in if l == 0 else xs
            S.barrier()
            ps_pool["gen"] = list(range(7))
            car = Carver(arena, ARN)
            xt, r_xt = carve_xt(car)
            for pc in range(8):
                i = pc % 2
                stg = xt[i][:, :, 0:384]
                S.dma(stg, w_ada[l].rearrange("(kc p) n -> p kc n", p=128)[:, :, pc * 384:(pc + 1) * 384],
                      f"d_xt{i}", writes=[r_xt[i]])
                ps, r_ps = psum()
                for mm in range(3):
                    for kc in range(8):
                        S.op("pe", lambda e: e.matmul(ps[:, mm * 4:mm * 4 + 3], stg[:, kc, mm * 128:(mm + 1) * 128],
                                                      cts[:, kc, :], start=(kc == 0), stop=(kc == 7)),
                             reads=[r_xt[i], r_cts], writes=[r_ps] if (mm == 0 and kc == 0) else (),
                             partial=() if (mm == 0 and kc == 0) else [r_ps], inc=(mm == 2 and kc == 7))
                for mm in range(3):
                    m = pc * 3 + mm
                    S.op("dve", lambda e: e.tensor_scalar(out=mod[:, m, :], in0=ps[:, mm * 4:mm * 4 + 3],
                                                          scalar1=vcol(l, V_BADA, m), scalar2=None, op0=ALU.add),
                         reads=[r_ps, r_vecs], partial=[r_mod])
            for j in range(3):
                S.op("dve", lambda e: e.scalar_tensor_tensor(out=gm[:, :, j], in0=mod[:, 8:16, j], scalar=1.0,
                                                             in1=vecs[:, l * NVL + V_NG:l * NVL + V_NG + 8],
                                                             op0=ALU.add, op1=ALU.mult),
                     reads=[r_mod, r_vecs], partial=[r_gm])

            for b in range(NB):
                mcols = {"ctx": 2, "lat": b}
                act_seqs = seqs
                S.barrier()
                ps_pool["gen"] = list(range(7))
                car = Carver(arena, ARN)
                sq, r_sq = car.take(8 * 512, "sq")
                sq3 = sq.rearrange("p (k n) -> p k n", k=8)
                xt, r_xt = carve_xt(car)
                wcar = Carver(work, WORKN)
                f32a, r_f32a = carve_f32a(wcar, 2)
                ti = 0
                for (sname, soff, L, TN, _) in act_seqs:
                    mc = mcols[sname]
                    for t0 in range(0, L, TN):
                        i = ti % 2
                        ti += 1
                        S.dma(xt[i][:, :, 0:TN],
                              xsrc[b].rearrange("(k p) n -> p k n", p=128)[:, :, soff + t0:soff + t0 + TN],
                              f"d_xt{i}", reads=[hres("x", b, soff + t0)], writes=[r_xt[i]])
                        S.op("act", lambda e: e.activation(out=sq3[:, :, 0:TN], in_=xt[i][:, :, 0:TN], func=AF.Square),
                             reads=[r_xt[i]], writes=[r_sq])
                        ps, r_ps = psum()
                        for kc in range(8):
                            S.op("pe", lambda e: e.matmul(ps[:, 0:TN], ones, sq3[:, kc, 0:TN], start=(kc == 0), stop=(kc == 7)),
                                 reads=[r_sq, r_consts], writes=[r_ps] if kc == 0 else (), partial=[r_ps] if kc else (),
                                 inc=(kc == 7))
                        S.op("act", lambda e: e.activation(out=rstd[i][:, 0:TN], in_=ps[:, 0:TN], func=AF.Sqrt, bias=epsc[:, 0:1], scale=1.0 / D),
                             reads=[r_ps, r_consts], writes=[r_rstd[i]])
                        S.op("dve", lambda e: e.reciprocal(out=rstd[i][:, 0:TN], in_=rstd[i][:, 0:TN]),
                             reads=[r_rstd[i]], writes=[r_rstd[i]])
                        for kc in range(8):
                            fi = kc % 2
                            S.op("dve", lambda e: e.scalar_tensor_tensor(out=f32a[fi][:, 0:TN], in0=xt[i][:, kc, 0:TN],
                                                                         scalar=gm[:, kc, mc:mc + 1], in1=rstd[i][:, 0:TN],
                                                                         op0=ALU.mult, op1=ALU.mult),
                                 reads=[r_xt[i], r_gm, r_rstd[i]], writes=[r_f32a[fi]])
                            S.op("act", lambda e: e.activation(out=ht[i][:, kc, 0:TN], in_=f32a[fi][:, 0:TN], func=AF.Identity,
                                                               bias=mod[:, kc, mc:mc + 1], scale=1.0),
                                 reads=[r_f32a[fi], r_mod], writes=[r_ht[i]] if kc == 0 else (), partial=[r_ht[i]] if kc else ())
                        S.dma(hT.rearrange("(k p) n -> p k n", p=128)[:, :, soff + t0:soff + t0 + TN], ht[i][:, :, 0:TN],
                              f"d_ht{i}", reads=[r_ht[i]], writes=[hres("h", soff + t0)])

                for (sname, soff, L, TN, _) in (act_seqs if not last else act_seqs[1:]):
                    S.barrier()
                    ps_pool["gen"] = [0, 1, 2]
                    car = Carver(arena, ARN)
                    LC = L // 128
                    wfu, r_wfu = car.take(8 * 512, "wfu"); wfu3 = wfu.rearrange("p (k n) -> p k n", k=8)
                    wfz, r_wfz = car.take(8 * 512, "wfz"); wfz3 = wfz.rearrange("p (k n) -> p k n", k=8)
                    AB, r_AB = car.take(LC * 1024, "AB")
                    AB5 = AB.rearrange("p (lc g cs c) -> p lc g cs c", lc=LC, g=4, cs=2)
                    AB3 = AB.rearrange("p (lc x) -> p lc x", lc=LC)
                    LG = LGL if sname == "lat" else 2
                    dsl = []
                    for i in range(2):
                        v, r = car.take(LG * 2 * TN, f"dft{i}")
                        dsl.append((v.rearrange("p (a cs n) -> p a cs n", a=LG, cs=2), r))
                    wcar = Carver(work, WORKN)
                    UT = []
                    for i in range(2):
                        v, r = wcar.take(4 * 512, f"UT{i}")
                        UT.append((v.rearrange("p (g n) -> p g n", g=4), r))
                    szb, r_sz = wcar.take(4 * 512, "sz"); sz3 = szb.rearrange("p (g n) -> p g n", g=4)
                    tfb = []
                    for i in range(2):
                        v, r = wcar.take(4 * 512, f"tf{i}")
                        tfb.append((v.rearrange("p (g n) -> p g n", g=4), r))
                    load_cast(w_in[l], 8, 512, wfu3, r_wfu, col0=C_FU)
                    load_cast(w_in[l], 8, 512, wfz3, r_wfz, col0=C_FZ)
                    ntile = L // TN
                    load_h(b, soff, 0, TN, 0)
                    for tix in range(ntile):
                        t0 = tix * TN
                        i = tix % 2
                        if tix + 1 < ntile:
                            load_h(b, soff, t0 + TN, TN, (tix + 1) % 2)
                        ut, r_ut = UT[i]
                        for g in range(4):
                            ps, r_ps = inproj(ht[i], r_ht[i], TN, wfu3, r_wfu, g)
                            S.op("act", lambda e: e.activation(out=ut[:, g, 0:TN], in_=ps[:, 0:TN], func=AF.Identity,
                                                               bias=vcol(l, V_BIN, C_FU // 128 + g), scale=1.0),
                                 reads=[r_ps, r_vecs], writes=[r_ut] if g == 0 else (), partial=[r_ut] if g else ())
                        for st in range(TN // 128):
                            lc = (t0 // 128) + st
                            for half in range(2):
                                ps, r_ps = psum()
                                for gg in range(2):
                                    g = half * 2 + gg
                                    S.op("pe", lambda e: e.matmul(ps[:, gg * 256:(gg + 1) * 256], ut[:, g, st * 128:(st + 1) * 128], csc,
                                                                  start=True, stop=True),
                                         reads=[r_ut, r_consts], writes=[r_ps] if gg == 0 else (), partial=[r_ps] if gg else (),
                                         inc=(gg == 1))
                                dst = AB3[:, lc, half * 512:(half + 1) * 512]
                                if rr["evac"] % 2 == 0:
                                    S.op("act", lambda e: e.activation(out=dst, in_=ps[:, :], func=AF.Copy), reads=[r_ps], partial=[r_AB])
                                else:
                                    S.op("dve", lambda e: e.tensor_copy(out=dst, in_=ps[:, :]), reads=[r_ps], partial=[r_AB])
                                rr["evac"] += 1
                    dsrc = dftl_in if sname == "lat" else dftc_in
                    nlg = LC // LG
                    di = 0
                    load_h(b, soff, 0, TN, 0)
                    for tix in range(ntile):
                        t0 = tix * TN
                        i = tix % 2
                        if tix + 1 < ntile:
                            load_h(b, soff, t0 + TN, TN, (tix + 1) % 2)
                        accs = [psum("acc") for _ in range(4)]
                        for lg in range(nlg):
                            dv, r_dv = dsl[di % 2]
                            S.dma(dv, dsrc[tix, lg], f"d_dft{di % 2}", writes=[r_dv])
                            di += 1
                            for a in range(LG):
                                lc = lg * LG + a
                                for cs in range(2):
                                    first = (lc == 0 and cs == 0)
                                    lastm = (lc == LC - 1 and cs == 1)
                                    for g in range(4):
                                        ps, r_ps = accs[g]
                                        S.op("pe", lambda e: e.matmul(ps[:, 0:TN], AB5[:, lc, g, cs, :], dv[:, a, cs, 0:TN],
                                                                      start=first, stop=lastm),
                                             reads=[r_AB, r_dv], writes=[r_ps] if first else (), partial=() if first else [r_ps],
                                             inc=(lastm or (a == LG - 1 and cs == 1 and g == 3)))
                        tf, r_tf = tfb[i]
                        for g in range(4):
                            ps, r_ps = inproj(ht[i], r_ht[i], TN, wfz3, r_wfz, g)
                            S.op("act", lambda e: e.activation(out=sz3[:, g, 0:TN], in_=ps[:, 0:TN], func=AF.Silu,
                                                               bias=vcol(l, V_BIN, C_FZ // 128 + g), scale=1.0),
                                 reads=[r_ps, r_vecs], writes=[r_sz] if g == 0 else (), partial=[r_sz] if g else ())
                            psY, r_psY = accs[g]
                            S.op("dve", lambda e: e.tensor_tensor(out=tf[:, g, 0:TN], in0=psY[:, 0:TN], in1=sz3[:, g, 0:TN], op=ALU.mult),
                                 reads=[r_psY, r_sz], writes=[r_tf] if g == 0 else (), partial=[r_tf] if g else ())
                        S.dma(tT[1].rearrange("(k p) n -> p k n", p=128)[:, :, soff + t0:soff + t0 + TN], tf[:, :, 0:TN],
                              f"d_tf{i}", reads=[r_tf], writes=[hres("t", 1, soff + t0)])

                for (sname, soff, L, TN, _) in (act_seqs if not last else act_seqs[1:]):
                    S.barrier()
                    ps_pool["gen"] = list(range(7))
                    car = Carver(arena, ARN)
                    wc, r_wc = car.take(8 * 2048, "wconv"); wc3 = wc.rearrange("p (k n) -> p k n", k=8)
                    ub, r_u = car.take(4 * (L + 2), "u"); u3 = ub.rearrange("p (c n) -> p c n", c=4)
                    wcar = Carver(work, WORKN)
                    tab = []
                    for i in range(2):
                        v, r = wcar.take(4 * 512, f"ta{i}")
                        tab.append((v.rearrange("p (g n) -> p g n", g=4), r))
                    f32a, r_f32a = carve_f32a(wcar, 6)
                    load_cast(w_in[l], 8, 2048, wc3, r_wc, col0=0)
                    S.op("pool", lambda e: e.memset(u3[:, :, 0:1], 0.0), partial=[r_u])
                    S.op("pool", lambda e: e.memset(u3[:, :, L + 1:L + 2], 0.0), partial=[r_u])
                    ntile = L // TN
                    load_h(b, soff, 0, TN, 0)
                    for tix in range(ntile):
                        t0 = tix * TN
                        i = tix % 2
                        if tix + 1 < ntile:
                            load_h(b, soff, t0 + TN, TN, (tix + 1) % 2)
                        for c in range(4):
                            fi = c % 2
                            ps, r_ps = inproj(ht[i], r_ht[i], TN, wc3, r_wc, C_AX // 128 + c)
                            S.op("act", lambda e: e.activation(out=f32a[fi][:, 0:TN], in_=ps[:, 0:TN], func=AF.Identity,
                                                               bias=vcol(l, V_BIN, C_AX // 128 + c), scale=1.0),
                                 reads=[r_ps, r_vecs], writes=[r_f32a[fi]])
                            ps2, r_ps2 = inproj(ht[i], r_ht[i], TN, wc3, r_wc, C_AC // 128 + c)
                            S.op("dve", lambda e: e.scalar_tensor_tensor(out=u3[:, c, 1 + t0:1 + t0 + TN], in0=ps2[:, 0:TN],
                                                                         scalar=vcol(l, V_BIN, C_AC // 128 + c), in1=f32a[fi][:, 0:TN],
                                                                         op0=ALU.add, op1=ALU.mult),
                                 reads=[r_ps2, r_vecs, r_f32a[fi]], partial=[r_u])
                    load_h(b, soff, 0, TN, 0)
                    for tix in range(ntile):
                        t0 = tix * TN
                        i = tix % 2
                        if tix + 1 < ntile:
                            load_h(b, soff, t0 + TN, TN, (tix + 1) % 2)
                        ta, r_ta = tab[i]
                        for c in range(4):
                            A, rA = f32a[0 + 3 * (c % 2)], r_f32a[0 + 3 * (c % 2)]
                            Bz, rB = f32a[1 + 3 * (c % 2)], r_f32a[1 + 3 * (c % 2)]
                            Cg, rC = f32a[2 + 3 * (c % 2)], r_f32a[2 + 3 * (c % 2)]
                            S.op("dve", lambda e: e.tensor_scalar(out=A[:, 0:TN], in0=u3[:, c, t0:t0 + TN],
                                                                   scalar1=vcol(l, V_WC, c * 3 + 0), scalar2=None, op0=ALU.mult),
                                 reads=[r_u, r_vecs], writes=[rA])
                            S.op("dve", lambda e: e.scalar_tensor_tensor(out=A[:, 0:TN], in0=u3[:, c, t0 + 1:t0 + 1 + TN],
                                                                          scalar=vcol(l, V_WC, c * 3 + 1), in1=A[:, 0:TN],
                                                                          op0=ALU.mult, op1=ALU.add),
                                 reads=[r_u, r_vecs, rA], writes=[rA])
                            S.op("dve", lambda e: e.scalar_tensor_tensor(out=A[:, 0:TN], in0=u3[:, c, t0 + 2:t0 + 2 + TN],
                                                                          scalar=vcol(l, V_WC, c * 3 + 2), in1=A[:, 0:TN],
                                                                          op0=ALU.mult, op1=ALU.add),
                                 reads=[r_u, r_vecs, rA], writes=[rA])
                            psz, r_psz = inproj(ht[i], r_ht[i], TN, wc3, r_wc, C_AZ // 128 + c)
                            S.op("act", lambda e: e.activation(out=Bz[:, 0:TN], in_=psz[:, 0:TN], func=AF.Silu,
                                                               bias=vcol(l, V_BIN, C_AZ // 128 + c), scale=1.0),
                                 reads=[r_psz, r_vecs], writes=[rB])
                            psb, r_psb = inproj(ht[i], r_ht[i], TN, wc3, r_wc, C_AB // 128 + c)
                            S.op("dve", lambda e: e.scalar_tensor_tensor(out=Cg[:, 0:TN], in0=psb[:, 0:TN],
                                                                         scalar=vcol(l, V_BIN, C_AB // 128 + c), in1=Bz[:, 0:TN],
                                                                         op0=ALU.add, op1=ALU.mult),
                                 reads=[r_psb, r_vecs, rB], writes=[rC])
                            S.op("dve", lambda e: e.scalar_tensor_tensor(out=ta[:, c, 0:TN], in0=A[:, 0:TN],
                                                                         scalar=vcol(l, V_BC, c), in1=Cg[:, 0:TN],
                                                                         op0=ALU.add, op1=ALU.mult),
                                 reads=[rA, rC, r_vecs], writes=[r_ta] if c == 0 else (), partial=[r_ta] if c else ())
                        S.dma(tT[0].rearrange("(k p) n -> p k n", p=128)[:, :, soff + t0:soff + t0 + TN], ta[:, :, 0:TN],
                              f"d_ta{i}", reads=[r_ta], writes=[hres("t", 0, soff + t0)])

                S.barrier()
                ps_pool["gen"] = [0, 1, 2, 3, 4]
                car = Carver(arena, ARN)
                wkv, r_wkv = car.take(8 * 1024, "wkv"); wkv3 = wkv.rearrange("p (k n) -> p k n", k=8)
                wqz3, r_wqz = wkv3, r_wkv
                KT = {}
                VV = {}
                for (sname, soff, L, TN, _) in act_seqs:
                    v, r = car.take(4 * L, "KT" + sname)
                    KT[sname] = (v.rearrange("p (c n) -> p c n", c=4), r)
                    v, r = car.take((L // 128) * NH * 65, "V" + sname)
                    VV[sname] = (v.rearrange("p (t h d) -> p t h d", t=L // 128, h=NH), r)
                key0 = geo["order"][0]
                nres = len(geo["pats"][key0])
                bres, r_bres = car.take(nres * NH * 128, "bres"); bres3 = bres.rearrange("p (t k) -> p t k", k=128)
                bdyn, r_bdyn = car.take(5 * NH * 128, "bdyn"); bdyn3 = bdyn.rearrange("p (t k) -> p t k", k=128)
                wcar = Carver(work, WORKN)
                qTb, r_qT = wcar.take(4 * 512, "qT"); qT3 = qTb.rearrange("p (c n) -> p c n", c=4)
                PTs = []
                for i in range(2):
                    v, r = wcar.take(7 * 128, f"PT{i}")
                    PTs.append((v, r))
                Ons = []
                for i in range(2):
                    v, r = wcar.take(512, f"On{i}")
                    Ons.append((v, r))
                tcb = []
                for i in range(2):
                    v, r = wcar.take(4 * 512, f"tc{i}")
                    tcb.append((v.rearrange("p (g n) -> p g n", g=4), r))
                f32a, r_f32a = carve_f32a(wcar, 5)
                scz = [f32a[0], f32a[1], f32a[2], f32a[3]]; r_scz = r_f32a[0:4]
                rec, r_rec = f32a[4], r_f32a[4]
                load_cast(w_in[l], 8, 1024, wkv3, r_wkv, col0=C_K)

                def load_bias(tab0, ntabs, dst3, r_dst):
                    tot = ntabs * NH
                    c = 0
                    while c < tot:
                        w = min(16, tot - c)
                        i = rr["wst"] % 2
                        rr["wst"] += 1
                        stg = wst[i][:, 0:w * 128].rearrange("p (t k) -> p t k", k=128)
                        S.dma(stg, bias_in[l, tab0 * NH + c:tab0 * NH + c + w].rearrange("t q k -> q t k"),
                              f"d_wst{i}", writes=[r_wst[i]])
                        S.op("pool", lambda e: e.tensor_copy(out=dst3[:, c:c + w, :], in_=stg), reads=[r_wst[i]], partial=[r_dst])
                        c += w
                load_bias(geo["tab_base"][key0], nres, bres3, r_bres)
                for (sname, soff, L, TN, _) in act_seqs:
                    kt3, r_kt = KT[sname]
                    v4, r_v = VV[sname]
                    S.op("pool", lambda e: e.memset(v4[:, :, :, 64:65], 1.0), partial=[r_v])
                    ntile = L // TN
                    load_h(b, soff, 0, TN, 0)
                    for tix in range(ntile):
                        t0 = tix * TN
                        i = tix % 2
                        if tix + 1 < ntile:
                            load_h(b, soff, t0 + TN, TN, (tix + 1) % 2)
                        for c in range(4):
                            ps, r_ps = inproj(ht[i], r_ht[i], TN, wkv3, r_wkv, c)
                            S.op("act", lambda e: e.activation(out=kt3[:, c, t0:t0 + TN], in_=ps[:, 0:TN], func=AF.Identity,
                                                               bias=vcol(l, V_BIN, C_K // 128 + c), scale=1.0),
                                 reads=[r_ps, r_vecs], partial=[r_kt])
                        for st in range(TN // 128):
                            ps, r_ps = psum()
                            for kc in range(8):
                                S.op("pe", lambda e: e.matmul(ps[:, :], ht[i][:, kc, st * 128:(st + 1) * 128], wkv3[:, kc, 512:1024],
                                                              start=(kc == 0), stop=(kc == 7)),
                                     reads=[r_wkv, r_ht[i]], writes=[r_ps] if kc == 0 else (), partial=[r_ps] if kc else (), inc=(kc == 7))
                            S.op("dve", lambda e: e.tensor_copy(out=v4[:, t0 // 128 + st, :, 0:64],
                                                                in_=ps[:, :].rearrange("p (h d) -> p h d", h=NH)),
                                 reads=[r_ps], partial=[r_v])
                load_cast(w_in[l], 8, 512, wqz3[:, :, 0:512], r_wqz, col0=C_Q)
                load_cast(w_in[l], 8, 512, wqz3[:, :, 512:1024], r_wqz, col0=C_CZ)
                cur_dyn = [None]
                for (sname, soff, L, TN, _) in (act_seqs if not last else act_seqs[1:]):
                    ntile = L // TN
                    kc3, r_kc = KT["ctx"]
                    vc4, r_vc = VV["ctx"]
                    kl3, r_kl = KT[sname]
                    vl4, r_vl = VV[sname]
                    load_h(b, soff, 0, TN, 0)
                    sti = 0
                    for tix in range(ntile):
                        t0 = tix * TN
                        i = tix % 2
                        if tix + 1 < ntile:
                            load_h(b, soff, t0 + TN, TN, (tix + 1) % 2)
                        for c in range(4):
                            ps, r_ps = inproj(ht[i], r_ht[i], TN, wqz3, r_wqz, c)
                            S.op("dve", lambda e: e.tensor_scalar(out=qT3[:, c, 0:TN], in0=ps[:, 0:TN],
                                                                  scalar1=vcol(l, V_BIN, C_Q // 128 + c), scalar2=HD ** -0.5,
                                                                  op0=ALU.add, op1=ALU.mult),
                                 reads=[r_ps, r_vecs], writes=[r_qT] if c == 0 else (), partial=[r_qT] if c else ())
                        for c in range(4):
                            ps, r_ps = inproj(ht[i], r_ht[i], TN, wqz3, r_wqz, 4 + c)
                            S.op("act", lambda e: e.activation(out=scz[c][:, 0:TN], in_=ps[:, 0:TN], func=AF.Silu,
                                                               bias=vcol(l, V_BIN, C_CZ // 128 + c), scale=1.0),
                                 reads=[r_ps, r_vecs], writes=[r_scz[c]])
                        tc, r_tc = tcb[i]
                        for st in range(TN // 128):
                            gsub = t0 // 128 + st
                            chunks = []
                            if sname == "lat":
                                key, chl = geo["subs"][gsub]
                                if key == key0:
                                    bt3, r_bt, tb = bres3, r_bres, 0
                                else:
                                    if cur_dyn[0] != (key,):
                                        load_bias(geo["tab_base"][key], len(chl), bdyn3, r_bdyn)
                                        cur_dyn[0] = (key,)
                                    bt3, r_bt, tb = bdyn3, r_bdyn, 0
                                for ci, ch in enumerate(chl):
                                    chunks.append((kl3, r_kl, vl4, r_vl, ch, (bt3, r_bt, ci)))
                            for ch in range(CTX // 128):
                                chunks.append((kc3, r_kc, vc4, r_vc, ch, None))
                            nch = len(chunks)
                            on, r_on = Ons[sti % 2]
                            for hg in range(2):
                                psO, r_psO = psum("o")
                                for hl in range(4):
                                    h = hg * 4 + hl
                                    c = h // 2
                                    pb = (h % 2) * 64
                                    pt, r_pt = PTs[(sti * 8 + h) % 2]
                                    banks = [psum() for _ in range((nch + 3) // 4)]
                                    for ci, (k3, r_k, v4_, r_v_, ch, bt) in enumerate(chunks):
                                        psS, r_psS = banks[ci // 4]
                                        o = psS[:, (ci % 4) * 128:(ci % 4 + 1) * 128]
                                        firstb = (ci % 4 == 0)
                                        lastb = (ci % 4 == 3) or (ci == nch - 1)
                                        S.op("pe", lambda e: e.matmul(o, k3[pb:pb + 64, c, ch * 128:(ch + 1) * 128],
                                                                      qT3[pb:pb + 64, c, st * 128:(st + 1) * 128],
                                                                      start=True, stop=(bt is None)),
                                             reads=[r_k, r_qT], writes=[r_psS] if firstb else (), partial=() if firstb else [r_psS],
                                             inc=(bt is None and lastb))
                                        if bt is not None:
                                            bt3, r_bt, tix_ = bt
                                            S.op("pe", lambda e: e.matmul(o, bt3[:, tix_ * NH + h, :], ident, start=False, stop=True),
                                                 reads=[r_bt, r_consts], partial=[r_psS], inc=lastb)
                                    for bi, (psS, r_psS) in enumerate(banks):
                                        n = min(4, nch - bi * 4) * 128
                                        S.op("act", lambda e: e.activation(out=pt[:, bi * 512:bi * 512 + n], in_=psS[:, 0:n], func=AF.Exp),
                                             reads=[r_psS], writes=[r_pt] if bi == 0 else (), partial=[r_pt] if bi else ())
                                    for ci, (k3, r_k, v4_, r_v_, ch, bt) in enumerate(chunks):
                                        S.op("pe", lambda e: e.matmul(psO[:, hl * 65:hl * 65 + 65], pt[:, ci * 128:(ci + 1) * 128],
                                                                      v4_[:, ch, h, :], start=(ci == 0), stop=(ci == nch - 1)),
                                             reads=[r_pt, r_v_], writes=[r_psO] if (hl == 0 and ci == 0) else (),
                                             partial=() if (hl == 0 and ci == 0) else [r_psO], inc=(ci == nch - 1))
                                pso4 = psO[:, 0:260].rearrange("p (h d) -> p h d", h=4)
                                S.op("dve", lambda e: e.reciprocal(out=rec[:, hg * 4:hg * 4 + 4], in_=pso4[:, :, 64]),
                                     reads=[r_psO], writes=[r_rec] if hg == 0 else (), partial=[r_rec] if hg else ())
                                for hl in range(4):
                                    h = hg * 4 + hl
                                    S.op("dve", lambda e: e.tensor_scalar(out=on[:, h * 64:(h + 1) * 64], in0=psO[:, hl * 65:hl * 65 + 64],
                                                                          scalar1=rec[:, h:h + 1], scalar2=None, op0=ALU.mult),
                                         reads=[r_psO, r_rec], writes=[r_on] if h == 0 else (), partial=[r_on] if h else ())
                            for c in range(4):
                                S.op("pe", lambda e: e.transpose(pst[:, c * 128:(c + 1) * 128], on[:, c * 128:(c + 1) * 128], ident),
                                     reads=[r_on, r_consts], writes=[r_pst] if c == 0 else (), partial=[r_pst] if c else (), inc=(c == 3))
                            for c in range(4):
                                S.op("dve", lambda e: e.scalar_tensor_tensor(out=tc[:, c, st * 128:(st + 1) * 128],
                                                                             in0=pst[:, c * 128:(c + 1) * 128],
                                                                             scalar=vcol(l, V_BIN, C_V // 128 + c),
                                                                             in1=scz[c][:, st * 128:(st + 1) * 128],
                                                                             op0=ALU.add, op1=ALU.mult),
                                     reads=[r_pst, r_vecs, r_scz[c]],
                                     writes=[r_tc] if (st == 0 and c == 0) else (), partial=() if (st == 0 and c == 0) else [r_tc])
                            sti += 1
                        S.dma(tT[2].rearrange("(k p) n -> p k n", p=128)[:, :, soff + t0:soff + t0 + TN], tc[:, :, 0:TN],
                              f"d_tc{i}", reads=[r_tc], writes=[hres("t", 2, soff + t0)])

                S.barrier()
                ps_pool["gen"] = list(range(7))
                car = Carver(arena, ARN)
                wg, r_wg = car.take(8 * 3072, "wg"); wg3 = wg.rearrange("p (k n) -> p k n", k=8)
                wbr = []
                for j in range(3):
                    v, r = car.take(4 * 1024, f"wbr{j}")
                    wbr.append((v.rearrange("p (k n) -> p k n", k=4), r))
                wo, r_wo = car.take(8 * 1024, "wo"); wo3 = wo.rearrange("p (k n) -> p k n", k=8)
                tin = []
                for i in range(2):
                    row = []
                    for j in range(3):
                        v, r = car.take(4 * 512, f"tin{i}{j}")
                        row.append((v.rearrange("p (k n) -> p k n", k=4), r))
                    tin.append(row)
                wcar = Carver(work, WORKN)
                Gb, r_G = wcar.take(8 * 512, "G"); G3 = Gb.rearrange("p (k n) -> p k n", k=8)
                f32a, r_f32a = carve_f32a(wcar, 3)
                xcs, r_xcs = carve_f32a(wcar, 2)
                load_cast(w_in[l], 8, 3072, wg3, r_wg, col0=C_GA)
                for j in range(3):
                    load_cast(w_br[l, j], 4, 1024, wbr[j][0], wbr[j][1])
                load_cast(w_out[l], 8, 1024, wo3, r_wo)
                xi = 0
                for (sname, soff, L, TN, _) in (act_seqs if not last else act_seqs[1:]):
                    mc = mcols[sname]
                    ntile = L // TN

                    def load_tile(tix):
                        t0_ = tix * TN
                        i_ = tix % 2
                        load_h(b, soff, t0_, TN, i_)
                        for j in range(3):
                            S.dma(tin[i_][j][0][:, :, 0:TN],
                                  tT[j].rearrange("(k p) n -> p k n", p=128)[:, :, soff + t0_:soff + t0_ + TN],
                                  f"d_tin{i_}{j}", reads=[hres("t", j, soff + t0_)], writes=[tin[i_][j][1]])
                    load_tile(0)
                    for tix in range(ntile):
                        t0 = tix * TN
                        i = tix % 2
                        if tix + 1 < ntile:
                            load_tile(tix + 1)
                        for m in range(8):
                            for j in range(3):
                                tj, r_tj = tin[i][j]
                                wj, r_wj = wbr[j]
                                psy, r_psy = psum()
                                for kc in range(4):
                                    S.op("pe", lambda e: e.matmul(psy[:, 0:TN], wj[:, kc, m * 128:(m + 1) * 128], tj[:, kc, 0:TN],
                                                                  start=(kc == 0), stop=(kc == 3)),
                                         reads=[r_wj, r_tj], writes=[r_psy] if kc == 0 else (), partial=[r_psy] if kc else (), inc=(kc == 3))
                                psg, r_psg = inproj(ht[i], r_ht[i], TN, wg3, r_wg, j * 8 + m)
                                sg, r_sg = f32a[j], r_f32a[j]
                                S.op("act", lambda e: e.activation(out=sg[:, 0:TN], in_=psg[:, 0:TN], func=AF.Sigmoid,
                                                                   bias=vcol(l, V_BIN, C_GA // 128 + j * 8 + m), scale=1.0),
                                     reads=[r_psg, r_vecs], writes=[r_sg])
                                S.op("dve", lambda e: e.tensor_tensor(out=sg[:, 0:TN], in0=psy[:, 0:TN], in1=sg[:, 0:TN], op=ALU.mult),
                                     reads=[r_psy, r_sg], writes=[r_sg])
                            S.op("pool", lambda e: e.tensor_tensor(out=f32a[0][:, 0:TN], in0=f32a[0][:, 0:TN], in1=f32a[1][:, 0:TN], op=ALU.add),
                                 reads=[r_f32a[0], r_f32a[1]], writes=[r_f32a[0]])
                            S.op("pool", lambda e: e.tensor_tensor(out=G3[:, m, 0:TN], in0=f32a[0][:, 0:TN], in1=f32a[2][:, 0:TN], op=ALU.add),
                                 reads=[r_f32a[0], r_f32a[2]], writes=[r_G] if m == 0 else (), partial=[r_G] if m else ())
                        for mo in range(8):
                            xs_i = xi % 2
                            xi += 1
                            xc, r_xc = xcs[xs_i][:, 0:TN], r_xcs[xs_i]
                            S.dma(xc, xsrc[b, mo * 128:(mo + 1) * 128, soff + t0:soff + t0 + TN], f"d_xt{xs_i}",
                                  reads=[hres("x", b, soff + t0)], writes=[r_xc])
                            pso, r_pso = psum()
                            for kc in range(8):
                                S.op("pe", lambda e: e.matmul(pso[:, 0:TN], wo3[:, kc, mo * 128:(mo + 1) * 128], G3[:, kc, 0:TN],
                                                              start=(kc == 0), stop=(kc == 7)),
                                     reads=[r_wo, r_G], writes=[r_pso] if kc == 0 else (), partial=[r_pso] if kc else (), inc=(kc == 7))
                            S.op("dve", lambda e: e.scalar_tensor_tensor(out=xc, in0=pso[:, 0:TN], scalar=mod[:, 16 + mo, mc:mc + 1],
                                                                         in1=xc, op0=ALU.mult, op1=ALU.add),
                                 reads=[r_pso, r_mod, r_xc], writes=[r_xc])
                            S.dma(xs[b, mo * 128:(mo + 1) * 128, soff + t0:soff + t0 + TN], xc, f"d_xt{xs_i}",
                                  reads=[r_xc], partial=[hres("xn", b, soff + t0)])
                for (sname, soff, L, TN, _) in (act_seqs if not last else act_seqs[1:]):
                    for t0 in range(0, L, TN):
                        hbm_res[("x", b, soff + t0)] = hres("xn", b, soff + t0)
                        del hbm_res[("xn", b, soff + t0)]

                if last:
                    S.barrier()
                    ps_pool["gen"] = list(range(7))
                    car = Carver(arena, ARN)
                    sq, r_sq = car.take(8 * 512, "sq")
                    sq3 = sq.rearrange("p (k n) -> p k n", k=8)
                    xt, r_xt = carve_xt(car)
                    (sname, soff, L, TN, _) = seqs[1]
                    fgc = NVL * depth
                    for tix in range(L // TN):
                        t0 = tix * TN
                        i = tix % 2
                        S.dma(xt[i][:, :, 0:TN], xs[b].rearrange("(k p) n -> p k n", p=128)[:, :, soff + t0:soff + t0 + TN],
                              f"d_xt{i}", reads=[hres("x", b, soff + t0)], writes=[r_xt[i]])
                        S.op("act", lambda e: e.activation(out=sq3[:, :, 0:TN], in_=xt[i][:, :, 0:TN], func=AF.Square),
                             reads=[r_xt[i]], writes=[r_sq])
                        ps, r_ps = psum()
                        for kc in range(8):
                            S.op("pe", lambda e: e.matmul(ps[:, 0:TN], ones, sq3[:, kc, 0:TN], start=(kc == 0), stop=(kc == 7)),
                                 reads=[r_sq, r_consts], writes=[r_ps] if kc == 0 else (), partial=[r_ps] if kc else (), inc=(kc == 7))
                        S.op("act", lambda e: e.activation(out=rstd[i][:, 0:TN], in_=ps[:, 0:TN], func=AF.Sqrt, bias=epsc[:, 0:1], scale=1.0 / D),
                             reads=[r_ps, r_consts], writes=[r_rstd[i]])
                        S.op("dve", lambda e: e.reciprocal(out=rstd[i][:, 0:TN], in_=rstd[i][:, 0:TN]),
                             reads=[r_rstd[i]], writes=[r_rstd[i]])
                        for kc in range(8):
                            S.op("dve",
                                 lambda e: e.scalar_tensor_tensor(out=xt[i][:, kc, 0:TN], in0=xt[i][:, kc, 0:TN],
                                                                  scalar=vecs[:, fgc + kc:fgc + kc + 1], in1=rstd[i][:, 0:TN],
                                                                  op0=ALU.mult, op1=ALU.mult),
                                 reads=[r_vecs, r_rstd[i], r_xt[i]], writes=[r_xt[i]])
                        S.dma(y_out[b].rearrange("(k p) n -> p k n", p=128)[:, :, t0:t0 + TN], xt[i][:, :, 0:TN],
                              f"d_xt{i}", reads=[r_xt[i]], writes=[hres("y", b, t0)])
        S.barrier(engines=("sp",))
        print("program built: ninst", S.ninst, "nwaits", S.nwaits, "nsems", len(S.sems))
    return nc


_CACHE = {}


def _consts(seq):
    key = ("c", seq)
    if key not in _CACHE:
        cst = np.zeros((128, 512), dtype=NPBF)
        cst[:, 0:128] = np.eye(128).astype(NPBF)
        cst[:, 128:256] = np.ones((128, 128)).astype(NPBF)
        cst[:, 256:512] = _chan_table().astype(NPBF)
        dftl = _dft_pos_table(seq, 512, 4)
        dftc = _dft_pos_table(CTX, 256, 2)
        _CACHE[key] = (cst, dftl, dftc)
    return _CACHE[key]


def kernel(x, c, ctx, c_ctx, norm_g, w_ada, b_ada, w_in, b_in, w_conv, b_conv,
           rpb, w_br_a, w_br_f, w_br_c, w_out, final_g):
    x = np.asarray(x, dtype=np.float32)
    depth = int(np.asarray(w_in).shape[0])
    seq = int(x.shape[1])
    B = int(x.shape[0])
    assert B == NB * NCORES
    geo = _attn_geometry(seq)
    cst, dftl, dftc = _consts(seq)
    f = lambda a: np.ascontiguousarray(np.asarray(a, dtype=np.float32))
    c, ctx, c_ctx = f(c), f(ctx), f(c_ctx)
    norm_g, b_ada, b_in, w_conv, b_conv, final_g = f(norm_g), f(b_ada), f(b_in), f(w_conv), f(b_conv), f(final_g)
    w_ada, w_in, w_out = f(w_ada), f(w_in), f(w_out)
    w_br = np.ascontiguousarray(np.stack([f(w_br_a), f(w_br_f), f(w_br_c)], axis=1))
    rpb = f(rpb)
    NV = NVL * depth + 8
    vecs = np.zeros((128, NV), dtype=np.float32)
    for l in range(depth):
        o = l * NVL
        vecs[:, o + V_NG:o + V_NG + 8] = norm_g[l].reshape(8, 128).T
        vecs[:, o + V_BADA:o + V_BADA + 24] = b_ada[l].reshape(24, 128).T
        vecs[:, o + V_BIN:o + V_BIN + 64] = b_in[l].reshape(64, 128).T
        vecs[:, o + V_WC:o + V_WC + 12] = w_conv[l].reshape(3, 4, 128).transpose(2, 1, 0).reshape(128, 12)
        vecs[:, o + V_BC:o + V_BC + 4] = b_conv[l].reshape(4, 128).T
    vecs[:, NVL * depth:NVL * depth + 8] = final_g.reshape(8, 128).T
    bias_tab = np.stack([_bias_tables(rpb[l], geo).reshape(geo["ntab"] * NH, 128, 128) for l in range(depth)], axis=0)
    xall = np.concatenate([ctx.transpose(0, 2, 1), x.transpose(0, 2, 1)], axis=2)
    key = ("nc", seq, depth)
    if key not in _CACHE:
        _CACHE[key] = build_program(seq, depth)
    nc = _CACHE[key]
    in_maps = []
    for core in range(NCORES):
        b0 = core * NB
        cT = np.stack([c[b0], c[b0 + 1], c_ctx], axis=1)
        cT = np.ascontiguousarray(cT.reshape(8, 128, 3).transpose(1, 0, 2))
        in_maps.append({
            "x_in": np.ascontiguousarray(xall[b0:b0 + NB]),
            "cT": cT, "vecs": vecs, "w_ada": w_ada, "w_in": w_in, "w_br": w_br, "w_out": w_out,
            "bias_tab": bias_tab, "consts": cst, "dftl": dftl, "dftc": dftc,
        })
    res = run_bass_kernel_spmd(nc, in_maps, core_ids=list(range(NCORES)))
    ys = [np.asarray(r["y"], dtype=np.float32) for r in res.results]
    y = np.concatenate(ys, axis=0)
    return np.ascontiguousarray(y.transpose(0, 2, 1))
```

```python
import contextlib
import numpy as np
import ml_dtypes
import concourse.bass as bass
import concourse.mybir as mybir
from concourse.bass_utils import run_bass_kernel_spmd

F32 = mybir.dt.float32
BF16 = mybir.dt.bfloat16
AF = mybir.ActivationFunctionType
ALU = mybir.AluOpType
NPBF = ml_dtypes.bfloat16

D = 1024
SEQ = 4096
DEPTH = 4
NCORES = 8
NB = 2
CTX = 256
GW = 64
NH = 8
HD = 64
WIN_H = 8
WIN_W = 16
DIN = 8192
EPS = 1e-6
NEG = -30000.0
C_AX, C_AB, C_AC, C_AZ, C_FU, C_FZ, C_Q, C_K, C_V, C_CZ, C_GA, C_GF, C_GC = (
    0, 512, 1024, 1536, 2048, 2560, 3072, 3584, 4096, 4608, 5120, 6144, 7168)
V_NG, V_BADA, V_BIN, V_WC, V_BC, NVL = 0, 8, 32, 96, 108, 112


class Res:
    __slots__ = ("name", "lw", "rd")

    def __init__(self, name):
        self.name = name
        self.lw = {}
        self.rd = {}


class Sched:
    def __init__(self, nc, es):
        self.nc = nc
        self.es = es
        self.eng = {"pe": nc.tensor, "act": nc.scalar, "dve": nc.vector, "pool": nc.gpsimd, "sp": nc.sync}
        self.sems = {}
        self.cnt = {}
        self.isdma = {}
        self.waited = {e: {} for e in self.eng}
        for e in ("pe", "act", "dve", "pool"):
            self._sem(e, False)
        self.ninst = 0
        self.nwaits = 0

    def _sem(self, key, isdma):
        if key not in self.sems:
            self.sems[key] = self.es.enter_context(self.nc.semaphore("s_" + str(key)))
            self.cnt[key] = 0
            self.isdma[key] = isdma
        return self.sems[key]

    def _wait(self, e, k, v):
        if self.isdma[k]:
            v = self.cnt[k]
        w = self.waited[e]
        if w.get(k, 0) >= v:
            return
        if e == "pe" and k == "pe":
            return
        self.eng[e].wait_ge(self.sems[k], v)
        w[k] = v
        self.nwaits += 1

    def deps(self, e, reads, writes, partial):
        for r in reads:
            for k, v in r.lw.items():
                self._wait(e, k, v)
        for w_ in writes:
            for k, v in w_.lw.items():
                self._wait(e, k, v)
            for k, v in w_.rd.items():
                self._wait(e, k, v)
        for w_ in partial:
            for k, v in w_.rd.items():
                self._wait(e, k, v)

    def mark(self, ev, reads, writes, partial):
        k, v = ev
        for r in reads:
            if r.rd.get(k, 0) < v:
                r.rd[k] = v
        for w_ in writes:
            w_.lw = {k: v}
            w_.rd = {}
        for w_ in partial:
            if w_.lw.get(k, 0) < v:
                w_.lw[k] = v

    def op(self, e, fn, reads=(), writes=(), partial=(), inc=True):
        self.deps(e, reads, writes, partial)
        ins = fn(self.eng[e])
        self.ninst += 1
        if inc:
            self.cnt[e] += 1
            ins.then_inc(self.sems[e], 1)
            ev = (e, self.cnt[e])
        else:
            ev = (e, self.cnt[e] + 1)
        self.mark(ev, reads, writes, partial)
        return ins

    def dma(self, out, in_, semkey, reads=(), writes=(), partial=(), e="sp"):
        self._sem(semkey, True)
        self.deps(e, reads, writes, partial)
        ins = self.eng[e].dma_start(out=out, in_=in_)
        self.ninst += 1
        self.cnt[semkey] += 16
        ins.then_inc(self.sems[semkey], 16)
        self.mark((semkey, self.cnt[semkey]), reads, writes, partial)
        return ins

    def barrier(self, engines=("pe", "act", "dve", "pool", "sp")):
        for e in engines:
            for k in list(self.sems.keys()):
                if self.cnt[k] > 0:
                    self._wait(e, k, self.cnt[k])


class Buf:
    def __init__(self, ap_fn, name):
        self.t = ap_fn
        self.r = Res(name)


def _dft_pos_table(L, TN, LG):
    m = np.arange(L, dtype=np.float64)
    cosv = np.cos(2 * np.pi * m / L) / np.sqrt(L)
    sinv = -np.sin(2 * np.pi * m / L) / np.sqrt(L)
    l = np.arange(L, dtype=np.int64)
    idx = (l[:, None] * l[None, :]) % L
    ntile = L // TN
    nlc = L // 128
    nlg = nlc // LG
    out = np.empty((ntile, nlg, 128, LG, 2, TN), dtype=NPBF)
    for cs, tab in ((0, cosv), (1, sinv)):
        full = tab[idx].astype(NPBF)
        full = full.reshape(nlg, LG, 128, ntile, TN)
        out[:, :, :, :, cs, :] = full.transpose(3, 0, 2, 1, 4)
    return out


def _chan_table():
    c = np.arange(128, dtype=np.int64)
    idx = (c[:, None] * c[None, :]) % 128
    ang = 2 * np.pi * idx / 128.0
    return np.concatenate([np.cos(ang), np.sin(ang)], axis=1) / np.sqrt(128.0)


def _attn_geometry(seq):
    R = seq // GW
    kh = min(WIN_H, R)

    def rs(r):
        return int(np.clip(r - kh // 2, 0, R - kh))
    subs = []
    pats = {}
    for j in range(R // 2):
        r0 = 2 * j
        a0, a1 = rs(r0) - r0, rs(r0 + 1) - r0
        lo, hi = rs(r0), rs(r0 + 1) + kh - 1
        chunks = list(range(lo // 2, hi // 2 + 1))
        key = (a0, a1)
        deltas = tuple(2 * c - r0 for c in chunks)
        if key not in pats:
            pats[key] = deltas
        assert pats[key] == deltas
        subs.append((key, chunks))
    cnts = {}
    for key, _ in subs:
        cnts[key] = cnts.get(key, 0) + 1
    order = sorted(pats.keys(), key=lambda k: -cnts[k])
    tab_base = {}
    n = 0
    for key in order:
        tab_base[key] = n
        n += len(pats[key])
    return dict(R=R, kh=kh, subs=subs, pats=pats, order=order, tab_base=tab_base, ntab=n)


def _bias_tables(rpb_l, geo):
    kh = geo["kh"]
    out = np.full((geo["ntab"], NH, 128, 128), NEG, dtype=np.float32)
    qc = np.arange(GW)
    kc = np.arange(GW)
    cs = np.clip(qc - WIN_W // 2, 0, GW - WIN_W)
    col_ok = (kc[None, :] >= cs[:, None]) & (kc[None, :] < cs[:, None] + WIN_W)
    dc = np.clip(kc[None, :] - qc[:, None] + WIN_W - 1, 0, 2 * WIN_W - 2)
    for key in geo["order"]:
        a0, a1 = key
        for ci, dl in enumerate(geo["pats"][key]):
            t = geo["tab_base"][key] + ci
            for qr in range(2):
                a = a0 if qr == 0 else a1
                for krl in range(2):
                    rel = dl + krl
                    if not (a <= rel < a + kh):
                        continue
                    dr = rel - qr + WIN_H - 1
                    blk = np.where(col_ok[None], rpb_l[:, dr][:, dc], np.float32(NEG))
                    out[t, :, qr * 64:(qr + 1) * 64, krl * 64:(krl + 1) * 64] = blk
    return out


def build_program(seq, depth):
    geo = _attn_geometry(seq)
    LT = CTX + seq
    nc = bass.Bass("TRN2", target_bir_lowering=False)
    NV = NVL * depth + 8
    x_in = nc.dram_tensor("x_in", [NB, D, LT], F32, kind="ExternalInput").ap()
    cT_in = nc.dram_tensor("cT", [128, 8, 3], F32, kind="ExternalInput").ap()
    vecs_in = nc.dram_tensor("vecs", [128, NV], F32, kind="ExternalInput").ap()
    w_ada = nc.dram_tensor("w_ada", [depth, D, 3 * D], F32, kind="ExternalInput").ap()
    w_in = nc.dram_tensor("w_in", [depth, D, DIN], F32, kind="ExternalInput").ap()
    w_br = nc.dram_tensor("w_br", [depth, 3, 512, D], F32, kind="ExternalInput").ap()
    w_out = nc.dram_tensor("w_out", [depth, D, D], F32, kind="ExternalInput").ap()
    bias_in = nc.dram_tensor("bias_tab", [depth, geo["ntab"] * NH, 128, 128], F32, kind="ExternalInput").ap()
    consts_in = nc.dram_tensor("consts", [128, 512], BF16, kind="ExternalInput").ap()
    LGL = 4
    dftl_in = nc.dram_tensor("dftl", [seq // 512, seq // 128 // LGL, 128, LGL, 2, 512], BF16, kind="ExternalInput").ap()
    dftc_in = nc.dram_tensor("dftc", [1, 1, 128, 2, 2, 256], BF16, kind="ExternalInput").ap()
    y_out = nc.dram_tensor("y", [NB, D, seq], F32, kind="ExternalOutput").ap()
    xs = nc.dram_tensor("xs", [NB, D, LT], F32, kind="Internal").ap()
    hT = nc.dram_tensor("hT", [D, LT], BF16, kind="Internal").ap()
    tT = nc.dram_tensor("tT", [3, 512, LT], BF16, kind="Internal").ap()

    es = contextlib.ExitStack()
    with es:
        S = Sched(nc, es)

        def sb(name, shape, dt):
            return es.enter_context(nc.sbuf_tensor("sb_" + name, shape, dt))

        vecs = sb("vecs", [128, NV], F32); r_vecs = Res("vecs")
        consts = sb("consts", [128, 512], BF16); r_consts = Res("consts")
        ident = consts[:, 0:128]
        ones = consts[:, 128:256]
        csc = consts[:, 256:512]
        epsc = sb("epsc", [128, 1], F32)
        cts = sb("cts", [128, 8, 3], F32); r_cts = Res("cts")
        mod = sb("mod", [128, 24, 3], F32); r_mod = Res("mod")
        gm = sb("gm", [128, 8, 3], F32); r_gm = Res("gm")
        ht = [sb(f"ht{i}", [128, 8, 512], BF16) for i in range(2)]; r_ht = [Res(f"ht{i}") for i in range(2)]
        wst = [sb(f"wst{i}", [128, 2048], F32) for i in range(2)]; r_wst = [Res(f"wst{i}") for i in range(2)]
        rstd = [sb(f"rstd{i}", [128, 512], F32) for i in range(2)]; r_rstd = [Res(f"rstd{i}") for i in range(2)]
        WORKN = 16384
        work = sb("work", [128, WORKN], BF16)
        ARN = 57344
        arena = sb("arena", [128, ARN], BF16)
        psf = [es.enter_context(nc.psum_tensor(f"psf{i}", [128, 512], F32)) for i in range(7)]
        r_psf = [Res(f"psf{i}") for i in range(7)]
        pst = es.enter_context(nc.psum_tensor("pst", [128, 1024], BF16)); r_pst = Res("pst")
        ps_pool = {"gen": list(range(7)), "acc": [3, 4, 5, 6], "o": [5, 6]}
        ps_rr = {"gen": 0, "acc": 0, "o": 0}

        def psum(pool="gen"):
            lst = ps_pool[pool]
            i = lst[ps_rr[pool] % len(lst)]
            ps_rr[pool] += 1
            return psf[i], r_psf[i]

        hbm_res = {}

        def hres(*key):
            if key not in hbm_res:
                hbm_res[key] = Res(str(key))
            return hbm_res[key]

        class Carver:
            def __init__(self, base, total):
                self.base = base
                self.total = total
                self.off = 0

            def take(self, n, name):
                assert self.off + n <= self.total, (name, self.off, n, self.total)
                v = self.base[:, self.off:self.off + n]
                self.off += n
                return v, Res(name)

        def take_f32(carver, n, name):
            v, r = carver.take(2 * n, name)
            return v.bitcast(F32), r

        def carve_xt(carver):
            xs_, rs_ = [], []
            for i in range(2):
                v, r = take_f32(carver, 8 * 512, f"xt{i}")
                xs_.append(v.rearrange("p (k n) -> p k n", k=8))
                rs_.append(r)
            return xs_, rs_

        def carve_f32a(carver, n):
            fs_, rs_ = [], []
            for i in range(n):
                v, r = take_f32(carver, 512, f"f32a{i}")
                fs_.append(v)
                rs_.append(r)
            return fs_, rs_

        rr = {"evac": 0, "wst": 0, "cast": 0}

        class WSet:
            def __init__(self, n, name):
                self.chunks = [Res(f"{name}{i}") for i in range(n)]

        def cast(out, in_, reads, partial):
            if rr["cast"] % 2 == 0:
                S.op("act", lambda e: e.activation(out=out, in_=in_, func=AF.Copy), reads=reads, partial=partial)
            else:
                S.op("dve", lambda e: e.tensor_copy(out=out, in_=in_), reads=reads, partial=partial)
            rr["cast"] += 1

        def load_cast(src2d, kcn, ncols, dst3, wset, col0=0, dcol0=0):
            src = src2d.rearrange("(kc p) n -> p kc n", p=128)
            piece = max(1, 2048 // kcn)
            piece = min(piece, ncols)
            c = 0
            while c < ncols:
                w = min(piece, ncols - c)
                i = rr["wst"] % 2
                rr["wst"] += 1
                stg = wst[i][:, 0:kcn * w].rearrange("p (k n) -> p k n", k=kcn)
                S.dma(stg, src[:, :, col0 + c:col0 + c + w], f"d_wst{i}", writes=[r_wst[i]])
                cks = wset.chunks[(dcol0 + c) // 128:(dcol0 + c + w + 127) // 128]
                cast(dst3[:, :, dcol0 + c:dcol0 + c + w], stg, [r_wst[i]], cks)
                c += w

        def inproj(htb, r_htb, N, wv, r_w, mchunk):
            ps, r_ps = psum()
            for kc in range(8):
                S.op("pe", lambda e: e.matmul(ps[:, 0:N], wv[:, kc, mchunk * 128:(mchunk + 1) * 128], htb[:, kc, 0:N],
                                              start=(kc == 0), stop=(kc == 7)),
                     reads=[r_w.chunks[mchunk], r_htb], writes=[r_ps] if kc == 0 else (), partial=[r_ps] if kc else (), inc=(kc == 7))
            return ps, r_ps

        def vcol(l, off, j=0):
            c = l * NVL + off + j
            return vecs[:, c:c + 1]

        def load_h(b, seqoff, t0, N, slot):
            S.dma(ht[slot][:, :, 0:N],
                  hT.rearrange("(k p) n -> p k n", p=128)[:, :, seqoff + t0:seqoff + t0 + N],
                  f"d_ht{slot}", reads=[hres("h", seqoff + t0)], writes=[r_ht[slot]])

        seqs = [("ctx", 0, CTX, 256, 2), ("lat", CTX, seq, 512, 3)]

        S.dma(vecs[:, :], vecs_in[:, :], "d_misc", writes=[r_vecs])
        S.dma(consts[:, :], consts_in[:, :], "d_misc", writes=[r_consts])
        S.dma(cts[:, :, :], cT_in[:, :, :], "d_misc", writes=[r_cts])
        S.op("pool", lambda e: e.memset(epsc[:, :], EPS), partial=[r_consts])
        S.op("act", lambda e: e.activation(out=cts[:, :, :], in_=cts[:, :, :], func=AF.Silu), reads=[r_cts], writes=[r_cts])

        for l in range(depth):
            last = (l == depth - 1)
            xsrc = x_in if l == 0 else xs
            S.barrier()
            ps_pool["gen"] = list(range(7))
            car = Carver(arena, ARN)
            xt, r_xt = carve_xt(car)
            for pc in range(8):
                i = pc % 2
                stg = xt[i][:, :, 0:384]
                S.dma(stg, w_ada[l].rearrange("(kc p) n -> p kc n", p=128)[:, :, pc * 384:(pc + 1) * 384],
                      f"d_xt{i}", writes=[r_xt[i]])
                ps, r_ps = psum()
                for mm in range(3):
                    for kc in range(8):
                        S.op("pe", lambda e: e.matmul(ps[:, mm * 4:mm * 4 + 3], stg[:, kc, mm * 128:(mm + 1) * 128],
                                                      cts[:, kc, :], start=(kc == 0), stop=(kc == 7)),
                             reads=[r_xt[i], r_cts], writes=[r_ps] if (mm == 0 and kc == 0) else (),
                             partial=() if (mm == 0 and kc == 0) else [r_ps], inc=(mm == 2 and kc == 7))
                for mm in range(3):
                    m = pc * 3 + mm
                    S.op("dve", lambda e: e.tensor_scalar(out=mod[:, m, :], in0=ps[:, mm * 4:mm * 4 + 3],
                                                          scalar1=vcol(l, V_BADA, m), scalar2=None, op0=ALU.add),
                         reads=[r_ps, r_vecs], partial=[r_mod])
            for j in range(3):
                S.op("dve", lambda e: e.scalar_tensor_tensor(out=gm[:, :, j], in0=mod[:, 8:16, j], scalar=1.0,
                                                             in1=vecs[:, l * NVL + V_NG:l * NVL + V_NG + 8],
                                                             op0=ALU.add, op1=ALU.mult),
                     reads=[r_mod, r_vecs], partial=[r_gm])

            for b in range(NB):
                mcols = {"ctx": 2, "lat": b}
                act_seqs = seqs
                S.barrier()
                ps_pool["gen"] = list(range(7))
                car = Carver(arena, ARN)
                sq, r_sq = car.take(8 * 512, "sq")
                sq3 = sq.rearrange("p (k n) -> p k n", k=8)
                xt, r_xt = carve_xt(car)
                wcar = Carver(work, WORKN)
                f32a, r_f32a = carve_f32a(wcar, 2)
                ti = 0
                for (sname, soff, L, TN, _) in act_seqs:
                    mc = mcols[sname]
                    for t0 in range(0, L, TN):
                        i = ti % 2
                        ti += 1
                        S.dma(xt[i][:, :, 0:TN],
                              xsrc[b].rearrange("(k p) n -> p k n", p=128)[:, :, soff + t0:soff + t0 + TN],
                              f"d_xt{i}", reads=[hres("x", b, soff + t0)], writes=[r_xt[i]])
                        S.op("act", lambda e: e.activation(out=sq3[:, :, 0:TN], in_=xt[i][:, :, 0:TN], func=AF.Square),
                             reads=[r_xt[i]], writes=[r_sq])
                        ps, r_ps = psum()
                        for kc in range(8):
                            S.op("pe", lambda e: e.matmul(ps[:, 0:TN], ones, sq3[:, kc, 0:TN], start=(kc == 0), stop=(kc == 7)),
                                 reads=[r_sq, r_consts], writes=[r_ps] if kc == 0 else (), partial=[r_ps] if kc else (),
                                 inc=(kc == 7))
                        S.op("act", lambda e: e.activation(out=rstd[i][:, 0:TN], in_=ps[:, 0:TN], func=AF.Sqrt, bias=epsc[:, 0:1], scale=1.0 / D),
                             reads=[r_ps, r_consts], writes=[r_rstd[i]])
                        S.op("dve", lambda e: e.reciprocal(out=rstd[i][:, 0:TN], in_=rstd[i][:, 0:TN]),
                             reads=[r_rstd[i]], writes=[r_rstd[i]])
                        for kc in range(8):
                            fi = kc % 2
                            S.op("dve", lambda e: e.scalar_tensor_tensor(out=f32a[fi][:, 0:TN], in0=xt[i][:, kc, 0:TN],
                                                                         scalar=gm[:, kc, mc:mc + 1], in1=rstd[i][:, 0:TN],
                                                                         op0=ALU.mult, op1=ALU.mult),
                                 reads=[r_xt[i], r_gm, r_rstd[i]], writes=[r_f32a[fi]])
                            S.op("act", lambda e: e.activation(out=ht[i][:, kc, 0:TN], in_=f32a[fi][:, 0:TN], func=AF.Identity,
                                                               bias=mod[:, kc, mc:mc + 1], scale=1.0),
                                 reads=[r_f32a[fi], r_mod], writes=[r_ht[i]] if kc == 0 else (), partial=[r_ht[i]] if kc else ())
                        S.dma(hT.rearrange("(k p) n -> p k n", p=128)[:, :, soff + t0:soff + t0 + TN], ht[i][:, :, 0:TN],
                              f"d_ht{i}", reads=[r_ht[i]], writes=[hres("h", soff + t0)])

                for (sname, soff, L, TN, _) in (act_seqs if not last else act_seqs[1:]):
                    S.barrier()
                    ps_pool["gen"] = [0, 1, 2]
                    car = Carver(arena, ARN)
                    LC = L // 128
                    wfu, _ = car.take(8 * 512, "wfu"); r_wfu = WSet(4, "wfu"); wfu3 = wfu.rearrange("p (k n) -> p k n", k=8)
                    wfz, _ = car.take(8 * 512, "wfz"); r_wfz = WSet(4, "wfz"); wfz3 = wfz.rearrange("p (k n) -> p k n", k=8)
                    AB, r_AB = car.take(LC * 1024, "AB")
                    AB5 = AB.rearrange("p (lc g cs c) -> p lc g cs c", lc=LC, g=4, cs=2)
                    AB3 = AB.rearrange("p (lc x) -> p lc x", lc=LC)
                    LG = LGL if sname == "lat" else 2
                    dsl = []
                    for i in range(4):
                        v, r = car.take(LG * 2 * TN, f"dft{i}")
                        dsl.append((v.rearrange("p (a cs n) -> p a cs n", a=LG, cs=2), r))
                    wcar = Carver(work, WORKN)
                    UT = []
                    for i in range(2):
                        v, r = wcar.take(4 * 512, f"UT{i}")
                        UT.append((v.rearrange("p (g n) -> p g n", g=4), r))
                    szb, r_sz = wcar.take(4 * 512, "sz"); sz3 = szb.rearrange("p (g n) -> p g n", g=4)
                    tfb = []
                    for i in range(2):
                        v, r = wcar.take(4 * 512, f"tf{i}")
                        tfb.append((v.rearrange("p (g n) -> p g n", g=4), r))
                    load_cast(w_in[l], 8, 512, wfu3, r_wfu, col0=C_FU)
                    load_cast(w_in[l], 8, 512, wfz3, r_wfz, col0=C_FZ)
                    ntile = L // TN
                    load_h(b, soff, 0, TN, 0)
                    for tix in range(ntile):
                        t0 = tix * TN
                        i = tix % 2
                        if tix + 1 < ntile:
                            load_h(b, soff, t0 + TN, TN, (tix + 1) % 2)
                        ut, r_ut = UT[i]
                        for g in range(4):
                            ps, r_ps = inproj(ht[i], r_ht[i], TN, wfu3, r_wfu, g)
                            S.op("act", lambda e: e.activation(out=ut[:, g, 0:TN], in_=ps[:, 0:TN], func=AF.Identity,
                                                               bias=vcol(l, V_BIN, C_FU // 128 + g), scale=1.0),
                                 reads=[r_ps, r_vecs], writes=[r_ut] if g == 0 else (), partial=[r_ut] if g else ())
                        for st in range(TN // 128):
                            lc = (t0 // 128) + st
                            for half in range(2):
                                ps, r_ps = psum()
                                for gg in range(2):
                                    g = half * 2 + gg
                                    S.op("pe", lambda e: e.matmul(ps[:, gg * 256:(gg + 1) * 256], ut[:, g, st * 128:(st + 1) * 128], csc,
                                                                  start=True, stop=True),
                                         reads=[r_ut, r_consts], writes=[r_ps] if gg == 0 else (), partial=[r_ps] if gg else (),
                                         inc=(gg == 1))
                                dst = AB3[:, lc, half * 512:(half + 1) * 512]
                                if rr["evac"] % 2 == 0:
                                    S.op("act", lambda e: e.activation(out=dst, in_=ps[:, :], func=AF.Copy), reads=[r_ps], partial=[r_AB])
                                else:
                                    S.op("dve", lambda e: e.tensor_copy(out=dst, in_=ps[:, :]), reads=[r_ps], partial=[r_AB])
                                rr["evac"] += 1
                    dsrc = dftl_in if sname == "lat" else dftc_in
                    nlg = LC // LG
                    di = 0
                    load_h(b, soff, 0, TN, 0)
                    for tix in range(ntile):
                        t0 = tix * TN
                        i = tix % 2
                        if tix + 1 < ntile:
                            load_h(b, soff, t0 + TN, TN, (tix + 1) % 2)
                        accs = [psum("acc") for _ in range(4)]
                        for lg in range(nlg):
                            dv, r_dv = dsl[di % 4]
                            S.dma(dv, dsrc[tix, lg], f"d_dft{di % 4}", writes=[r_dv])
                            di += 1
                            for a in range(LG):
                                lc = lg * LG + a
                                for cs in range(2):
                                    first = (lc == 0 and cs == 0)
                                    lastm = (lc == LC - 1 and cs == 1)
                                    for g in range(4):
                                        ps, r_ps = accs[g]
                                        S.op("pe", lambda e: e.matmul(ps[:, 0:TN], AB5[:, lc, g, cs, :], dv[:, a, cs, 0:TN],
                                                                      start=first, stop=lastm),
                                             reads=[r_AB, r_dv], writes=[r_ps] if first else (), partial=() if first else [r_ps],
                                             inc=(lastm or (a == LG - 1 and cs == 1 and g == 3)))
                        tf, r_tf = tfb[i]
                        for g in range(4):
                            ps, r_ps = inproj(ht[i], r_ht[i], TN, wfz3, r_wfz, g)
                            S.op("act", lambda e: e.activation(out=sz3[:, g, 0:TN], in_=ps[:, 0:TN], func=AF.Silu,
                                                               bias=vcol(l, V_BIN, C_FZ // 128 + g), scale=1.0),
                                 reads=[r_ps, r_vecs], writes=[r_sz] if g == 0 else (), partial=[r_sz] if g else ())
                            psY, r_psY = accs[g]
                            S.op("dve", lambda e: e.tensor_tensor(out=tf[:, g, 0:TN], in0=psY[:, 0:TN], in1=sz3[:, g, 0:TN], op=ALU.mult),
                                 reads=[r_psY, r_sz], writes=[r_tf] if g == 0 else (), partial=[r_tf] if g else ())
                        S.dma(tT[1].rearrange("(k p) n -> p k n", p=128)[:, :, soff + t0:soff + t0 + TN], tf[:, :, 0:TN],
                              f"d_tf{i}", reads=[r_tf], writes=[hres("t", 1, soff + t0)])

                for (sname, soff, L, TN, _) in (act_seqs if not last else act_seqs[1:]):
                    S.barrier()
                    ps_pool["gen"] = list(range(7))
                    car = Carver(arena, ARN)
                    wc, _ = car.take(8 * 2048, "wconv"); r_wc = WSet(16, "wconv"); wc3 = wc.rearrange("p (k n) -> p k n", k=8)
                    ub, r_u = car.take(4 * (L + 2), "u"); u3 = ub.rearrange("p (c n) -> p c n", c=4)
                    wcar = Carver(work, WORKN)
                    tab = []
                    for i in range(2):
                        v, r = wcar.take(4 * 512, f"ta{i}")
                        tab.append((v.rearrange("p (g n) -> p g n", g=4), r))
                    f32a, r_f32a = carve_f32a(wcar, 6)
                    load_cast(w_in[l], 8, 2048, wc3, r_wc, col0=0)
                    S.op("pool", lambda e: e.memset(u3[:, :, 0:1], 0.0), partial=[r_u])
                    S.op("pool", lambda e: e.memset(u3[:, :, L + 1:L + 2], 0.0), partial=[r_u])
                    ntile = L // TN
                    load_h(b, soff, 0, TN, 0)
                    for tix in range(ntile):
                        t0 = tix * TN
                        i = tix % 2
                        if tix + 1 < ntile:
                            load_h(b, soff, t0 + TN, TN, (tix + 1) % 2)
                        for c in range(4):
                            fi = c % 2
                            ps, r_ps = inproj(ht[i], r_ht[i], TN, wc3, r_wc, C_AX // 128 + c)
                            S.op("act", lambda e: e.activation(out=f32a[fi][:, 0:TN], in_=ps[:, 0:TN], func=AF.Identity,
                                                               bias=vcol(l, V_BIN, C_AX // 128 + c), scale=1.0),
                                 reads=[r_ps, r_vecs], writes=[r_f32a[fi]])
                            ps2, r_ps2 = inproj(ht[i], r_ht[i], TN, wc3, r_wc, C_AC // 128 + c)
                            S.op("dve", lambda e: e.scalar_tensor_tensor(out=u3[:, c, 1 + t0:1 + t0 + TN], in0=ps2[:, 0:TN],
                                                                         scalar=vcol(l, V_BIN, C_AC // 128 + c), in1=f32a[fi][:, 0:TN],
                                                                         op0=ALU.add, op1=ALU.mult),
                                 reads=[r_ps2, r_vecs, r_f32a[fi]], partial=[r_u])
                    load_h(b, soff, 0, TN, 0)
                    for tix in range(ntile):
                        t0 = tix * TN
                        i = tix % 2
                        if tix + 1 < ntile:
                            load_h(b, soff, t0 + TN, TN, (tix + 1) % 2)
                        ta, r_ta = tab[i]
                        for c in range(4):
                            A, rA = f32a[0 + 3 * (c % 2)], r_f32a[0 + 3 * (c % 2)]
                            Bz, rB = f32a[1 + 3 * (c % 2)], r_f32a[1 + 3 * (c % 2)]
                            Cg, rC = f32a[2 + 3 * (c % 2)], r_f32a[2 + 3 * (c % 2)]
                            S.op("dve", lambda e: e.tensor_scalar(out=A[:, 0:TN], in0=u3[:, c, t0:t0 + TN],
                                                                   scalar1=vcol(l, V_WC, c * 3 + 0), scalar2=None, op0=ALU.mult),
                                 reads=[r_u, r_vecs], writes=[rA])
                            S.op("dve", lambda e: e.scalar_tensor_tensor(out=A[:, 0:TN], in0=u3[:, c, t0 + 1:t0 + 1 + TN],
                                                                          scalar=vcol(l, V_WC, c * 3 + 1), in1=A[:, 0:TN],
                                                                          op0=ALU.mult, op1=ALU.add),
                                 reads=[r_u, r_vecs, rA], writes=[rA])
                            S.op("dve", lambda e: e.scalar_tensor_tensor(out=A[:, 0:TN], in0=u3[:, c, t0 + 2:t0 + 2 + TN],
                                                                          scalar=vcol(l, V_WC, c * 3 + 2), in1=A[:, 0:TN],
                                                                          op0=ALU.mult, op1=ALU.add),
                                 reads=[r_u, r_vecs, rA], writes=[rA])
                            psz, r_psz = inproj(ht[i], r_ht[i], TN, wc3, r_wc, C_AZ // 128 + c)
                            S.op("act", lambda e: e.activation(out=Bz[:, 0:TN], in_=psz[:, 0:TN], func=AF.Silu,
                                                               bias=vcol(l, V_BIN, C_AZ // 128 + c), scale=1.0),
                                 reads=[r_psz, r_vecs], writes=[rB])
                            psb, r_psb = inproj(ht[i], r_ht[i], TN, wc3, r_wc, C_AB // 128 + c)
                            S.op("dve", lambda e: e.scalar_tensor_tensor(out=Cg[:, 0:TN], in0=psb[:, 0:TN],
                                                                         scalar=vcol(l, V_BIN, C_AB // 128 + c), in1=Bz[:, 0:TN],
                                                                         op0=ALU.add, op1=ALU.mult),
                                 reads=[r_psb, r_vecs, rB], writes=[rC])
                            S.op("dve", lambda e: e.scalar_tensor_tensor(out=ta[:, c, 0:TN], in0=A[:, 0:TN],
                                                                         scalar=vcol(l, V_BC, c), in1=Cg[:, 0:TN],
                                                                         op0=ALU.add, op1=ALU.mult),
                                 reads=[rA, rC, r_vecs], writes=[r_ta] if c == 0 else (), partial=[r_ta] if c else ())
                        S.dma(tT[0].rearrange("(k p) n -> p k n", p=128)[:, :, soff + t0:soff + t0 + TN], ta[:, :, 0:TN],
                              f"d_ta{i}", reads=[r_ta], writes=[hres("t", 0, soff + t0)])

                S.barrier()
                ps_pool["gen"] = [0, 1, 2, 3, 4]
                car = Carver(arena, ARN)
                wkv, _ = car.take(8 * 1024, "wkv"); r_wkv = WSet(8, "wkv"); wkv3 = wkv.rearrange("p (k n) -> p k n", k=8)
                wqz3, r_wqz = wkv3, r_wkv
                KT = {}
                VV = {}
                for (sname, soff, L, TN, _) in act_seqs:
                    v, r = car.take(4 * L, "KT" + sname)
                    KT[sname] = (v.rearrange("p (c n) -> p c n", c=4), r)
                    v, r = car.take((L // 128) * NH * 65, "V" + sname)
                    VV[sname] = (v.rearrange("p (t h d) -> p t h d", t=L // 128, h=NH), r)
                key0 = geo["order"][0]
                nres = len(geo["pats"][key0])
                bres, r_bres = car.take(nres * NH * 128, "bres"); bres3 = bres.rearrange("p (t k) -> p t k", k=128)
                bdyn, r_bdyn = car.take(5 * NH * 128, "bdyn"); bdyn3 = bdyn.rearrange("p (t k) -> p t k", k=128)
                wcar = Carver(work, WORKN)
                qTb, r_qT = wcar.take(4 * 512, "qT"); qT3 = qTb.rearrange("p (c n) -> p c n", c=4)
                PTs = []
                for i in range(2):
                    v, r = wcar.take(7 * 128, f"PT{i}")
                    PTs.append((v, r))
                Ons = []
                for i in range(2):
                    v, r = wcar.take(512, f"On{i}")
                    Ons.append((v, r))
                tcb = []
                for i in range(2):
                    v, r = wcar.take(4 * 512, f"tc{i}")
                    tcb.append((v.rearrange("p (g n) -> p g n", g=4), r))
                f32a, r_f32a = carve_f32a(wcar, 5)
                scz = [f32a[0], f32a[1], f32a[2], f32a[3]]; r_scz = r_f32a[0:4]
                rec, r_rec = f32a[4], r_f32a[4]
                load_cast(w_in[l], 8, 1024, wkv3, r_wkv, col0=C_K)

                def load_bias(tab0, ntabs, dst3, r_dst):
                    tot = ntabs * NH
                    c = 0
                    while c < tot:
                        w = min(16, tot - c)
                        i = rr["wst"] % 2
                        rr["wst"] += 1
                        stg = wst[i][:, 0:w * 128].rearrange("p (t k) -> p t k", k=128)
                        S.dma(stg, bias_in[l, tab0 * NH + c:tab0 * NH + c + w].rearrange("t q k -> q t k"),
                              f"d_wst{i}", writes=[r_wst[i]])
                        cast(dst3[:, c:c + w, :], stg, [r_wst[i]], [r_dst])
                        c += w
                load_bias(geo["tab_base"][key0], nres, bres3, r_bres)
                for (sname, soff, L, TN, _) in act_seqs:
                    kt3, r_kt = KT[sname]
                    v4, r_v = VV[sname]
                    S.op("pool", lambda e: e.memset(v4[:, :, :, 64:65], 1.0), partial=[r_v])
                    ntile = L // TN
                    load_h(b, soff, 0, TN, 0)
                    for tix in range(ntile):
                        t0 = tix * TN
                        i = tix % 2
                        if tix + 1 < ntile:
                            load_h(b, soff, t0 + TN, TN, (tix + 1) % 2)
                        for c in range(4):
                            ps, r_ps = inproj(ht[i], r_ht[i], TN, wkv3, r_wkv, c)
                            S.op("act", lambda e: e.activation(out=kt3[:, c, t0:t0 + TN], in_=ps[:, 0:TN], func=AF.Identity,
                                                               bias=vcol(l, V_BIN, C_K // 128 + c), scale=1.0),
                                 reads=[r_ps, r_vecs], partial=[r_kt])
                        for st in range(TN // 128):
                            ps, r_ps = psum()
                            for kc in range(8):
                                S.op("pe", lambda e: e.matmul(ps[:, :], ht[i][:, kc, st * 128:(st + 1) * 128], wkv3[:, kc, 512:1024],
                                                              start=(kc == 0), stop=(kc == 7)),
                                     reads=r_wkv.chunks[4:8] + [r_ht[i]], writes=[r_ps] if kc == 0 else (), partial=[r_ps] if kc else (), inc=(kc == 7))
                            S.op("dve", lambda e: e.tensor_copy(out=v4[:, t0 // 128 + st, :, 0:64],
                                                                in_=ps[:, :].rearrange("p (h d) -> p h d", h=NH)),
                                 reads=[r_ps], partial=[r_v])
                load_cast(w_in[l], 8, 512, wqz3, r_wqz, col0=C_Q, dcol0=0)
                load_cast(w_in[l], 8, 512, wqz3, r_wqz, col0=C_CZ, dcol0=512)
                cur_dyn = [None]
                for (sname, soff, L, TN, _) in (act_seqs if not last else act_seqs[1:]):
                    ntile = L // TN
                    kc3, r_kc = KT["ctx"]
                    vc4, r_vc = VV["ctx"]
                    kl3, r_kl = KT[sname]
                    vl4, r_vl = VV[sname]
                    load_h(b, soff, 0, TN, 0)
                    sti0 = [0]
                    jobn = [0]
                    for tix in range(ntile):
                        t0 = tix * TN
                        i = tix % 2
                        if tix + 1 < ntile:
                            load_h(b, soff, t0 + TN, TN, (tix + 1) % 2)
                        for c in range(4):
                            ps, r_ps = inproj(ht[i], r_ht[i], TN, wqz3, r_wqz, c)
                            S.op("dve", lambda e: e.tensor_scalar(out=qT3[:, c, 0:TN], in0=ps[:, 0:TN],
                                                                  scalar1=vcol(l, V_BIN, C_Q // 128 + c), scalar2=HD ** -0.5,
                                                                  op0=ALU.add, op1=ALU.mult),
                                 reads=[r_ps, r_vecs], writes=[r_qT] if c == 0 else (), partial=[r_qT] if c else ())
                        for c in range(4):
                            ps, r_ps = inproj(ht[i], r_ht[i], TN, wqz3, r_wqz, 4 + c)
                            S.op("act", lambda e: e.activation(out=scz[c][:, 0:TN], in_=ps[:, 0:TN], func=AF.Silu,
                                                               bias=vcol(l, V_BIN, C_CZ // 128 + c), scale=1.0),
                                 reads=[r_ps, r_vecs], writes=[r_scz[c]])
                        tc, r_tc = tcb[i]
                        sub_state = {}

                        def sub_get(st, gsub):
                            if st in sub_state:
                                return sub_state[st]
                            chunks = []
                            if sname == "lat":
                                key, chl = geo["subs"][gsub]
                                if key == key0:
                                    bt3, r_bt = bres3, r_bres
                                else:
                                    if cur_dyn[0] != (key,):
                                        load_bias(geo["tab_base"][key], len(chl), bdyn3, r_bdyn)
                                        cur_dyn[0] = (key,)
                                    bt3, r_bt = bdyn3, r_bdyn
                                for ci, ch in enumerate(chl):
                                    chunks.append((kl3, r_kl, vl4, r_vl, ch, (bt3, r_bt, ci)))
                            for ch in range(CTX // 128):
                                chunks.append((kc3, r_kc, vc4, r_vc, ch, None))
                            on, r_on = Ons[(sti0[0] + st) % 2]
                            sub_state[st] = dict(chunks=chunks, on=on, r_on=r_on, psO=None)
                            return sub_state[st]

                        def emit_scores(job):
                            st, h = job["st"], job["h"]
                            ss = sub_get(st, job["gsub"])
                            chunks = ss["chunks"]
                            nch = len(chunks)
                            c = h // 2
                            pb = (h % 2) * 64
                            banks = [psum() for _ in range((nch + 3) // 4)]
                            job["banks"] = banks
                            job["pt"] = PTs[jobn[0] % 2]
                            jobn[0] += 1
                            for ci, (k3, r_k, v4_, r_v_, ch, bt) in enumerate(chunks):
                                psS, r_psS = banks[ci // 4]
                                o = psS[:, (ci % 4) * 128:(ci % 4 + 1) * 128]
                                firstb = (ci % 4 == 0)
                                lastb = (ci % 4 == 3) or (ci == nch - 1)
                                S.op("pe", lambda e: e.matmul(o, k3[pb:pb + 64, c, ch * 128:(ch + 1) * 128],
                                                              qT3[pb:pb + 64, c, st * 128:(st + 1) * 128],
                                                              start=True, stop=(bt is None)),
                                     reads=[r_k, r_qT], writes=[r_psS] if firstb else (), partial=() if firstb else [r_psS],
                                     inc=(bt is None and lastb))
                                if bt is not None:
                                    bt3, r_bt, tix_ = bt
                                    S.op("pe", lambda e: e.matmul(o, bt3[:, tix_ * NH + h, :], ident, start=False, stop=True),
                                         reads=[r_bt, r_consts], partial=[r_psS], inc=lastb)

                        def emit_exp(job):
                            ss = sub_get(job["st"], job["gsub"])
                            nch = len(ss["chunks"])
                            pt, r_pt = job["pt"]
                            for bi, (psS, r_psS) in enumerate(job["banks"]):
                                n = min(4, nch - bi * 4) * 128
                                S.op("act", lambda e: e.activation(out=pt[:, bi * 512:bi * 512 + n], in_=psS[:, 0:n], func=AF.Exp),
                                     reads=[r_psS], writes=[r_pt] if bi == 0 else (), partial=[r_pt] if bi else ())

                        def emit_pv(job):
                            st, h = job["st"], job["h"]
                            ss = sub_get(st, job["gsub"])
                            chunks = ss["chunks"]
                            nch = len(chunks)
                            hl, hg = h % 4, h // 4
                            pt, r_pt = job["pt"]
                            on, r_on = ss["on"], ss["r_on"]
                            if hl == 0:
                                ss["psO"] = psum("o")
                            psO, r_psO = ss["psO"]
                            for ci, (k3, r_k, v4_, r_v_, ch, bt) in enumerate(chunks):
                                S.op("pe", lambda e: e.matmul(psO[:, hl * 65:hl * 65 + 65], pt[:, ci * 128:(ci + 1) * 128],
                                                              v4_[:, ch, h, :], start=(ci == 0), stop=(ci == nch - 1)),
                                     reads=[r_pt, r_v_], writes=[r_psO] if (hl == 0 and ci == 0) else (),
                                     partial=() if (hl == 0 and ci == 0) else [r_psO], inc=(ci == nch - 1))
                            if hl == 3:
                                pso4 = psO[:, 0:260].rearrange("p (h d) -> p h d", h=4)
                                S.op("dve", lambda e: e.reciprocal(out=rec[:, hg * 4:hg * 4 + 4], in_=pso4[:, :, 64]),
                                     reads=[r_psO], writes=[r_rec] if hg == 0 else (), partial=[r_rec] if hg else ())
                                for hl2 in range(4):
                                    h2 = hg * 4 + hl2
                                    S.op("dve", lambda e: e.tensor_scalar(out=on[:, h2 * 64:(h2 + 1) * 64], in0=psO[:, hl2 * 65:hl2 * 65 + 64],
                                                                          scalar1=rec[:, h2:h2 + 1], scalar2=None, op0=ALU.mult),
                                         reads=[r_psO, r_rec], writes=[r_on] if h2 == 0 else (), partial=[r_on] if h2 else ())
                            if h == NH - 1:
                                for c in range(4):
                                    S.op("pe", lambda e: e.transpose(pst[:, c * 128:(c + 1) * 128], on[:, c * 128:(c + 1) * 128], ident),
                                         reads=[r_on, r_consts], writes=[r_pst] if c == 0 else (), partial=[r_pst] if c else (), inc=(c == 3))
                                for c in range(4):
                                    S.op("dve", lambda e: e.scalar_tensor_tensor(out=tc[:, c, st * 128:(st + 1) * 128],
                                                                                 in0=pst[:, c * 128:(c + 1) * 128],
                                                                                 scalar=vcol(l, V_BIN, C_V // 128 + c),
                                                                                 in1=scz[c][:, st * 128:(st + 1) * 128],
                                                                                 op0=ALU.add, op1=ALU.mult),
                                         reads=[r_pst, r_vecs, r_scz[c]],
                                         writes=[r_tc] if (st == 0 and c == 0) else (), partial=() if (st == 0 and c == 0) else [r_tc])

                        jobs = [dict(st=st, h=h, gsub=t0 // 128 + st) for st in range(TN // 128) for h in range(NH)]
                        emit_scores(jobs[0])
                        for ji, job in enumerate(jobs):
                            if ji + 1 < len(jobs):
                                emit_scores(jobs[ji + 1])
                            emit_exp(job)
                            emit_pv(job)
                        sti0[0] += TN // 128
                        S.dma(tT[2].rearrange("(k p) n -> p k n", p=128)[:, :, soff + t0:soff + t0 + TN], tc[:, :, 0:TN],
                              f"d_tc{i}", reads=[r_tc], writes=[hres("t", 2, soff + t0)])

                S.barrier()
                ps_pool["gen"] = list(range(7))
                car = Carver(arena, ARN)
                wg, _ = car.take(8 * 3072, "wg"); r_wg = WSet(24, "wg"); wg3 = wg.rearrange("p (k n) -> p k n", k=8)
                wbr = []
                for j in range(3):
                    v, r = car.take(4 * 1024, f"wbr{j}")
                    wbr.append((v.rearrange("p (k n) -> p k n", k=4), WSet(8, f"wbr{j}")))
                wo, _ = car.take(8 * 1024, "wo"); r_wo = WSet(8, "wo"); wo3 = wo.rearrange("p (k n) -> p k n", k=8)
                tin = []
                for i in range(2):
                    row = []
                    for j in range(3):
                        v, r = car.take(4 * 512, f"tin{i}{j}")
                        row.append((v.rearrange("p (k n) -> p k n", k=4), r))
                    tin.append(row)
                wcar = Carver(work, WORKN)
                Gb, r_G = wcar.take(8 * 512, "G"); G3 = Gb.rearrange("p (k n) -> p k n", k=8)
                f32a, r_f32a = carve_f32a(wcar, 3)
                xcs, r_xcs = carve_f32a(wcar, 2)
                load_cast(w_in[l], 8, 3072, wg3, r_wg, col0=C_GA)
                for j in range(3):
                    load_cast(w_br[l, j], 4, 1024, wbr[j][0], wbr[j][1])
                load_cast(w_out[l], 8, 1024, wo3, r_wo)
                xi = 0
                for (sname, soff, L, TN, _) in (act_seqs if not last else act_seqs[1:]):
                    mc = mcols[sname]
                    ntile = L // TN

                    def load_tile(tix):
                        t0_ = tix * TN
                        i_ = tix % 2
                        load_h(b, soff, t0_, TN, i_)
                        for j in range(3):
                            S.dma(tin[i_][j][0][:, :, 0:TN],
                                  tT[j].rearrange("(k p) n -> p k n", p=128)[:, :, soff + t0_:soff + t0_ + TN],
                                  f"d_tin{i_}{j}", reads=[hres("t", j, soff + t0_)], writes=[tin[i_][j][1]])
                    load_tile(0)
                    for tix in range(ntile):
                        t0 = tix * TN
                        i = tix % 2
                        if tix + 1 < ntile:
                            load_tile(tix + 1)
                        for m in range(8):
                            for j in range(3):
                                tj, r_tj = tin[i][j]
                                wj, r_wj = wbr[j]
                                psy, r_psy = psum()
                                for kc in range(4):
                                    S.op("pe", lambda e: e.matmul(psy[:, 0:TN], wj[:, kc, m * 128:(m + 1) * 128], tj[:, kc, 0:TN],
                                                                  start=(kc == 0), stop=(kc == 3)),
                                         reads=[r_wj.chunks[m], r_tj], writes=[r_psy] if kc == 0 else (), partial=[r_psy] if kc else (), inc=(kc == 3))
                                psg, r_psg = inproj(ht[i], r_ht[i], TN, wg3, r_wg, j * 8 + m)
                                sg, r_sg = f32a[j], r_f32a[j]
                                S.op("act", lambda e: e.activation(out=sg[:, 0:TN], in_=psg[:, 0:TN], func=AF.Sigmoid,
                                                                   bias=vcol(l, V_BIN, C_GA // 128 + j * 8 + m), scale=1.0),
                                     reads=[r_psg, r_vecs], writes=[r_sg])
                                S.op("dve", lambda e: e.tensor_tensor(out=sg[:, 0:TN], in0=psy[:, 0:TN], in1=sg[:, 0:TN], op=ALU.mult),
                                     reads=[r_psy, r_sg], writes=[r_sg])
                            S.op("pool", lambda e: e.tensor_tensor(out=f32a[0][:, 0:TN], in0=f32a[0][:, 0:TN], in1=f32a[1][:, 0:TN], op=ALU.add),
                                 reads=[r_f32a[0], r_f32a[1]], writes=[r_f32a[0]])
                            S.op("pool", lambda e: e.tensor_tensor(out=G3[:, m, 0:TN], in0=f32a[0][:, 0:TN], in1=f32a[2][:, 0:TN], op=ALU.add),
                                 reads=[r_f32a[0], r_f32a[2]], writes=[r_G] if m == 0 else (), partial=[r_G] if m else ())
                        for mo in range(8):
                            xs_i = xi % 2
                            xi += 1
                            xc, r_xc = xcs[xs_i][:, 0:TN], r_xcs[xs_i]
                            S.dma(xc, xsrc[b, mo * 128:(mo + 1) * 128, soff + t0:soff + t0 + TN], f"d_xt{xs_i}",
                                  reads=[hres("x", b, soff + t0)], writes=[r_xc])
                            pso, r_pso = psum()
                            for kc in range(8):
                                S.op("pe", lambda e: e.matmul(pso[:, 0:TN], wo3[:, kc, mo * 128:(mo + 1) * 128], G3[:, kc, 0:TN],
                                                              start=(kc == 0), stop=(kc == 7)),
                                     reads=[r_wo.chunks[mo], r_G], writes=[r_pso] if kc == 0 else (), partial=[r_pso] if kc else (), inc=(kc == 7))
                            S.op("dve", lambda e: e.scalar_tensor_tensor(out=xc, in0=pso[:, 0:TN], scalar=mod[:, 16 + mo, mc:mc + 1],
                                                                         in1=xc, op0=ALU.mult, op1=ALU.add),
                                 reads=[r_pso, r_mod, r_xc], writes=[r_xc])
                            S.dma(xs[b, mo * 128:(mo + 1) * 128, soff + t0:soff + t0 + TN], xc, f"d_xt{xs_i}",
                                  reads=[r_xc], partial=[hres("xn", b, soff + t0)])
                for (sname, soff, L, TN, _) in (act_seqs if not last else act_seqs[1:]):
                    for t0 in range(0, L, TN):
                        hbm_res[("x", b, soff + t0)] = hres("xn", b, soff + t0)
                        del hbm_res[("xn", b, soff + t0)]

                if last:
                    S.barrier()
                    ps_pool["gen"] = list(range(7))
                    car = Carver(arena, ARN)
                    sq, r_sq = car.take(8 * 512, "sq")
                    sq3 = sq.rearrange("p (k n) -> p k n", k=8)
                    xt, r_xt = carve_xt(car)
                    (sname, soff, L, TN, _) = seqs[1]
                    fgc = NVL * depth
                    for tix in range(L // TN):
                        t0 = tix * TN
                        i = tix % 2
                        S.dma(xt[i][:, :, 0:TN], xs[b].rearrange("(k p) n -> p k n", p=128)[:, :, soff + t0:soff + t0 + TN],
                              f"d_xt{i}", reads=[hres("x", b, soff + t0)], writes=[r_xt[i]])
                        S.op("act", lambda e: e.activation(out=sq3[:, :, 0:TN], in_=xt[i][:, :, 0:TN], func=AF.Square),
                             reads=[r_xt[i]], writes=[r_sq])
                        ps, r_ps = psum()
                        for kc in range(8):
                            S.op("pe", lambda e: e.matmul(ps[:, 0:TN], ones, sq3[:, kc, 0:TN], start=(kc == 0), stop=(kc == 7)),
                                 reads=[r_sq, r_consts], writes=[r_ps] if kc == 0 else (), partial=[r_ps] if kc else (), inc=(kc == 7))
                        S.op("act", lambda e: e.activation(out=rstd[i][:, 0:TN], in_=ps[:, 0:TN], func=AF.Sqrt, bias=epsc[:, 0:1], scale=1.0 / D),
                             reads=[r_ps, r_consts], writes=[r_rstd[i]])
                        S.op("dve", lambda e: e.reciprocal(out=rstd[i][:, 0:TN], in_=rstd[i][:, 0:TN]),
                             reads=[r_rstd[i]], writes=[r_rstd[i]])
                        for kc in range(8):
                            S.op("dve",
                                 lambda e: e.scalar_tensor_tensor(out=xt[i][:, kc, 0:TN], in0=xt[i][:, kc, 0:TN],
                                                                  scalar=vecs[:, fgc + kc:fgc + kc + 1], in1=rstd[i][:, 0:TN],
                                                                  op0=ALU.mult, op1=ALU.mult),
                                 reads=[r_vecs, r_rstd[i], r_xt[i]], writes=[r_xt[i]])
                        S.dma(y_out[b].rearrange("(k p) n -> p k n", p=128)[:, :, t0:t0 + TN], xt[i][:, :, 0:TN],
                              f"d_xt{i}", reads=[r_xt[i]], writes=[hres("y", b, t0)])
        S.barrier(engines=("sp",))
        print("program built: ninst", S.ninst, "nwaits", S.nwaits, "nsems", len(S.sems))
    return nc


_CACHE = {}


def _consts(seq):
    key = ("c", seq)
    if key not in _CACHE:
        cst = np.zeros((128, 512), dtype=NPBF)
        cst[:, 0:128] = np.eye(128).astype(NPBF)
        cst[:, 128:256] = np.ones((128, 128)).astype(NPBF)
        cst[:, 256:512] = _chan_table().astype(NPBF)
        dftl = _dft_pos_table(seq, 512, 4)
        dftc = _dft_pos_table(CTX, 256, 2)
        _CACHE[key] = (cst, dftl, dftc)
    return _CACHE[key]


def kernel(x, c, ctx, c_ctx, norm_g, w_ada, b_ada, w_in, b_in, w_conv, b_conv,
           rpb, w_br_a, w_br_f, w_br_c, w_out, final_g):
    x = np.asarray(x, dtype=np.float32)
    depth = int(np.asarray(w_in).shape[0])
    seq = int(x.shape[1])
    B = int(x.shape[0])
    assert B == NB * NCORES
    geo = _attn_geometry(seq)
    cst, dftl, dftc = _consts(seq)
    f = lambda a: np.ascontiguousarray(np.asarray(a, dtype=np.float32))
    c, ctx, c_ctx = f(c), f(ctx), f(c_ctx)
    norm_g, b_ada, b_in, w_conv, b_conv, final_g = f(norm_g), f(b_ada), f(b_in), f(w_conv), f(b_conv), f(final_g)
    w_ada, w_in, w_out = f(w_ada), f(w_in), f(w_out)
    w_br = np.ascontiguousarray(np.stack([f(w_br_a), f(w_br_f), f(w_br_c)], axis=1))
    rpb = f(rpb)
    NV = NVL * depth + 8
    vecs = np.zeros((128, NV), dtype=np.float32)
    for l in range(depth):
        o = l * NVL
        vecs[:, o + V_NG:o + V_NG + 8] = norm_g[l].reshape(8, 128).T
        vecs[:, o + V_BADA:o + V_BADA + 24] = b_ada[l].reshape(24, 128).T
        vecs[:, o + V_BIN:o + V_BIN + 64] = b_in[l].reshape(64, 128).T
        vecs[:, o + V_WC:o + V_WC + 12] = w_conv[l].reshape(3, 4, 128).transpose(2, 1, 0).reshape(128, 12)
        vecs[:, o + V_BC:o + V_BC + 4] = b_conv[l].reshape(4, 128).T
    vecs[:, NVL * depth:NVL * depth + 8] = final_g.reshape(8, 128).T
    bias_tab = np.stack([_bias_tables(rpb[l], geo).reshape(geo["ntab"] * NH, 128, 128) for l in range(depth)], axis=0)
    xall = np.concatenate([ctx.transpose(0, 2, 1), x.transpose(0, 2, 1)], axis=2)
    key = ("nc", seq, depth)
    if key not in _CACHE:
        _CACHE[key] = build_program(seq, depth)
    nc = _CACHE[key]
    in_maps = []
    for core in range(NCORES):
        b0 = core * NB
        cT = np.stack([c[b0], c[b0 + 1], c_ctx], axis=1)
        cT = np.ascontiguousarray(cT.reshape(8, 128, 3).transpose(1, 0, 2))
        in_maps.append({
            "x_in": np.ascontiguousarray(xall[b0:b0 + NB]),
            "cT": cT, "vecs": vecs, "w_ada": w_ada, "w_in": w_in, "w_br": w_br, "w_out": w_out,
            "bias_tab": bias_tab, "consts": cst, "dftl": dftl, "dftc": dftc,
        })
    res = run_bass_kernel_spmd(nc, in_maps, core_ids=list(range(NCORES)))
    ys = [np.asarray(r["y"], dtype=np.float32) for r in res.results]
    y = np.concatenate(ys, axis=0)
    return np.ascontiguousarray(y.transpose(0, 2, 1))
```

```python
import contextlib
import numpy as np
import ml_dtypes
import concourse.bass as bass
import concourse.mybir as mybir
from concourse.bass_utils import run_bass_kernel_spmd

F32 = mybir.dt.float32
BF16 = mybir.dt.bfloat16
AF = mybir.ActivationFunctionType
ALU = mybir.AluOpType
NPBF = ml_dtypes.bfloat16

D = 1024
SEQ = 4096
DEPTH = 4
NCORES = 8
NB = 2
CTX = 256
GW = 64
NH = 8
HD = 64
WIN_H = 8
WIN_W = 16
DIN = 8192
EPS = 1e-6
NEG = -30000.0
C_AX, C_AB, C_AC, C_AZ, C_FU, C_FZ, C_Q, C_K, C_V, C_CZ, C_GA, C_GF, C_GC = (
    0, 512, 1024, 1536, 2048, 2560, 3072, 3584, 4096, 4608, 5120, 6144, 7168)
V_NG, V_BADA, V_BIN, V_WC, V_BC, NVL = 0, 8, 32, 96, 108, 112


class Res:
    __slots__ = ("name", "lw", "rd")

    def __init__(self, name):
        self.name = name
        self.lw = {}
        self.rd = {}


class Sched:
    def __init__(self, nc, es):
        self.nc = nc
        self.es = es
        self.eng = {"pe": nc.tensor, "act": nc.scalar, "dve": nc.vector, "pool": nc.gpsimd, "sp": nc.sync}
        self.sems = {}
        self.cnt = {}
        self.isdma = {}
        self.waited = {e: {} for e in self.eng}
        for e in ("pe", "act", "dve", "pool"):
            self._sem(e, False)
        self.ninst = 0
        self.nwaits = 0

    def _sem(self, key, isdma):
        if key not in self.sems:
            self.sems[key] = self.es.enter_context(self.nc.semaphore("s_" + str(key)))
            self.cnt[key] = 0
            self.isdma[key] = isdma
        return self.sems[key]

    def _wait(self, e, k, v):
        if self.isdma[k]:
            v = self.cnt[k]
        w = self.waited[e]
        if w.get(k, 0) >= v:
            return
        if e == "pe" and k == "pe":
            return
        self.eng[e].wait_ge(self.sems[k], v)
        w[k] = v
        self.nwaits += 1

    def deps(self, e, reads, writes, partial):
        for r in reads:
            for k, v in r.lw.items():
                self._wait(e, k, v)
        for w_ in writes:
            for k, v in w_.lw.items():
                self._wait(e, k, v)
            for k, v in w_.rd.items():
                self._wait(e, k, v)
        for w_ in partial:
            for k, v in w_.rd.items():
                self._wait(e, k, v)

    def mark(self, ev, reads, writes, partial):
        k, v = ev
        for r in reads:
            if r.rd.get(k, 0) < v:
                r.rd[k] = v
        for w_ in writes:
            w_.lw = {k: v}
            w_.rd = {}
        for w_ in partial:
            if w_.lw.get(k, 0) < v:
                w_.lw[k] = v

    def op(self, e, fn, reads=(), writes=(), partial=(), inc=True):
        self.deps(e, reads, writes, partial)
        ins = fn(self.eng[e])
        self.ninst += 1
        if inc:
            self.cnt[e] += 1
            ins.then_inc(self.sems[e], 1)
            ev = (e, self.cnt[e])
        else:
            ev = (e, self.cnt[e] + 1)
        self.mark(ev, reads, writes, partial)
        return ins

    def dma(self, out, in_, semkey, reads=(), writes=(), partial=(), e="sp"):
        self._sem(semkey, True)
        self.deps(e, reads, writes, partial)
        ins = self.eng[e].dma_start(out=out, in_=in_)
        self.ninst += 1
        self.cnt[semkey] += 16
        ins.then_inc(self.sems[semkey], 16)
        self.mark((semkey, self.cnt[semkey]), reads, writes, partial)
        return ins

    def barrier(self, engines=("pe", "act", "dve", "pool", "sp")):
        for e in engines:
            for k in list(self.sems.keys()):
                if self.cnt[k] > 0:
                    self._wait(e, k, self.cnt[k])


class Buf:
    def __init__(self, ap_fn, name):
        self.t = ap_fn
        self.r = Res(name)


def _dft_pos_table(L, TN, LG):
    m = np.arange(L, dtype=np.float64)
    cosv = np.cos(2 * np.pi * m / L) / np.sqrt(L)
    sinv = -np.sin(2 * np.pi * m / L) / np.sqrt(L)
    l = np.arange(L, dtype=np.int64)
    idx = (l[:, None] * l[None, :]) % L
    ntile = L // TN
    nlc = L // 128
    nlg = nlc // LG
    out = np.empty((ntile, nlg, 128, LG, 2, TN), dtype=NPBF)
    for cs, tab in ((0, cosv), (1, sinv)):
        full = tab[idx].astype(NPBF)
        full = full.reshape(nlg, LG, 128, ntile, TN)
        out[:, :, :, :, cs, :] = full.transpose(3, 0, 2, 1, 4)
    return out


def _chan_table():
    c = np.arange(128, dtype=np.int64)
    idx = (c[:, None] * c[None, :]) % 128
    ang = 2 * np.pi * idx / 128.0
    return np.concatenate([np.cos(ang), np.sin(ang)], axis=1) / np.sqrt(128.0)


def _attn_geometry(seq):
    R = seq // GW
    kh = min(WIN_H, R)

    def rs(r):
        return int(np.clip(r - kh // 2, 0, R - kh))
    subs = []
    pats = {}
    for j in range(R // 2):
        r0 = 2 * j
        a0, a1 = rs(r0) - r0, rs(r0 + 1) - r0
        lo, hi = rs(r0), rs(r0 + 1) + kh - 1
        chunks = list(range(lo // 2, hi // 2 + 1))
        key = (a0, a1)
        deltas = tuple(2 * c - r0 for c in chunks)
        if key not in pats:
            pats[key] = deltas
        assert pats[key] == deltas
        subs.append((key, chunks))
    cnts = {}
    for key, _ in subs:
        cnts[key] = cnts.get(key, 0) + 1
    order = sorted(pats.keys(), key=lambda k: -cnts[k])
    tab_base = {}
    n = 0
    for key in order:
        tab_base[key] = n
        n += len(pats[key])
    return dict(R=R, kh=kh, subs=subs, pats=pats, order=order, tab_base=tab_base, ntab=n)


def _bias_tables(rpb_l, geo):
    kh = geo["kh"]
    out = np.full((geo["ntab"], NH, 128, 128), NEG, dtype=np.float32)
    qc = np.arange(GW)
    kc = np.arange(GW)
    cs = np.clip(qc - WIN_W // 2, 0, GW - WIN_W)
    col_ok = (kc[None, :] >= cs[:, None]) & (kc[None, :] < cs[:, None] + WIN_W)
    dc = np.clip(kc[None, :] - qc[:, None] + WIN_W - 1, 0, 2 * WIN_W - 2)
    for key in geo["order"]:
        a0, a1 = key
        for ci, dl in enumerate(geo["pats"][key]):
            t = geo["tab_base"][key] + ci
            for qr in range(2):
                a = a0 if qr == 0 else a1
                for krl in range(2):
                    rel = dl + krl
                    if not (a <= rel < a + kh):
                        continue
                    dr = rel - qr + WIN_H - 1
                    blk = np.where(col_ok[None], rpb_l[:, dr][:, dc], np.float32(NEG))
                    out[t, :, qr * 64:(qr + 1) * 64, krl * 64:(krl + 1) * 64] = blk
    return out


def build_program(seq, depth):
    geo = _attn_geometry(seq)
    LT = CTX + seq
    nc = bass.Bass("TRN2", target_bir_lowering=False)
    NV = NVL * depth + 8
    x_in = nc.dram_tensor("x_in", [NB, D, LT], F32, kind="ExternalInput").ap()
    cT_in = nc.dram_tensor("cT", [128, 8, 3], F32, kind="ExternalInput").ap()
    vecs_in = nc.dram_tensor("vecs", [128, NV], F32, kind="ExternalInput").ap()
    w_ada = nc.dram_tensor("w_ada", [depth, D, 3 * D], F32, kind="ExternalInput").ap()
    w_in = nc.dram_tensor("w_in", [depth, D, DIN], F32, kind="ExternalInput").ap()
    w_br = nc.dram_tensor("w_br", [depth, 3, 512, D], F32, kind="ExternalInput").ap()
    w_out = nc.dram_tensor("w_out", [depth, D, D], F32, kind="ExternalInput").ap()
    bias_in = nc.dram_tensor("bias_tab", [depth, geo["ntab"] * NH, 128, 128], F32, kind="ExternalInput").ap()
    consts_in = nc.dram_tensor("consts", [128, 512], BF16, kind="ExternalInput").ap()
    LGL = 4
    dftl_in = nc.dram_tensor("dftl", [seq // 512, seq // 128 // LGL, 128, LGL, 2, 512], BF16, kind="ExternalInput").ap()
    dftc_in = nc.dram_tensor("dftc", [1, 1, 128, 2, 2, 256], BF16, kind="ExternalInput").ap()
    y_out = nc.dram_tensor("y", [NB, D, seq], F32, kind="ExternalOutput").ap()
    xs = nc.dram_tensor("xs", [NB, D, LT], F32, kind="Internal").ap()
    hT = nc.dram_tensor("hT", [NB, D, LT], BF16, kind="Internal").ap()
    tT = nc.dram_tensor("tT", [NB, 3, 512, LT], BF16, kind="Internal").ap()

    es = contextlib.ExitStack()
    with es:
        S = Sched(nc, es)

        def sb(name, shape, dt):
            return es.enter_context(nc.sbuf_tensor("sb_" + name, shape, dt))

        vecs = sb("vecs", [128, NV], F32); r_vecs = Res("vecs")
        consts = sb("consts", [128, 512], BF16); r_consts = Res("consts")
        ident = consts[:, 0:128]
        ones = consts[:, 128:256]
        csc = consts[:, 256:512]
        epsc = sb("epsc", [128, 1], F32)
        cts = sb("cts", [128, 8, 3], F32); r_cts = Res("cts")
        mod = sb("mod", [128, 24, 3], F32); r_mod = Res("mod")
        gm = sb("gm", [128, 8, 3], F32); r_gm = Res("gm")
        ht = [sb(f"ht{i}", [128, 8, 512], BF16) for i in range(2)]; r_ht = [Res(f"ht{i}") for i in range(2)]
        wst = [sb(f"wst{i}", [128, 2048], F32) for i in range(4)]; r_wst = [Res(f"wst{i}") for i in range(4)]
        rstd = [sb(f"rstd{i}", [128, 512], F32) for i in range(2)]; r_rstd = [Res(f"rstd{i}") for i in range(2)]
        WORKN = 16384
        work = sb("work", [128, WORKN], BF16)
        ARN = 57344
        arena = sb("arena", [128, ARN], BF16)
        psf = [es.enter_context(nc.psum_tensor(f"psf{i}", [128, 512], F32)) for i in range(7)]
        r_psf = [Res(f"psf{i}") for i in range(7)]
        pst = es.enter_context(nc.psum_tensor("pst", [128, 1024], BF16)); r_pst = Res("pst")
        ps_pool = {"gen": list(range(7)), "acc": [3, 4, 5, 6], "o": [5, 6]}
        ps_rr = {"gen": 0, "acc": 0, "o": 0}

        def psum(pool="gen"):
            lst = ps_pool[pool]
            i = lst[ps_rr[pool] % len(lst)]
            ps_rr[pool] += 1
            return psf[i], r_psf[i]

        hbm_res = {}

        def hres(*key):
            if key not in hbm_res:
                hbm_res[key] = Res(str(key))
            return hbm_res[key]

        class Carver:
            def __init__(self, base, total):
                self.base = base
                self.total = total
                self.off = 0

            def take(self, n, name):
                assert self.off + n <= self.total, (name, self.off, n, self.total)
                v = self.base[:, self.off:self.off + n]
                self.off += n
                return v, Res(name)

        def take_f32(carver, n, name):
            v, r = carver.take(2 * n, name)
            return v.bitcast(F32), r

        def carve_xt(carver):
            xs_, rs_ = [], []
            for i in range(2):
                v, r = take_f32(carver, 8 * 512, f"xt{i}")
                xs_.append(v.rearrange("p (k n) -> p k n", k=8))
                rs_.append(r)
            return xs_, rs_

        def carve_f32a(carver, n):
            fs_, rs_ = [], []
            for i in range(n):
                v, r = take_f32(carver, 512, f"f32a{i}")
                fs_.append(v)
                rs_.append(r)
            return fs_, rs_

        rr = {"evac": 0, "wst": 0, "cast": 0}

        class WSet:
            def __init__(self, n, name):
                self.chunks = [Res(f"{name}{i}") for i in range(n)]

        def cast(out, in_, reads, partial):
            if rr["cast"] % 2 == 0:
                S.op("act", lambda e: e.activation(out=out, in_=in_, func=AF.Copy), reads=reads, partial=partial)
            else:
                S.op("dve", lambda e: e.tensor_copy(out=out, in_=in_), reads=reads, partial=partial)
            rr["cast"] += 1

        def load_cast(src2d, kcn, ncols, dst3, wset, col0=0, dcol0=0):
            src = src2d.rearrange("(kc p) n -> p kc n", p=128)
            piece = max(1, 2048 // kcn)
            piece = min(piece, ncols)
            c = 0
            while c < ncols:
                w = min(piece, ncols - c)
                i = rr["wst"] % 4
                rr["wst"] += 1
                stg = wst[i][:, 0:kcn * w].rearrange("p (k n) -> p k n", k=kcn)
                S.dma(stg, src[:, :, col0 + c:col0 + c + w], f"d_wst{i}", writes=[r_wst[i]])
                cks = wset.chunks[(dcol0 + c) // 128:(dcol0 + c + w + 127) // 128]
                cast(dst3[:, :, dcol0 + c:dcol0 + c + w], stg, [r_wst[i]], cks)
                c += w

        def inproj(htb, r_htb, N, wv, r_w, mchunk):
            ps, r_ps = psum()
            for kc in range(8):
                S.op("pe", lambda e: e.matmul(ps[:, 0:N], wv[:, kc, mchunk * 128:(mchunk + 1) * 128], htb[:, kc, 0:N],
                                              start=(kc == 0), stop=(kc == 7)),
                     reads=[r_w.chunks[mchunk], r_htb], writes=[r_ps] if kc == 0 else (), partial=[r_ps] if kc else (), inc=(kc == 7))
            return ps, r_ps

        def vcol(l, off, j=0):
            c = l * NVL + off + j
            return vecs[:, c:c + 1]

        def load_h(b, seqoff, t0, N, slot):
            S.dma(ht[slot][:, :, 0:N],
                  hT[b].rearrange("(k p) n -> p k n", p=128)[:, :, seqoff + t0:seqoff + t0 + N],
                  f"d_ht{slot}", reads=[hres("h", b, seqoff + t0)], writes=[r_ht[slot]])

        seqs = [("ctx", 0, CTX, 256, 2), ("lat", CTX, seq, 512, 3)]

        S.dma(vecs[:, :], vecs_in[:, :], "d_misc", writes=[r_vecs])
        S.dma(consts[:, :], consts_in[:, :], "d_misc", writes=[r_consts])
        S.dma(cts[:, :, :], cT_in[:, :, :], "d_misc", writes=[r_cts])
        S.op("pool", lambda e: e.memset(epsc[:, :], EPS), partial=[r_consts])
        S.op("act", lambda e: e.activation(out=cts[:, :, :], in_=cts[:, :, :], func=AF.Silu), reads=[r_cts], writes=[r_cts])

        for l in range(depth):
            last = (l == depth - 1)
            xsrc = x_in if l == 0 else xs
            S.barrier()
            ps_pool["gen"] = list(range(7))
            car = Carver(arena, ARN)
            xt, r_xt = carve_xt(car)
            for pc in range(8):
                i = pc % 2
                stg = xt[i][:, :, 0:384]
                S.dma(stg, w_ada[l].rearrange("(kc p) n -> p kc n", p=128)[:, :, pc * 384:(pc + 1) * 384],
                      f"d_xt{i}", writes=[r_xt[i]])
                ps, r_ps = psum()
                for mm in range(3):
                    for kc in range(8):
                        S.op("pe", lambda e: e.matmul(ps[:, mm * 4:mm * 4 + 3], stg[:, kc, mm * 128:(mm + 1) * 128],
                                                      cts[:, kc, :], start=(kc == 0), stop=(kc == 7)),
                             reads=[r_xt[i], r_cts], writes=[r_ps] if (mm == 0 and kc == 0) else (),
                             partial=() if (mm == 0 and kc == 0) else [r_ps], inc=(mm == 2 and kc == 7))
                for mm in range(3):
                    m = pc * 3 + mm
                    S.op("dve", lambda e: e.tensor_scalar(out=mod[:, m, :], in0=ps[:, mm * 4:mm * 4 + 3],
                                                          scalar1=vcol(l, V_BADA, m), scalar2=None, op0=ALU.add),
                         reads=[r_ps, r_vecs], partial=[r_mod])
            for j in range(3):
                S.op("dve", lambda e: e.scalar_tensor_tensor(out=gm[:, :, j], in0=mod[:, 8:16, j], scalar=1.0,
                                                             in1=vecs[:, l * NVL + V_NG:l * NVL + V_NG + 8],
                                                             op0=ALU.add, op1=ALU.mult),
                     reads=[r_mod, r_vecs], partial=[r_gm])

            for b in range(NB):
                mcols = {"ctx": 2, "lat": b}
                act_seqs = seqs
                S.barrier()
                ps_pool["gen"] = list(range(7))
                car = Carver(arena, ARN)
                sq, r_sq = car.take(8 * 512, "sq")
                sq3 = sq.rearrange("p (k n) -> p k n", k=8)
                xt, r_xt = carve_xt(car)
                wcar = Carver(work, WORKN)
                f32a, r_f32a = carve_f32a(wcar, 2)
                p1tiles = [(sname, soff, t0, TN, mcols[sname]) for (sname, soff, L, TN, _) in act_seqs for t0 in range(0, L, TN)]

                def p1load(n):
                    (sname_, soff_, t0_, TN_, mc_) = p1tiles[n]
                    S.dma(xt[n % 2][:, :, 0:TN_],
                          xsrc[b].rearrange("(k p) n -> p k n", p=128)[:, :, soff_ + t0_:soff_ + t0_ + TN_],
                          f"d_xt{n % 2}", reads=[hres("x", b, soff_ + t0_)], writes=[r_xt[n % 2]])
                p1load(0)
                for ti, (sname, soff, t0, TN, mc) in enumerate(p1tiles):
                    if True:
                        i = ti % 2
                        if ti + 1 < len(p1tiles):
                            p1load(ti + 1)
                        S.op("act", lambda e: e.activation(out=sq3[:, :, 0:TN], in_=xt[i][:, :, 0:TN], func=AF.Square),
                             reads=[r_xt[i]], writes=[r_sq])
                        ps, r_ps = psum()
                        for kc in range(8):
                            S.op("pe", lambda e: e.matmul(ps[:, 0:TN], ones, sq3[:, kc, 0:TN], start=(kc == 0), stop=(kc == 7)),
                                 reads=[r_sq, r_consts], writes=[r_ps] if kc == 0 else (), partial=[r_ps] if kc else (),
                                 inc=(kc == 7))
                        S.op("act", lambda e: e.activation(out=rstd[i][:, 0:TN], in_=ps[:, 0:TN], func=AF.Sqrt, bias=epsc[:, 0:1], scale=1.0 / D),
                             reads=[r_ps, r_consts], writes=[r_rstd[i]])
                        S.op("dve", lambda e: e.reciprocal(out=rstd[i][:, 0:TN], in_=rstd[i][:, 0:TN]),
                             reads=[r_rstd[i]], writes=[r_rstd[i]])
                        for kc in range(8):
                            fi = kc % 2
                            S.op("dve", lambda e: e.scalar_tensor_tensor(out=f32a[fi][:, 0:TN], in0=xt[i][:, kc, 0:TN],
                                                                         scalar=gm[:, kc, mc:mc + 1], in1=rstd[i][:, 0:TN],
                                                                         op0=ALU.mult, op1=ALU.mult),
                                 reads=[r_xt[i], r_gm, r_rstd[i]], writes=[r_f32a[fi]])
                            S.op("act", lambda e: e.activation(out=ht[i][:, kc, 0:TN], in_=f32a[fi][:, 0:TN], func=AF.Identity,
                                                               bias=mod[:, kc, mc:mc + 1], scale=1.0),
                                 reads=[r_f32a[fi], r_mod], writes=[r_ht[i]] if kc == 0 else (), partial=[r_ht[i]] if kc else ())
                        S.dma(hT[b].rearrange("(k p) n -> p k n", p=128)[:, :, soff + t0:soff + t0 + TN], ht[i][:, :, 0:TN],
                              f"d_ht{i}", reads=[r_ht[i]], writes=[hres("h", b, soff + t0)])

            for b in range(NB):
                mcols = {"ctx": 2, "lat": b}
                act_seqs = seqs
                for si_, (sname, soff, L, TN, _) in enumerate(act_seqs if not last else act_seqs[1:]):
                    if b == 0 and si_ == 0:
                        S.barrier()
                        ps_pool["gen"] = [0, 1, 2]
                        car = Carver(arena, ARN)
                        wfu, _ = car.take(8 * 512, "wfu"); r_wfu = WSet(4, "wfu"); wfu3 = wfu.rearrange("p (k n) -> p k n", k=8)
                        wfz, _ = car.take(8 * 512, "wfz"); r_wfz = WSet(4, "wfz"); wfz3 = wfz.rearrange("p (k n) -> p k n", k=8)
                        ABfull, r_AB = car.take((seq // 128) * 1024, "AB")
                        dslfull = [car.take(LGL * 2 * 512, f"dft{i}") for i in range(4)]
                        wcar = Carver(work, WORKN)
                        UT = []
                        for i in range(2):
                            v, r = wcar.take(4 * 512, f"UT{i}")
                            UT.append((v.rearrange("p (g n) -> p g n", g=4), r))
                        szb, r_sz = wcar.take(4 * 512, "sz"); sz3 = szb.rearrange("p (g n) -> p g n", g=4)
                        tfb = []
                        for i in range(2):
                            v, r = wcar.take(4 * 512, f"tf{i}")
                            tfb.append((v.rearrange("p (g n) -> p g n", g=4), r))
                        load_cast(w_in[l], 8, 512, wfu3, r_wfu, col0=C_FU)
                        load_cast(w_in[l], 8, 512, wfz3, r_wfz, col0=C_FZ)
                    LC = L // 128
                    AB = ABfull[:, 0:LC * 1024]
                    AB5 = AB.rearrange("p (lc g cs c) -> p lc g cs c", lc=LC, g=4, cs=2)
                    AB3 = AB.rearrange("p (lc x) -> p lc x", lc=LC)
                    LG = LGL if sname == "lat" else 2
                    dsl = [(v_[:, 0:LG * 2 * TN].rearrange("p (a cs n) -> p a cs n", a=LG, cs=2), r_) for (v_, r_) in dslfull]
                    ntile = L // TN
                    load_h(b, soff, 0, TN, 0)
                    for tix in range(ntile):
                        t0 = tix * TN
                        i = tix % 2
                        if tix + 1 < ntile:
                            load_h(b, soff, t0 + TN, TN, (tix + 1) % 2)
                        ut, r_ut = UT[i]
                        for g in range(4):
                            ps, r_ps = inproj(ht[i], r_ht[i], TN, wfu3, r_wfu, g)
                            S.op("act", lambda e: e.activation(out=ut[:, g, 0:TN], in_=ps[:, 0:TN], func=AF.Identity,
                                                               bias=vcol(l, V_BIN, C_FU // 128 + g), scale=1.0),
                                 reads=[r_ps, r_vecs], writes=[r_ut] if g == 0 else (), partial=[r_ut] if g else ())
                        for st in range(TN // 128):
                            lc = (t0 // 128) + st
                            for half in range(2):
                                ps, r_ps = psum()
                                for gg in range(2):
                                    g = half * 2 + gg
                                    S.op("pe", lambda e: e.matmul(ps[:, gg * 256:(gg + 1) * 256], ut[:, g, st * 128:(st + 1) * 128], csc,
                                                                  start=True, stop=True),
                                         reads=[r_ut, r_consts], writes=[r_ps] if gg == 0 else (), partial=[r_ps] if gg else (),
                                         inc=(gg == 1))
                                dst = AB3[:, lc, half * 512:(half + 1) * 512]
                                if rr["evac"] % 2 == 0:
                                    S.op("act", lambda e: e.activation(out=dst, in_=ps[:, :], func=AF.Copy), reads=[r_ps], partial=[r_AB])
                                else:
                                    S.op("dve", lambda e: e.tensor_copy(out=dst, in_=ps[:, :]), reads=[r_ps], partial=[r_AB])
                                rr["evac"] += 1
                    dsrc = dftl_in if sname == "lat" else dftc_in
                    nlg = LC // LG
                    di = 0
                    load_h(b, soff, 0, TN, 0)
                    for tix in range(ntile):
                        t0 = tix * TN
                        i = tix % 2
                        if tix + 1 < ntile:
                            load_h(b, soff, t0 + TN, TN, (tix + 1) % 2)
                        accs = [psum("acc") for _ in range(4)]
                        for lg in range(nlg):
                            dv, r_dv = dsl[di % 4]
                            S.dma(dv, dsrc[tix, lg], f"d_dft{di % 4}", writes=[r_dv])
                            di += 1
                            for a in range(LG):
                                lc = lg * LG + a
                                for cs in range(2):
                                    first = (lc == 0 and cs == 0)
                                    lastm = (lc == LC - 1 and cs == 1)
                                    for g in range(4):
                                        ps, r_ps = accs[g]
                                        S.op("pe", lambda e: e.matmul(ps[:, 0:TN], AB5[:, lc, g, cs, :], dv[:, a, cs, 0:TN],
                                                                      start=first, stop=lastm),
                                             reads=[r_AB, r_dv], writes=[r_ps] if first else (), partial=() if first else [r_ps],
                                             inc=(lastm or (a == LG - 1 and cs == 1 and g == 3)))
                        tf, r_tf = tfb[i]
                        for g in range(4):
                            ps, r_ps = inproj(ht[i], r_ht[i], TN, wfz3, r_wfz, g)
                            S.op("act", lambda e: e.activation(out=sz3[:, g, 0:TN], in_=ps[:, 0:TN], func=AF.Silu,
                                                               bias=vcol(l, V_BIN, C_FZ // 128 + g), scale=1.0),
                                 reads=[r_ps, r_vecs], writes=[r_sz] if g == 0 else (), partial=[r_sz] if g else ())
                            psY, r_psY = accs[g]
                            S.op("dve", lambda e: e.tensor_tensor(out=tf[:, g, 0:TN], in0=psY[:, 0:TN], in1=sz3[:, g, 0:TN], op=ALU.mult),
                                 reads=[r_psY, r_sz], writes=[r_tf] if g == 0 else (), partial=[r_tf] if g else ())
                        S.dma(tT[b, 1].rearrange("(k p) n -> p k n", p=128)[:, :, soff + t0:soff + t0 + TN], tf[:, :, 0:TN],
                              f"d_tf{i}", reads=[r_tf], writes=[hres("t", b, 1, soff + t0)])

            for b in range(NB):
                mcols = {"ctx": 2, "lat": b}
                act_seqs = seqs
                for si_, (sname, soff, L, TN, _) in enumerate(act_seqs if not last else act_seqs[1:]):
                    if b == 0 and si_ == 0:
                        S.barrier()
                        ps_pool["gen"] = list(range(7))
                        car = Carver(arena, ARN)
                        wc, _ = car.take(8 * 2048, "wconv"); r_wc = WSet(16, "wconv"); wc3 = wc.rearrange("p (k n) -> p k n", k=8)
                        ubfull, r_u = car.take(4 * (seq + 2), "u")
                        wcar = Carver(work, WORKN)
                        tab = []
                        for i in range(2):
                            v, r = wcar.take(4 * 512, f"ta{i}")
                            tab.append((v.rearrange("p (g n) -> p g n", g=4), r))
                        f32a, r_f32a = carve_f32a(wcar, 6)
                        load_cast(w_in[l], 8, 2048, wc3, r_wc, col0=0)
                    u3 = ubfull[:, 0:4 * (L + 2)].rearrange("p (c n) -> p c n", c=4)
                    S.op("pool", lambda e: e.memset(u3[:, :, 0:1], 0.0), partial=[r_u])
                    S.op("pool", lambda e: e.memset(u3[:, :, L + 1:L + 2], 0.0), partial=[r_u])
                    ntile = L // TN
                    load_h(b, soff, 0, TN, 0)
                    for tix in range(ntile):
                        t0 = tix * TN
                        i = tix % 2
                        if tix + 1 < ntile:
                            load_h(b, soff, t0 + TN, TN, (tix + 1) % 2)
                        for c in range(4):
                            fi = c % 2
                            ps, r_ps = inproj(ht[i], r_ht[i], TN, wc3, r_wc, C_AX // 128 + c)
                            S.op("act", lambda e: e.activation(out=f32a[fi][:, 0:TN], in_=ps[:, 0:TN], func=AF.Identity,
                                                               bias=vcol(l, V_BIN, C_AX // 128 + c), scale=1.0),
                                 reads=[r_ps, r_vecs], writes=[r_f32a[fi]])
                            ps2, r_ps2 = inproj(ht[i], r_ht[i], TN, wc3, r_wc, C_AC // 128 + c)
                            S.op("dve", lambda e: e.scalar_tensor_tensor(out=u3[:, c, 1 + t0:1 + t0 + TN], in0=ps2[:, 0:TN],
                                                                         scalar=vcol(l, V_BIN, C_AC // 128 + c), in1=f32a[fi][:, 0:TN],
                                                                         op0=ALU.add, op1=ALU.mult),
                                 reads=[r_ps2, r_vecs, r_f32a[fi]], partial=[r_u])
                    load_h(b, soff, 0, TN, 0)
                    for tix in range(ntile):
                        t0 = tix * TN
                        i = tix % 2
                        if tix + 1 < ntile:
                            load_h(b, soff, t0 + TN, TN, (tix + 1) % 2)
                        ta, r_ta = tab[i]
                        for c in range(4):
                            A, rA = f32a[0 + 3 * (c % 2)], r_f32a[0 + 3 * (c % 2)]
                            Bz, rB = f32a[1 + 3 * (c % 2)], r_f32a[1 + 3 * (c % 2)]
                            Cg, rC = f32a[2 + 3 * (c % 2)], r_f32a[2 + 3 * (c % 2)]
                            S.op("dve", lambda e: e.tensor_scalar(out=A[:, 0:TN], in0=u3[:, c, t0:t0 + TN],
                                                                   scalar1=vcol(l, V_WC, c * 3 + 0), scalar2=None, op0=ALU.mult),
                                 reads=[r_u, r_vecs], writes=[rA])
                            S.op("dve", lambda e: e.scalar_tensor_tensor(out=A[:, 0:TN], in0=u3[:, c, t0 + 1:t0 + 1 + TN],
                                                                          scalar=vcol(l, V_WC, c * 3 + 1), in1=A[:, 0:TN],
                                                                          op0=ALU.mult, op1=ALU.add),
                                 reads=[r_u, r_vecs, rA], writes=[rA])
                            S.op("dve", lambda e: e.scalar_tensor_tensor(out=A[:, 0:TN], in0=u3[:, c, t0 + 2:t0 + 2 + TN],
                                                                          scalar=vcol(l, V_WC, c * 3 + 2), in1=A[:, 0:TN],
                                                                          op0=ALU.mult, op1=ALU.add),
                                 reads=[r_u, r_vecs, rA], writes=[rA])
                            psz, r_psz = inproj(ht[i], r_ht[i], TN, wc3, r_wc, C_AZ // 128 + c)
                            S.op("act", lambda e: e.activation(out=Bz[:, 0:TN], in_=psz[:, 0:TN], func=AF.Silu,
                                                               bias=vcol(l, V_BIN, C_AZ // 128 + c), scale=1.0),
                                 reads=[r_psz, r_vecs], writes=[rB])
                            psb, r_psb = inproj(ht[i], r_ht[i], TN, wc3, r_wc, C_AB // 128 + c)
                            S.op("dve", lambda e: e.scalar_tensor_tensor(out=Cg[:, 0:TN], in0=psb[:, 0:TN],
                                                                         scalar=vcol(l, V_BIN, C_AB // 128 + c), in1=Bz[:, 0:TN],
                                                                         op0=ALU.add, op1=ALU.mult),
                                 reads=[r_psb, r_vecs, rB], writes=[rC])
                            S.op("dve", lambda e: e.scalar_tensor_tensor(out=ta[:, c, 0:TN], in0=A[:, 0:TN],
                                                                         scalar=vcol(l, V_BC, c), in1=Cg[:, 0:TN],
                                                                         op0=ALU.add, op1=ALU.mult),
                                 reads=[rA, rC, r_vecs], writes=[r_ta] if c == 0 else (), partial=[r_ta] if c else ())
                        S.dma(tT[b, 0].rearrange("(k p) n -> p k n", p=128)[:, :, soff + t0:soff + t0 + TN], ta[:, :, 0:TN],
                              f"d_ta{i}", reads=[r_ta], writes=[hres("t", b, 0, soff + t0)])

            for b in range(NB):
                mcols = {"ctx": 2, "lat": b}
                act_seqs = seqs
                S.barrier()
                ps_pool["gen"] = [0, 1, 2, 3, 4]
                car = Carver(arena, ARN)
                wkv, _ = car.take(8 * 1024, "wkv"); r_wkv = WSet(8, "wkv"); wkv3 = wkv.rearrange("p (k n) -> p k n", k=8)
                wqz3, r_wqz = wkv3, r_wkv
                KT = {}
                VV = {}
                for (sname, soff, L, TN, _) in act_seqs:
                    v, r = car.take(4 * L, "KT" + sname)
                    KT[sname] = (v.rearrange("p (c n) -> p c n", c=4), r)
                    v, r = car.take((L // 128) * NH * 65, "V" + sname)
                    VV[sname] = (v.rearrange("p (t h d) -> p t h d", t=L // 128, h=NH), r)
                key0 = geo["order"][0]
                nres = len(geo["pats"][key0])
                bres, r_bres = car.take(nres * NH * 128, "bres"); bres3 = bres.rearrange("p (t k) -> p t k", k=128)
                bdyn, r_bdyn = car.take(5 * NH * 128, "bdyn"); bdyn3 = bdyn.rearrange("p (t k) -> p t k", k=128)
                wcar = Carver(work, WORKN)
                qTb, r_qT = wcar.take(4 * 512, "qT"); qT3 = qTb.rearrange("p (c n) -> p c n", c=4)
                PTs = []
                for i in range(2):
                    v, r = wcar.take(7 * 128, f"PT{i}")
                    PTs.append((v, r))
                Ons = []
                for i in range(2):
                    v, r = wcar.take(512, f"On{i}")
                    Ons.append((v, r))
                tcb = []
                for i in range(2):
                    v, r = wcar.take(4 * 512, f"tc{i}")
                    tcb.append((v.rearrange("p (g n) -> p g n", g=4), r))
                f32a, r_f32a = carve_f32a(wcar, 5)
                scz = [f32a[0], f32a[1], f32a[2], f32a[3]]; r_scz = r_f32a[0:4]
                rec, r_rec = f32a[4], r_f32a[4]
                load_cast(w_in[l], 8, 1024, wkv3, r_wkv, col0=C_K)

                def load_bias(tab0, ntabs, dst3, r_dst):
                    tot = ntabs * NH
                    c = 0
                    while c < tot:
                        w = min(16, tot - c)
                        i = rr["wst"] % 4
                        rr["wst"] += 1
                        stg = wst[i][:, 0:w * 128].rearrange("p (t k) -> p t k", k=128)
                        S.dma(stg, bias_in[l, tab0 * NH + c:tab0 * NH + c + w].rearrange("t q k -> q t k"),
                              f"d_wst{i}", writes=[r_wst[i]])
                        cast(dst3[:, c:c + w, :], stg, [r_wst[i]], [r_dst])
                        c += w
                load_bias(geo["tab_base"][key0], nres, bres3, r_bres)
                for (sname, soff, L, TN, _) in act_seqs:
                    kt3, r_kt = KT[sname]
                    v4, r_v = VV[sname]
                    S.op("pool", lambda e: e.memset(v4[:, :, :, 64:65], 1.0), partial=[r_v])
                    ntile = L // TN
                    load_h(b, soff, 0, TN, 0)
                    for tix in range(ntile):
                        t0 = tix * TN
                        i = tix % 2
                        if tix + 1 < ntile:
                            load_h(b, soff, t0 + TN, TN, (tix + 1) % 2)
                        for c in range(4):
                            ps, r_ps = inproj(ht[i], r_ht[i], TN, wkv3, r_wkv, c)
                            S.op("act", lambda e: e.activation(out=kt3[:, c, t0:t0 + TN], in_=ps[:, 0:TN], func=AF.Identity,
                                                               bias=vcol(l, V_BIN, C_K // 128 + c), scale=1.0),
                                 reads=[r_ps, r_vecs], partial=[r_kt])
                        for st in range(TN // 128):
                            ps, r_ps = psum()
                            for kc in range(8):
                                S.op("pe", lambda e: e.matmul(ps[:, :], ht[i][:, kc, st * 128:(st + 1) * 128], wkv3[:, kc, 512:1024],
                                                              start=(kc == 0), stop=(kc == 7)),
                                     reads=r_wkv.chunks[4:8] + [r_ht[i]], writes=[r_ps] if kc == 0 else (), partial=[r_ps] if kc else (), inc=(kc == 7))
                            S.op("dve", lambda e: e.tensor_copy(out=v4[:, t0 // 128 + st, :, 0:64],
                                                                in_=ps[:, :].rearrange("p (h d) -> p h d", h=NH)),
                                 reads=[r_ps], partial=[r_v])
                load_cast(w_in[l], 8, 512, wqz3, r_wqz, col0=C_Q, dcol0=0)
                load_cast(w_in[l], 8, 512, wqz3, r_wqz, col0=C_CZ, dcol0=512)
                cur_dyn = [None]
                for (sname, soff, L, TN, _) in (act_seqs if not last else act_seqs[1:]):
                    ntile = L // TN
                    kc3, r_kc = KT["ctx"]
                    vc4, r_vc = VV["ctx"]
                    kl3, r_kl = KT[sname]
                    vl4, r_vl = VV[sname]
                    load_h(b, soff, 0, TN, 0)
                    sti0 = [0]
                    jobn = [0]
                    for tix in range(ntile):
                        t0 = tix * TN
                        i = tix % 2
                        if tix + 1 < ntile:
                            load_h(b, soff, t0 + TN, TN, (tix + 1) % 2)
                        for c in range(4):
                            ps, r_ps = inproj(ht[i], r_ht[i], TN, wqz3, r_wqz, c)
                            S.op("dve", lambda e: e.tensor_scalar(out=qT3[:, c, 0:TN], in0=ps[:, 0:TN],
                                                                  scalar1=vcol(l, V_BIN, C_Q // 128 + c), scalar2=HD ** -0.5,
                                                                  op0=ALU.add, op1=ALU.mult),
                                 reads=[r_ps, r_vecs], writes=[r_qT] if c == 0 else (), partial=[r_qT] if c else ())
                        for c in range(4):
                            ps, r_ps = inproj(ht[i], r_ht[i], TN, wqz3, r_wqz, 4 + c)
                            S.op("act", lambda e: e.activation(out=scz[c][:, 0:TN], in_=ps[:, 0:TN], func=AF.Silu,
                                                               bias=vcol(l, V_BIN, C_CZ // 128 + c), scale=1.0),
                                 reads=[r_ps, r_vecs], writes=[r_scz[c]])
                        tc, r_tc = tcb[i]
                        sub_state = {}

                        def sub_get(st, gsub):
                            if st in sub_state:
                                return sub_state[st]
                            chunks = []
                            if sname == "lat":
                                key, chl = geo["subs"][gsub]
                                if key == key0:
                                    bt3, r_bt = bres3, r_bres
                                else:
                                    if cur_dyn[0] != (key,):
                                        load_bias(geo["tab_base"][key], len(chl), bdyn3, r_bdyn)
                                        cur_dyn[0] = (key,)
                                    bt3, r_bt = bdyn3, r_bdyn
                                for ci, ch in enumerate(chl):
                                    chunks.append((kl3, r_kl, vl4, r_vl, ch, (bt3, r_bt, ci)))
                            for ch in range(CTX // 128):
                                chunks.append((kc3, r_kc, vc4, r_vc, ch, None))
                            on, r_on = Ons[(sti0[0] + st) % 2]
                            sub_state[st] = dict(chunks=chunks, on=on, r_on=r_on, psO=None)
                            return sub_state[st]

                        def emit_scores(job):
                            st, h = job["st"], job["h"]
                            ss = sub_get(st, job["gsub"])
                            chunks = ss["chunks"]
                            nch = len(chunks)
                            c = h // 2
                            pb = (h % 2) * 64
                            banks = [psum() for _ in range((nch + 3) // 4)]
                            job["banks"] = banks
                            job["pt"] = PTs[jobn[0] % 2]
                            jobn[0] += 1
                            for ci, (k3, r_k, v4_, r_v_, ch, bt) in enumerate(chunks):
                                psS, r_psS = banks[ci // 4]
                                o = psS[:, (ci % 4) * 128:(ci % 4 + 1) * 128]
                                firstb = (ci % 4 == 0)
                                lastb = (ci % 4 == 3) or (ci == nch - 1)
                                S.op("pe", lambda e: e.matmul(o, k3[pb:pb + 64, c, ch * 128:(ch + 1) * 128],
                                                              qT3[pb:pb + 64, c, st * 128:(st + 1) * 128],
                                                              start=True, stop=(bt is None)),
                                     reads=[r_k, r_qT], writes=[r_psS] if firstb else (), partial=() if firstb else [r_psS],
                                     inc=(bt is None and lastb))
                                if bt is not None:
                                    bt3, r_bt, tix_ = bt
                                    S.op("pe", lambda e: e.matmul(o, bt3[:, tix_ * NH + h, :], ident, start=False, stop=True),
                                         reads=[r_bt, r_consts], partial=[r_psS], inc=lastb)

                        def emit_exp(job):
                            ss = sub_get(job["st"], job["gsub"])
                            nch = len(ss["chunks"])
                            pt, r_pt = job["pt"]
                            for bi, (psS, r_psS) in enumerate(job["banks"]):
                                n = min(4, nch - bi * 4) * 128
                                S.op("act", lambda e: e.activation(out=pt[:, bi * 512:bi * 512 + n], in_=psS[:, 0:n], func=AF.Exp),
                                     reads=[r_psS], writes=[r_pt] if bi == 0 else (), partial=[r_pt] if bi else ())

                        def emit_pv(job):
                            st, h = job["st"], job["h"]
                            ss = sub_get(st, job["gsub"])
                            chunks = ss["chunks"]
                            nch = len(chunks)
                            hl, hg = h % 4, h // 4
                            pt, r_pt = job["pt"]
                            on, r_on = ss["on"], ss["r_on"]
                            if hl == 0:
                                ss["psO"] = psum("o")
                            psO, r_psO = ss["psO"]
                            for ci, (k3, r_k, v4_, r_v_, ch, bt) in enumerate(chunks):
                                S.op("pe", lambda e: e.matmul(psO[:, hl * 65:hl * 65 + 65], pt[:, ci * 128:(ci + 1) * 128],
                                                              v4_[:, ch, h, :], start=(ci == 0), stop=(ci == nch - 1)),
                                     reads=[r_pt, r_v_], writes=[r_psO] if (hl == 0 and ci == 0) else (),
                                     partial=() if (hl == 0 and ci == 0) else [r_psO], inc=(ci == nch - 1))
                            if hl == 3:
                                pso4 = psO[:, 0:260].rearrange("p (h d) -> p h d", h=4)
                                S.op("dve", lambda e: e.reciprocal(out=rec[:, hg * 4:hg * 4 + 4], in_=pso4[:, :, 64]),
                                     reads=[r_psO], writes=[r_rec] if hg == 0 else (), partial=[r_rec] if hg else ())
                                for hl2 in range(4):
                                    h2 = hg * 4 + hl2
                                    S.op("dve", lambda e: e.tensor_scalar(out=on[:, h2 * 64:(h2 + 1) * 64], in0=psO[:, hl2 * 65:hl2 * 65 + 64],
                                                                          scalar1=rec[:, h2:h2 + 1], scalar2=None, op0=ALU.mult),
                                         reads=[r_psO, r_rec], writes=[r_on] if h2 == 0 else (), partial=[r_on] if h2 else ())
                            if h == NH - 1:
                                for c in range(4):
                                    S.op("pe", lambda e: e.transpose(pst[:, c * 128:(c + 1) * 128], on[:, c * 128:(c + 1) * 128], ident),
                                         reads=[r_on, r_consts], writes=[r_pst] if c == 0 else (), partial=[r_pst] if c else (), inc=(c == 3))
                                for c in range(4):
                                    S.op("dve", lambda e: e.scalar_tensor_tensor(out=tc[:, c, st * 128:(st + 1) * 128],
                                                                                 in0=pst[:, c * 128:(c + 1) * 128],
                                                                                 scalar=vcol(l, V_BIN, C_V // 128 + c),
                                                                                 in1=scz[c][:, st * 128:(st + 1) * 128],
                                                                                 op0=ALU.add, op1=ALU.mult),
                                         reads=[r_pst, r_vecs, r_scz[c]],
                                         writes=[r_tc] if (st == 0 and c == 0) else (), partial=() if (st == 0 and c == 0) else [r_tc])

                        jobs = [dict(st=st, h=h, gsub=t0 // 128 + st) for st in range(TN // 128) for h in range(NH)]
                        emit_scores(jobs[0])
                        for ji, job in enumerate(jobs):
                            if ji + 1 < len(jobs):
                                emit_scores(jobs[ji + 1])
                            emit_exp(job)
                            emit_pv(job)
                        sti0[0] += TN // 128
                        S.dma(tT[b, 2].rearrange("(k p) n -> p k n", p=128)[:, :, soff + t0:soff + t0 + TN], tc[:, :, 0:TN],
                              f"d_tc{i}", reads=[r_tc], writes=[hres("t", b, 2, soff + t0)])

            for b in range(NB):
                mcols = {"ctx": 2, "lat": b}
                act_seqs = seqs
                if b == 0:
                    S.barrier()
                    ps_pool["gen"] = list(range(7))
                    car = Carver(arena, ARN)
                    wg, _ = car.take(8 * 3072, "wg"); r_wg = WSet(24, "wg"); wg3 = wg.rearrange("p (k n) -> p k n", k=8)
                    wbr = []
                    for j in range(3):
                        v, r = car.take(4 * 1024, f"wbr{j}")
                        wbr.append((v.rearrange("p (k n) -> p k n", k=4), WSet(8, f"wbr{j}")))
                    wo, _ = car.take(8 * 1024, "wo"); r_wo = WSet(8, "wo"); wo3 = wo.rearrange("p (k n) -> p k n", k=8)
                    tin = []
                    for i in range(2):
                        row = []
                        for j in range(3):
                            v, r = car.take(4 * 512, f"tin{i}{j}")
                            row.append((v.rearrange("p (k n) -> p k n", k=4), r))
                        tin.append(row)
                    wcar = Carver(work, WORKN)
                    Gb, r_G = wcar.take(8 * 512, "G"); G3 = Gb.rearrange("p (k n) -> p k n", k=8)
                    f32a, r_f32a = carve_f32a(wcar, 3)
                    xq, r_xq = take_f32(wcar, 8 * 512, "xq")
                    xq3 = xq.rearrange("p (k n) -> p k n", k=8)
                    load_cast(w_in[l], 8, 3072, wg3, r_wg, col0=C_GA)
                    for j in range(3):
                        load_cast(w_br[l, j], 4, 1024, wbr[j][0], wbr[j][1])
                    load_cast(w_out[l], 8, 1024, wo3, r_wo)
                xi = 0
                for (sname, soff, L, TN, _) in (act_seqs if not last else act_seqs[1:]):
                    mc = mcols[sname]
                    ntile = L // TN

                    def load_tile(tix):
                        t0_ = tix * TN
                        i_ = tix % 2
                        load_h(b, soff, t0_, TN, i_)
                        for j in range(3):
                            S.dma(tin[i_][j][0][:, :, 0:TN],
                                  tT[b, j].rearrange("(k p) n -> p k n", p=128)[:, :, soff + t0_:soff + t0_ + TN],
                                  f"d_tin{i_}{j}", reads=[hres("t", b, j, soff + t0_)], writes=[tin[i_][j][1]])
                    load_tile(0)
                    for tix in range(ntile):
                        t0 = tix * TN
                        i = tix % 2
                        if tix + 1 < ntile:
                            load_tile(tix + 1)
                        S.dma(xq3[:, :, 0:TN], xsrc[b].rearrange("(k p) n -> p k n", p=128)[:, :, soff + t0:soff + t0 + TN],
                              "d_xq", reads=[hres("x", b, soff + t0)], writes=[r_xq])
                        for m in range(8):
                            for j in range(3):
                                tj, r_tj = tin[i][j]
                                wj, r_wj = wbr[j]
                                psy, r_psy = psum()
                                for kc in range(4):
                                    S.op("pe", lambda e: e.matmul(psy[:, 0:TN], wj[:, kc, m * 128:(m + 1) * 128], tj[:, kc, 0:TN],
                                                                  start=(kc == 0), stop=(kc == 3)),
                                         reads=[r_wj.chunks[m], r_tj], writes=[r_psy] if kc == 0 else (), partial=[r_psy] if kc else (), inc=(kc == 3))
                                psg, r_psg = inproj(ht[i], r_ht[i], TN, wg3, r_wg, j * 8 + m)
                                sg, r_sg = f32a[j], r_f32a[j]
                                S.op("act", lambda e: e.activation(out=sg[:, 0:TN], in_=psg[:, 0:TN], func=AF.Sigmoid,
                                                                   bias=vcol(l, V_BIN, C_GA // 128 + j * 8 + m), scale=1.0),
                                     reads=[r_psg, r_vecs], writes=[r_sg])
                                S.op("dve", lambda e: e.tensor_tensor(out=sg[:, 0:TN], in0=psy[:, 0:TN], in1=sg[:, 0:TN], op=ALU.mult),
                                     reads=[r_psy, r_sg], writes=[r_sg])
                            S.op("pool", lambda e: e.tensor_tensor(out=f32a[0][:, 0:TN], in0=f32a[0][:, 0:TN], in1=f32a[1][:, 0:TN], op=ALU.add),
                                 reads=[r_f32a[0], r_f32a[1]], writes=[r_f32a[0]])
                            S.op("pool", lambda e: e.tensor_tensor(out=G3[:, m, 0:TN], in0=f32a[0][:, 0:TN], in1=f32a[2][:, 0:TN], op=ALU.add),
                                 reads=[r_f32a[0], r_f32a[2]], writes=[r_G] if m == 0 else (), partial=[r_G] if m else ())
                        for mo in range(8):
                            xc = xq3[:, mo, 0:TN]
                            pso, r_pso = psum()
                            for kc in range(8):
                                S.op("pe", lambda e: e.matmul(pso[:, 0:TN], wo3[:, kc, mo * 128:(mo + 1) * 128], G3[:, kc, 0:TN],
                                                              start=(kc == 0), stop=(kc == 7)),
                                     reads=[r_wo.chunks[mo], r_G], writes=[r_pso] if kc == 0 else (), partial=[r_pso] if kc else (), inc=(kc == 7))
                            S.op("dve", lambda e: e.scalar_tensor_tensor(out=xc, in0=pso[:, 0:TN], scalar=mod[:, 16 + mo, mc:mc + 1],
                                                                         in1=xc, op0=ALU.mult, op1=ALU.add),
                                 reads=[r_pso, r_mod, r_xq], partial=[r_xq])
                        S.dma(xs[b].rearrange("(k p) n -> p k n", p=128)[:, :, soff + t0:soff + t0 + TN], xq3[:, :, 0:TN],
                              "d_xq", reads=[r_xq], writes=[hres("xn", b, soff + t0)])
                for (sname, soff, L, TN, _) in (act_seqs if not last else act_seqs[1:]):
                    for t0 in range(0, L, TN):
                        hbm_res[("x", b, soff + t0)] = hres("xn", b, soff + t0)
                        del hbm_res[("xn", b, soff + t0)]

            for b in range(NB):
                if last:
                    S.barrier()
                    ps_pool["gen"] = list(range(7))
                    car = Carver(arena, ARN)
                    sq, r_sq = car.take(8 * 512, "sq")
                    sq3 = sq.rearrange("p (k n) -> p k n", k=8)
                    xt, r_xt = carve_xt(car)
                    (sname, soff, L, TN, _) = seqs[1]
                    fgc = NVL * depth
                    for tix in range(L // TN):
                        t0 = tix * TN
                        i = tix % 2
                        S.dma(xt[i][:, :, 0:TN], xs[b].rearrange("(k p) n -> p k n", p=128)[:, :, soff + t0:soff + t0 + TN],
                              f"d_xt{i}", reads=[hres("x", b, soff + t0)], writes=[r_xt[i]])
                        S.op("act", lambda e: e.activation(out=sq3[:, :, 0:TN], in_=xt[i][:, :, 0:TN], func=AF.Square),
                             reads=[r_xt[i]], writes=[r_sq])
                        ps, r_ps = psum()
                        for kc in range(8):
                            S.op("pe", lambda e: e.matmul(ps[:, 0:TN], ones, sq3[:, kc, 0:TN], start=(kc == 0), stop=(kc == 7)),
                                 reads=[r_sq, r_consts], writes=[r_ps] if kc == 0 else (), partial=[r_ps] if kc else (), inc=(kc == 7))
                        S.op("act", lambda e: e.activation(out=rstd[i][:, 0:TN], in_=ps[:, 0:TN], func=AF.Sqrt, bias=epsc[:, 0:1], scale=1.0 / D),
                             reads=[r_ps, r_consts], writes=[r_rstd[i]])
                        S.op("dve", lambda e: e.reciprocal(out=rstd[i][:, 0:TN], in_=rstd[i][:, 0:TN]),
                             reads=[r_rstd[i]], writes=[r_rstd[i]])
                        for kc in range(8):
                            S.op("dve",
                                 lambda e: e.scalar_tensor_tensor(out=xt[i][:, kc, 0:TN], in0=xt[i][:, kc, 0:TN],
                                                                  scalar=vecs[:, fgc + kc:fgc + kc + 1], in1=rstd[i][:, 0:TN],
                                                                  op0=ALU.mult, op1=ALU.mult),
                                 reads=[r_vecs, r_rstd[i], r_xt[i]], writes=[r_xt[i]])
                        S.dma(y_out[b].rearrange("(k p) n -> p k n", p=128)[:, :, t0:t0 + TN], xt[i][:, :, 0:TN],
                              f"d_xt{i}", reads=[r_xt[i]], writes=[hres("y", b, t0)])
        S.barrier(engines=("sp",))
        print("program built: ninst", S.ninst, "nwaits", S.nwaits, "nsems", len(S.sems))
    return nc


_CACHE = {}


def _consts(seq):
    key = ("c", seq)
    if key not in _CACHE:
        cst = np.zeros((128, 512), dtype=NPBF)
        cst[:, 0:128] = np.eye(128).astype(NPBF)
        cst[:, 128:256] = np.ones((128, 128)).astype(NPBF)
        cst[:, 256:512] = _chan_table().astype(NPBF)
        dftl = _dft_pos_table(seq, 512, 4)
        dftc = _dft_pos_table(CTX, 256, 2)
        _CACHE[key] = (cst, dftl, dftc)
    return _CACHE[key]


def kernel(x, c, ctx, c_ctx, norm_g, w_ada, b_ada, w_in, b_in, w_conv, b_conv,
           rpb, w_br_a, w_br_f, w_br_c, w_out, final_g):
    x = np.asarray(x, dtype=np.float32)
    depth = int(np.asarray(w_in).shape[0])
    seq = int(x.shape[1])
    B = int(x.shape[0])
    assert B == NB * NCORES
    geo = _attn_geometry(seq)
    cst, dftl, dftc = _consts(seq)
    f = lambda a: np.ascontiguousarray(np.asarray(a, dtype=np.float32))
    c, ctx, c_ctx = f(c), f(ctx), f(c_ctx)
    norm_g, b_ada, b_in, w_conv, b_conv, final_g = f(norm_g), f(b_ada), f(b_in), f(w_conv), f(b_conv), f(final_g)
    w_ada, w_in, w_out = f(w_ada), f(w_in), f(w_out)
    w_br = np.ascontiguousarray(np.stack([f(w_br_a), f(w_br_f), f(w_br_c)], axis=1))
    rpb = f(rpb)
    NV = NVL * depth + 8
    vecs = np.zeros((128, NV), dtype=np.float32)
    for l in range(depth):
        o = l * NVL
        vecs[:, o + V_NG:o + V_NG + 8] = norm_g[l].reshape(8, 128).T
        vecs[:, o + V_BADA:o + V_BADA + 24] = b_ada[l].reshape(24, 128).T
        vecs[:, o + V_BIN:o + V_BIN + 64] = b_in[l].reshape(64, 128).T
        vecs[:, o + V_WC:o + V_WC + 12] = w_conv[l].reshape(3, 4, 128).transpose(2, 1, 0).reshape(128, 12)
        vecs[:, o + V_BC:o + V_BC + 4] = b_conv[l].reshape(4, 128).T
    vecs[:, NVL * depth:NVL * depth + 8] = final_g.reshape(8, 128).T
    bias_tab = np.stack([_bias_tables(rpb[l], geo).reshape(geo["ntab"] * NH, 128, 128) for l in range(depth)], axis=0)
    xall = np.concatenate([ctx.transpose(0, 2, 1), x.transpose(0, 2, 1)], axis=2)
    key = ("nc", seq, depth)
    if key not in _CACHE:
        _CACHE[key] = build_program(seq, depth)
    nc = _CACHE[key]
    in_maps = []
    for core in range(NCORES):
        b0 = core * NB
        cT = np.stack([c[b0], c[b0 + 1], c_ctx], axis=1)
        cT = np.ascontiguousarray(cT.reshape(8, 128, 3).transpose(1, 0, 2))
        in_maps.append({
            "x_in": np.ascontiguousarray(xall[b0:b0 + NB]),
            "cT": cT, "vecs": vecs, "w_ada": w_ada, "w_in": w_in, "w_br": w_br, "w_out": w_out,
            "bias_tab": bias_tab, "consts": cst, "dftl": dftl, "dftc": dftc,
        })
    res = run_bass_kernel_spmd(nc, in_maps, core_ids=list(range(NCORES)))
    ys = [np.asarray(r["y"], dtype=np.float32) for r in res.results]
    y = np.concatenate(ys, axis=0)
    return np.ascontiguousarray(y.transpose(0, 2, 1))
```

```python
import contextlib
import numpy as np
import ml_dtypes
import concourse.bass as bass
import concourse.mybir as mybir
from concourse.bass_utils import run_bass_kernel_spmd

F32 = mybir.dt.float32
BF16 = mybir.dt.bfloat16
AF = mybir.ActivationFunctionType
ALU = mybir.AluOpType
NPBF = ml_dtypes.bfloat16

D = 1024
SEQ = 4096
DEPTH = 4
NCORES = 8
NB = 2
CTX = 256
GW = 64
NH = 8
HD = 64
WIN_H = 8
WIN_W = 16
DIN = 8192
EPS = 1e-6
NEG = -30000.0
C_AX, C_AB, C_AC, C_AZ, C_FU, C_FZ, C_Q, C_K, C_V, C_CZ, C_GA, C_GF, C_GC = (
    0, 512, 1024, 1536, 2048, 2560, 3072, 3584, 4096, 4608, 5120, 6144, 7168)
V_NG, V_BADA, V_BIN, V_WC, V_BC, NVL = 0, 8, 32, 96, 108, 112


class Res:
    __slots__ = ("name", "lw", "rd")

    def __init__(self, name):
        self.name = name
        self.lw = {}
        self.rd = {}


class Sched:
    def __init__(self, nc, es):
        self.nc = nc
        self.es = es
        self.eng = {"pe": nc.tensor, "act": nc.scalar, "dve": nc.vector, "pool": nc.gpsimd, "sp": nc.sync}
        self.sems = {}
        self.cnt = {}
        self.isdma = {}
        self.waited = {e: {} for e in self.eng}
        for e in ("pe", "act", "dve", "pool"):
            self._sem(e, False)
        self.ninst = 0
        self.nwaits = 0

    def _sem(self, key, isdma):
        if key not in self.sems:
            self.sems[key] = self.es.enter_context(self.nc.semaphore("s_" + str(key)))
            self.cnt[key] = 0
            self.isdma[key] = isdma
        return self.sems[key]

    def _wait(self, e, k, v):
        if self.isdma[k]:
            v = self.cnt[k]
        w = self.waited[e]
        if w.get(k, 0) >= v:
            return
        if e == "pe" and k == "pe":
            return
        self.eng[e].wait_ge(self.sems[k], v)
        w[k] = v
        self.nwaits += 1

    def deps(self, e, reads, writes, partial):
        for r in reads:
            for k, v in r.lw.items():
                self._wait(e, k, v)
        for w_ in writes:
            for k, v in w_.lw.items():
                self._wait(e, k, v)
            for k, v in w_.rd.items():
                self._wait(e, k, v)
        for w_ in partial:
            for k, v in w_.rd.items():
                self._wait(e, k, v)

    def mark(self, ev, reads, writes, partial):
        k, v = ev
        for r in reads:
            if r.rd.get(k, 0) < v:
                r.rd[k] = v
        for w_ in writes:
            w_.lw = {k: v}
            w_.rd = {}
        for w_ in partial:
            if w_.lw.get(k, 0) < v:
                w_.lw[k] = v

    def op(self, e, fn, reads=(), writes=(), partial=(), inc=True):
        self.deps(e, reads, writes, partial)
        ins = fn(self.eng[e])
        self.ninst += 1
        if inc:
            self.cnt[e] += 1
            ins.then_inc(self.sems[e], 1)
            ev = (e, self.cnt[e])
        else:
            ev = (e, self.cnt[e] + 1)
        self.mark(ev, reads, writes, partial)
        return ins

    def dma(self, out, in_, semkey, reads=(), writes=(), partial=(), e="sp"):
        self._sem(semkey, True)
        self.deps(e, reads, writes, partial)
        ins = self.eng[e].dma_start(out=out, in_=in_)
        self.ninst += 1
        self.cnt[semkey] += 16
        ins.then_inc(self.sems[semkey], 16)
        self.mark((semkey, self.cnt[semkey]), reads, writes, partial)
        return ins

    def barrier(self, engines=("pe", "act", "dve", "pool", "sp")):
        for e in engines:
            for k in list(self.sems.keys()):
                if self.cnt[k] > 0:
                    self._wait(e, k, self.cnt[k])


class Buf:
    def __init__(self, ap_fn, name):
        self.t = ap_fn
        self.r = Res(name)


def _dft_pos_table(L, TN, LG):
    m = np.arange(L, dtype=np.float64)
    cosv = np.cos(2 * np.pi * m / L) / np.sqrt(L)
    sinv = -np.sin(2 * np.pi * m / L) / np.sqrt(L)
    l = np.arange(L, dtype=np.int64)
    idx = (l[:, None] * l[None, :]) % L
    ntile = L // TN
    nlc = L // 128
    nlg = nlc // LG
    out = np.empty((ntile, nlg, 128, LG, 2, TN), dtype=NPBF)
    for cs, tab in ((0, cosv), (1, sinv)):
        full = tab[idx].astype(NPBF)
        full = full.reshape(nlg, LG, 128, ntile, TN)
        out[:, :, :, :, cs, :] = full.transpose(3, 0, 2, 1, 4)
    return out


def _chan_table():
    c = np.arange(128, dtype=np.int64)
    idx = (c[:, None] * c[None, :]) % 128
    ang = 2 * np.pi * idx / 128.0
    return np.concatenate([np.cos(ang), np.sin(ang)], axis=1) / np.sqrt(128.0)


def _attn_geometry(seq):
    R = seq // GW
    kh = min(WIN_H, R)

    def rs(r):
        return int(np.clip(r - kh // 2, 0, R - kh))
    subs = []
    pats = {}
    for j in range(R // 2):
        r0 = 2 * j
        a0, a1 = rs(r0) - r0, rs(r0 + 1) - r0
        lo, hi = rs(r0), rs(r0 + 1) + kh - 1
        chunks = list(range(lo // 2, hi // 2 + 1))
        key = (a0, a1)
        deltas = tuple(2 * c - r0 for c in chunks)
        if key not in pats:
            pats[key] = deltas
        assert pats[key] == deltas
        subs.append((key, chunks))
    cnts = {}
    for key, _ in subs:
        cnts[key] = cnts.get(key, 0) + 1
    order = sorted(pats.keys(), key=lambda k: -cnts[k])
    tab_base = {}
    n = 0
    for key in order:
        tab_base[key] = n
        n += len(pats[key])
    return dict(R=R, kh=kh, subs=subs, pats=pats, order=order, tab_base=tab_base, ntab=n)


def _bias_tables(rpb_l, geo):
    kh = geo["kh"]
    out = np.full((geo["ntab"], NH, 128, 128), NEG, dtype=np.float32)
    qc = np.arange(GW)
    kc = np.arange(GW)
    cs = np.clip(qc - WIN_W // 2, 0, GW - WIN_W)
    col_ok = (kc[None, :] >= cs[:, None]) & (kc[None, :] < cs[:, None] + WIN_W)
    dc = np.clip(kc[None, :] - qc[:, None] + WIN_W - 1, 0, 2 * WIN_W - 2)
    for key in geo["order"]:
        a0, a1 = key
        for ci, dl in enumerate(geo["pats"][key]):
            t = geo["tab_base"][key] + ci
            for qr in range(2):
                a = a0 if qr == 0 else a1
                for krl in range(2):
                    rel = dl + krl
                    if not (a <= rel < a + kh):
                        continue
                    dr = rel - qr + WIN_H - 1
                    blk = np.where(col_ok[None], rpb_l[:, dr][:, dc], np.float32(NEG))
                    out[t, :, qr * 64:(qr + 1) * 64, krl * 64:(krl + 1) * 64] = blk
    res = np.empty((geo["ntab"] * NH, 128, 128), dtype=np.float32)
    for key in geo["order"]:
        tb, n = geo["tab_base"][key], len(geo["pats"][key])
        sub = out[tb:tb + n]
        res[tb * NH:(tb + n) * NH] = sub.transpose(1, 0, 3, 2).reshape(n * NH, 128, 128)
    return res


def build_program(seq, depth):
    geo = _attn_geometry(seq)
    LT = CTX + seq
    nc = bass.Bass("TRN2", target_bir_lowering=False)
    NV = NVL * depth + 8
    x_in = nc.dram_tensor("x_in", [NB, D, LT], F32, kind="ExternalInput").ap()
    cT_in = nc.dram_tensor("cT", [128, 8, 3], F32, kind="ExternalInput").ap()
    vecs_in = nc.dram_tensor("vecs", [128, NV], F32, kind="ExternalInput").ap()
    w_ada = nc.dram_tensor("w_ada", [depth, D, 3 * D], F32, kind="ExternalInput").ap()
    w_in = nc.dram_tensor("w_in", [depth, D, DIN], F32, kind="ExternalInput").ap()
    w_br = nc.dram_tensor("w_br", [depth, 3, 512, D], F32, kind="ExternalInput").ap()
    w_out = nc.dram_tensor("w_out", [depth, D, D], F32, kind="ExternalInput").ap()
    bias_in = nc.dram_tensor("bias_tab", [depth, geo["ntab"] * NH, 128, 128], F32, kind="ExternalInput").ap()
    consts_in = nc.dram_tensor("consts", [128, 512], BF16, kind="ExternalInput").ap()
    LGL = 4
    dftl_in = nc.dram_tensor("dftl", [seq // 512, seq // 128 // LGL, 128, LGL, 2, 512], BF16, kind="ExternalInput").ap()
    dftc_in = nc.dram_tensor("dftc", [1, 1, 128, 2, 2, 256], BF16, kind="ExternalInput").ap()
    y_out = nc.dram_tensor("y", [NB, D, seq], F32, kind="ExternalOutput").ap()
    xs = nc.dram_tensor("xs", [NB, D, LT], F32, kind="Internal").ap()
    hT = nc.dram_tensor("hT", [NB, D, LT], BF16, kind="Internal").ap()
    tT = nc.dram_tensor("tT", [NB, 3, 512, LT], BF16, kind="Internal").ap()

    es = contextlib.ExitStack()
    with es:
        S = Sched(nc, es)

        def sb(name, shape, dt):
            return es.enter_context(nc.sbuf_tensor("sb_" + name, shape, dt))

        vecs = sb("vecs", [128, NV], F32); r_vecs = Res("vecs")
        consts = sb("consts", [128, 512], BF16); r_consts = Res("consts")
        ident = consts[:, 0:128]
        ones = consts[:, 128:256]
        csc = consts[:, 256:512]
        epsc = sb("epsc", [128, 1], F32)
        cts = sb("cts", [128, 8, 3], F32); r_cts = Res("cts")
        mod = sb("mod", [128, 24, 3], F32); r_mod = Res("mod")
        gm = sb("gm", [128, 8, 3], F32); r_gm = Res("gm")
        ht = [sb(f"ht{i}", [128, 8, 512], BF16) for i in range(2)]; r_ht = [Res(f"ht{i}") for i in range(2)]
        wst = [sb(f"wst{i}", [128, 2048], F32) for i in range(4)]; r_wst = [Res(f"wst{i}") for i in range(4)]
        rstd = [sb(f"rstd{i}", [128, 512], F32) for i in range(2)]; r_rstd = [Res(f"rstd{i}") for i in range(2)]
        WORKN = 16384
        work = sb("work", [128, WORKN], BF16)
        ARN = 57344
        arena = sb("arena", [128, ARN], BF16)
        psf = [es.enter_context(nc.psum_tensor(f"psf{i}", [128, 512], F32)) for i in range(7)]
        r_psf = [Res(f"psf{i}") for i in range(7)]
        pst = es.enter_context(nc.psum_tensor("pst", [128, 1024], BF16)); r_pst = Res("pst")
        ps_pool = {"gen": list(range(7)), "acc": [3, 4, 5, 6], "o": [5, 6]}
        ps_rr = {"gen": 0, "acc": 0, "o": 0}

        def psum(pool="gen"):
            lst = ps_pool[pool]
            i = lst[ps_rr[pool] % len(lst)]
            ps_rr[pool] += 1
            return psf[i], r_psf[i]

        hbm_res = {}

        def hres(*key):
            if key not in hbm_res:
                hbm_res[key] = Res(str(key))
            return hbm_res[key]

        class Carver:
            def __init__(self, base, total):
                self.base = base
                self.total = total
                self.off = 0

            def take(self, n, name):
                assert self.off + n <= self.total, (name, self.off, n, self.total)
                v = self.base[:, self.off:self.off + n]
                self.off += n
                return v, Res(name)

        def take_f32(carver, n, name):
            v, r = carver.take(2 * n, name)
            return v.bitcast(F32), r

        def carve_xt(carver):
            xs_, rs_ = [], []
            for i in range(2):
                v, r = take_f32(carver, 8 * 512, f"xt{i}")
                xs_.append(v.rearrange("p (k n) -> p k n", k=8))
                rs_.append(r)
            return xs_, rs_

        def carve_f32a(carver, n):
            fs_, rs_ = [], []
            for i in range(n):
                v, r = take_f32(carver, 512, f"f32a{i}")
                fs_.append(v)
                rs_.append(r)
            return fs_, rs_

        rr = {"evac": 0, "wst": 0, "cast": 0}
        di_glob = [0]

        class WSet:
            def __init__(self, n, name):
                self.chunks = [Res(f"{name}{i}") for i in range(n)]

        def cast(out, in_, reads, partial):
            if rr["cast"] % 2 == 0:
                S.op("act", lambda e: e.activation(out=out, in_=in_, func=AF.Copy), reads=reads, partial=partial)
            else:
                S.op("dve", lambda e: e.tensor_copy(out=out, in_=in_), reads=reads, partial=partial)
            rr["cast"] += 1

        def load_cast(src2d, kcn, ncols, dst3, wset, col0=0, dcol0=0):
            src = src2d.rearrange("(kc p) n -> p kc n", p=128)
            piece = max(1, 2048 // kcn)
            piece = min(piece, ncols)
            c = 0
            while c < ncols:
                w = min(piece, ncols - c)
                i = rr["wst"] % 4
                rr["wst"] += 1
                stg = wst[i][:, 0:kcn * w].rearrange("p (k n) -> p k n", k=kcn)
                S.dma(stg, src[:, :, col0 + c:col0 + c + w], f"d_wst{i}", writes=[r_wst[i]])
                cks = wset.chunks[(dcol0 + c) // 128:(dcol0 + c + w + 127) // 128]
                cast(dst3[:, :, dcol0 + c:dcol0 + c + w], stg, [r_wst[i]], cks)
                c += w

        def inproj(htb, r_htb, N, wv, r_w, mchunk):
            ps, r_ps = psum()
            for kc in range(8):
                S.op("pe", lambda e: e.matmul(ps[:, 0:N], wv[:, kc, mchunk * 128:(mchunk + 1) * 128], htb[:, kc, 0:N],
                                              start=(kc == 0), stop=(kc == 7)),
                     reads=[r_w.chunks[mchunk], r_htb], writes=[r_ps] if kc == 0 else (), partial=[r_ps] if kc else (), inc=(kc == 7))
            return ps, r_ps

        def vcol(l, off, j=0):
            c = l * NVL + off + j
            return vecs[:, c:c + 1]

        def load_h(b, seqoff, t0, N, slot):
            S.dma(ht[slot][:, :, 0:N],
                  hT[b].rearrange("(k p) n -> p k n", p=128)[:, :, seqoff + t0:seqoff + t0 + N],
                  f"d_ht{slot}", reads=[hres("h", b, seqoff + t0)], writes=[r_ht[slot]])

        seqs = [("ctx", 0, CTX, 256, 2), ("lat", CTX, seq, 512, 3)]

        S.dma(vecs[:, :], vecs_in[:, :], "d_misc", writes=[r_vecs])
        S.dma(consts[:, :], consts_in[:, :], "d_misc", writes=[r_consts])
        S.dma(cts[:, :, :], cT_in[:, :, :], "d_misc", writes=[r_cts])
        S.op("pool", lambda e: e.memset(epsc[:, :], EPS), partial=[r_consts])
        S.op("act", lambda e: e.activation(out=cts[:, :, :], in_=cts[:, :, :], func=AF.Silu), reads=[r_cts], writes=[r_cts])

        for l in range(depth):
            last = (l == depth - 1)
            xsrc = x_in if l == 0 else xs
            S.barrier()
            ps_pool["gen"] = list(range(7))
            car = Carver(arena, ARN)
            xt, r_xt = carve_xt(car)
            for pc in range(8):
                i = pc % 2
                stg = xt[i][:, :, 0:384]
                S.dma(stg, w_ada[l].rearrange("(kc p) n -> p kc n", p=128)[:, :, pc * 384:(pc + 1) * 384],
                      f"d_xt{i}", writes=[r_xt[i]])
                ps, r_ps = psum()
                for mm in range(3):
                    for kc in range(8):
                        S.op("pe", lambda e: e.matmul(ps[:, mm * 4:mm * 4 + 3], stg[:, kc, mm * 128:(mm + 1) * 128],
                                                      cts[:, kc, :], start=(kc == 0), stop=(kc == 7)),
                             reads=[r_xt[i], r_cts], writes=[r_ps] if (mm == 0 and kc == 0) else (),
                             partial=() if (mm == 0 and kc == 0) else [r_ps], inc=(mm == 2 and kc == 7))
                for mm in range(3):
                    m = pc * 3 + mm
                    S.op("dve", lambda e: e.tensor_scalar(out=mod[:, m, :], in0=ps[:, mm * 4:mm * 4 + 3],
                                                          scalar1=vcol(l, V_BADA, m), scalar2=None, op0=ALU.add),
                         reads=[r_ps, r_vecs], partial=[r_mod])
            for j in range(3):
                S.op("dve", lambda e: e.scalar_tensor_tensor(out=gm[:, :, j], in0=mod[:, 8:16, j], scalar=1.0,
                                                             in1=vecs[:, l * NVL + V_NG:l * NVL + V_NG + 8],
                                                             op0=ALU.add, op1=ALU.mult),
                     reads=[r_mod, r_vecs], partial=[r_gm])

            for b in range(NB):
                mcols = {"ctx": 2, "lat": b}
                act_seqs = seqs
                S.barrier()
                ps_pool["gen"] = list(range(7))
                car = Carver(arena, ARN)
                sq, r_sq = car.take(8 * 512, "sq")
                r_sqh = [r_sq, Res("sq_hi")]
                sq3 = sq.rearrange("p (k n) -> p k n", k=8)
                xt, r_xt = carve_xt(car)
                wcar = Carver(work, WORKN)
                f32a, r_f32a = carve_f32a(wcar, 2)
                p1tiles = [(sname, soff, t0, TN, mcols[sname]) for (sname, soff, L, TN, _) in act_seqs for t0 in range(0, L, TN)]

                def p1load(n):
                    (sname_, soff_, t0_, TN_, mc_) = p1tiles[n]
                    S.dma(xt[n % 2][:, :, 0:TN_],
                          xsrc[b].rearrange("(k p) n -> p k n", p=128)[:, :, soff_ + t0_:soff_ + t0_ + TN_],
                          f"d_xt{n % 2}", reads=[hres("x", b, soff_ + t0_)], writes=[r_xt[n % 2]])
                p1load(0)
                for ti, (sname, soff, t0, TN, mc) in enumerate(p1tiles):
                    if True:
                        i = ti % 2
                        if ti + 1 < len(p1tiles):
                            p1load(ti + 1)
                        S.op("act", lambda e: e.activation(out=sq3[:, 0:4, 0:TN], in_=xt[i][:, 0:4, 0:TN], func=AF.Square),
                             reads=[r_xt[i]], writes=[r_sqh[0]])
                        S.op("dve", lambda e: e.tensor_tensor(out=sq3[:, 4:8, 0:TN], in0=xt[i][:, 4:8, 0:TN], in1=xt[i][:, 4:8, 0:TN], op=ALU.mult),
                             reads=[r_xt[i]], writes=[r_sqh[1]])
                        ps, r_ps = psum()
                        for kc in range(8):
                            S.op("pe", lambda e: e.matmul(ps[:, 0:TN], ones, sq3[:, kc, 0:TN], start=(kc == 0), stop=(kc == 7)),
                                 reads=[r_sqh[kc // 4], r_consts], writes=[r_ps] if kc == 0 else (), partial=[r_ps] if kc else (),
                                 inc=(kc == 7))
                        S.op("act", lambda e: e.activation(out=rstd[i][:, 0:TN], in_=ps[:, 0:TN], func=AF.Sqrt, bias=epsc[:, 0:1], scale=1.0 / D),
                             reads=[r_ps, r_consts], writes=[r_rstd[i]])
                        S.op("dve", lambda e: e.reciprocal(out=rstd[i][:, 0:TN], in_=rstd[i][:, 0:TN]),
                             reads=[r_rstd[i]], writes=[r_rstd[i]])
                        for kc in range(8):
                            fi = kc % 2
                            S.op("dve", lambda e: e.scalar_tensor_tensor(out=f32a[fi][:, 0:TN], in0=xt[i][:, kc, 0:TN],
                                                                         scalar=gm[:, kc, mc:mc + 1], in1=rstd[i][:, 0:TN],
                                                                         op0=ALU.mult, op1=ALU.mult),
                                 reads=[r_xt[i], r_gm, r_rstd[i]], writes=[r_f32a[fi]])
                            S.op("act", lambda e: e.activation(out=ht[i][:, kc, 0:TN], in_=f32a[fi][:, 0:TN], func=AF.Identity,
                                                               bias=mod[:, kc, mc:mc + 1], scale=1.0),
                                 reads=[r_f32a[fi], r_mod], writes=[r_ht[i]] if kc == 0 else (), partial=[r_ht[i]] if kc else ())
                        S.dma(hT[b].rearrange("(k p) n -> p k n", p=128)[:, :, soff + t0:soff + t0 + TN], ht[i][:, :, 0:TN],
                              f"d_ht{i}", reads=[r_ht[i]], writes=[hres("h", b, soff + t0)])

            for b in range(NB):
                mcols = {"ctx": 2, "lat": b}
                act_seqs = seqs
                for si_, (sname, soff, L, TN, _) in enumerate(act_seqs if not last else act_seqs[1:]):
                    if b == 0 and si_ == 0:
                        S.barrier()
                        ps_pool["gen"] = [0, 1, 2]
                        car = Carver(arena, ARN)
                        wfu, _ = car.take(8 * 512, "wfu"); r_wfu = WSet(4, "wfu"); wfu3 = wfu.rearrange("p (k n) -> p k n", k=8)
                        wfz, _ = car.take(8 * 512, "wfz"); r_wfz = WSet(4, "wfz"); wfz3 = wfz.rearrange("p (k n) -> p k n", k=8)
                        ABfull, r_AB = car.take((seq // 128) * 1024, "AB")
                        dslfull = [car.take(LGL * 2 * 512, f"dft{i}") for i in range(4)]
                        wcar = Carver(work, WORKN)
                        UT = []
                        for i in range(2):
                            v, r = wcar.take(4 * 512, f"UT{i}")
                            UT.append((v.rearrange("p (g n) -> p g n", g=4), r))
                        szb, r_sz = wcar.take(4 * 512, "sz"); sz3 = szb.rearrange("p (g n) -> p g n", g=4)
                        tfb = []
                        for i in range(2):
                            v, r = wcar.take(4 * 512, f"tf{i}")
                            tfb.append((v.rearrange("p (g n) -> p g n", g=4), r))
                        load_cast(w_in[l], 8, 512, wfu3, r_wfu, col0=C_FU)
                        load_cast(w_in[l], 8, 512, wfz3, r_wfz, col0=C_FZ)
                    LC = L // 128
                    AB = ABfull[:, 0:LC * 1024]
                    AB5 = AB.rearrange("p (lc g cs c) -> p lc g cs c", lc=LC, g=4, cs=2)
                    AB3 = AB.rearrange("p (lc x) -> p lc x", lc=LC)
                    LG = LGL if sname == "lat" else 2
                    dsl = [(v_[:, 0:LG * 2 * TN].rearrange("p (a cs n) -> p a cs n", a=LG, cs=2), r_) for (v_, r_) in dslfull]
                    ntile = L // TN
                    load_h(b, soff, 0, TN, 0)
                    for tix in range(ntile):
                        t0 = tix * TN
                        i = tix % 2
                        if tix + 1 < ntile:
                            load_h(b, soff, t0 + TN, TN, (tix + 1) % 2)
                        ut, r_ut = UT[i]
                        for g in range(4):
                            ps, r_ps = inproj(ht[i], r_ht[i], TN, wfu3, r_wfu, g)
                            S.op("act", lambda e: e.activation(out=ut[:, g, 0:TN], in_=ps[:, 0:TN], func=AF.Identity,
                                                               bias=vcol(l, V_BIN, C_FU // 128 + g), scale=1.0),
                                 reads=[r_ps, r_vecs], writes=[r_ut] if g == 0 else (), partial=[r_ut] if g else ())
                        for st in range(TN // 128):
                            lc = (t0 // 128) + st
                            for half in range(2):
                                ps, r_ps = psum()
                                for gg in range(2):
                                    g = half * 2 + gg
                                    S.op("pe", lambda e: e.matmul(ps[:, gg * 256:(gg + 1) * 256], ut[:, g, st * 128:(st + 1) * 128], csc,
                                                                  start=True, stop=True),
                                         reads=[r_ut, r_consts], writes=[r_ps] if gg == 0 else (), partial=[r_ps] if gg else (),
                                         inc=(gg == 1))
                                dst = AB3[:, lc, half * 512:(half + 1) * 512]
                                if rr["evac"] % 2 == 0:
                                    S.op("act", lambda e: e.activation(out=dst, in_=ps[:, :], func=AF.Copy), reads=[r_ps], partial=[r_AB])
                                else:
                                    S.op("dve", lambda e: e.tensor_copy(out=dst, in_=ps[:, :]), reads=[r_ps], partial=[r_AB])
                                rr["evac"] += 1
                    dsrc = dftl_in if sname == "lat" else dftc_in
                    nlg = LC // LG
                    pieces = [(tix_, lg_) for tix_ in range(ntile) for lg_ in range(nlg)]
                    dbase = di_glob[0]

                    def dft_load(n_):
                        tix_, lg_ = pieces[n_]
                        sl_ = (dbase + n_) % 4
                        S.dma(dsl[sl_][0], dsrc[tix_, lg_], f"d_dft{sl_}", writes=[dsl[sl_][1]])
                    for n_ in range(min(3, len(pieces))):
                        dft_load(n_)
                    load_h(b, soff, 0, TN, 0)
                    for tix in range(ntile):
                        t0 = tix * TN
                        i = tix % 2
                        if tix + 1 < ntile:
                            load_h(b, soff, t0 + TN, TN, (tix + 1) % 2)
                        accs = [psum("acc") for _ in range(4)]
                        for lg in range(nlg):
                            n_ = tix * nlg + lg
                            if n_ + 3 < len(pieces):
                                dft_load(n_ + 3)
                            dv, r_dv = dsl[(dbase + n_) % 4]
                            for a in range(LG):
                                lc = lg * LG + a
                                for cs in range(2):
                                    first = (lc == 0 and cs == 0)
                                    lastm = (lc == LC - 1 and cs == 1)
                                    for g in range(4):
                                        ps, r_ps = accs[g]
                                        S.op("pe", lambda e: e.matmul(ps[:, 0:TN], AB5[:, lc, g, cs, :], dv[:, a, cs, 0:TN],
                                                                      start=first, stop=lastm),
                                             reads=[r_AB, r_dv], writes=[r_ps] if first else (), partial=() if first else [r_ps],
                                             inc=(lastm or (a == LG - 1 and cs == 1 and g == 3)))
                        tf, r_tf = tfb[i]
                        for g in range(4):
                            ps, r_ps = inproj(ht[i], r_ht[i], TN, wfz3, r_wfz, g)
                            S.op("act", lambda e: e.activation(out=sz3[:, g, 0:TN], in_=ps[:, 0:TN], func=AF.Silu,
                                                               bias=vcol(l, V_BIN, C_FZ // 128 + g), scale=1.0),
                                 reads=[r_ps, r_vecs], writes=[r_sz] if g == 0 else (), partial=[r_sz] if g else ())
                            psY, r_psY = accs[g]
                            S.op("dve", lambda e: e.tensor_tensor(out=tf[:, g, 0:TN], in0=psY[:, 0:TN], in1=sz3[:, g, 0:TN], op=ALU.mult),
                                 reads=[r_psY, r_sz], writes=[r_tf] if g == 0 else (), partial=[r_tf] if g else ())
                        S.dma(tT[b, 1].rearrange("(k p) n -> p k n", p=128)[:, :, soff + t0:soff + t0 + TN], tf[:, :, 0:TN],
                              f"d_tf{i}", reads=[r_tf], writes=[hres("t", b, 1, soff + t0)])
                    di_glob[0] += len(pieces)

            for b in range(NB):
                mcols = {"ctx": 2, "lat": b}
                act_seqs = seqs
                for si_, (sname, soff, L, TN, _) in enumerate(act_seqs if not last else act_seqs[1:]):
                    if b == 0 and si_ == 0:
                        S.barrier()
                        ps_pool["gen"] = list(range(7))
                        car = Carver(arena, ARN)
                        wc, _ = car.take(8 * 2048, "wconv"); r_wc = WSet(16, "wconv"); wc3 = wc.rearrange("p (k n) -> p k n", k=8)
                        ubfull, r_u = car.take(4 * (seq + 2), "u")
                        wcar = Carver(work, WORKN)
                        tab = []
                        for i in range(2):
                            v, r = wcar.take(4 * 512, f"ta{i}")
                            tab.append((v.rearrange("p (g n) -> p g n", g=4), r))
                        f32a, r_f32a = carve_f32a(wcar, 6)
                        load_cast(w_in[l], 8, 2048, wc3, r_wc, col0=0)
                    u3 = ubfull[:, 0:4 * (L + 2)].rearrange("p (c n) -> p c n", c=4)
                    S.op("pool", lambda e: e.memset(u3[:, :, 0:1], 0.0), partial=[r_u])
                    S.op("pool", lambda e: e.memset(u3[:, :, L + 1:L + 2], 0.0), partial=[r_u])
                    ntile = L // TN
                    load_h(b, soff, 0, TN, 0)
                    for tix in range(ntile):
                        t0 = tix * TN
                        i = tix % 2
                        if tix + 1 < ntile:
                            load_h(b, soff, t0 + TN, TN, (tix + 1) % 2)
                        for c in range(4):
                            fi = c % 2
                            ps, r_ps = inproj(ht[i], r_ht[i], TN, wc3, r_wc, C_AX // 128 + c)
                            S.op("act", lambda e: e.activation(out=f32a[fi][:, 0:TN], in_=ps[:, 0:TN], func=AF.Identity,
                                                               bias=vcol(l, V_BIN, C_AX // 128 + c), scale=1.0),
                                 reads=[r_ps, r_vecs], writes=[r_f32a[fi]])
                            ps2, r_ps2 = inproj(ht[i], r_ht[i], TN, wc3, r_wc, C_AC // 128 + c)
                            S.op("dve", lambda e: e.scalar_tensor_tensor(out=u3[:, c, 1 + t0:1 + t0 + TN], in0=ps2[:, 0:TN],
                                                                         scalar=vcol(l, V_BIN, C_AC // 128 + c), in1=f32a[fi][:, 0:TN],
                                                                         op0=ALU.add, op1=ALU.mult),
                                 reads=[r_ps2, r_vecs, r_f32a[fi]], partial=[r_u])
                    load_h(b, soff, 0, TN, 0)
                    for tix in range(ntile):
                        t0 = tix * TN
                        i = tix % 2
                        if tix + 1 < ntile:
                            load_h(b, soff, t0 + TN, TN, (tix + 1) % 2)
                        ta, r_ta = tab[i]
                        for c in range(4):
                            A, rA = f32a[0 + 3 * (c % 2)], r_f32a[0 + 3 * (c % 2)]
                            Bz, rB = f32a[1 + 3 * (c % 2)], r_f32a[1 + 3 * (c % 2)]
                            Cg, rC = f32a[2 + 3 * (c % 2)], r_f32a[2 + 3 * (c % 2)]
                            S.op("dve", lambda e: e.tensor_scalar(out=A[:, 0:TN], in0=u3[:, c, t0:t0 + TN],
                                                                   scalar1=vcol(l, V_WC, c * 3 + 0), scalar2=None, op0=ALU.mult),
                                 reads=[r_u, r_vecs], writes=[rA])
                            S.op("dve", lambda e: e.scalar_tensor_tensor(out=A[:, 0:TN], in0=u3[:, c, t0 + 1:t0 + 1 + TN],
                                                                          scalar=vcol(l, V_WC, c * 3 + 1), in1=A[:, 0:TN],
                                                                          op0=ALU.mult, op1=ALU.add),
                                 reads=[r_u, r_vecs, rA], writes=[rA])
                            S.op("dve", lambda e: e.scalar_tensor_tensor(out=A[:, 0:TN], in0=u3[:, c, t0 + 2:t0 + 2 + TN],
                                                                          scalar=vcol(l, V_WC, c * 3 + 2), in1=A[:, 0:TN],
                                                                          op0=ALU.mult, op1=ALU.add),
                                 reads=[r_u, r_vecs, rA], writes=[rA])
                            psz, r_psz = inproj(ht[i], r_ht[i], TN, wc3, r_wc, C_AZ // 128 + c)
                            S.op("act", lambda e: e.activation(out=Bz[:, 0:TN], in_=psz[:, 0:TN], func=AF.Silu,
                                                               bias=vcol(l, V_BIN, C_AZ // 128 + c), scale=1.0),
                                 reads=[r_psz, r_vecs], writes=[rB])
                            psb, r_psb = inproj(ht[i], r_ht[i], TN, wc3, r_wc, C_AB // 128 + c)
                            S.op("dve", lambda e: e.scalar_tensor_tensor(out=Cg[:, 0:TN], in0=psb[:, 0:TN],
                                                                         scalar=vcol(l, V_BIN, C_AB // 128 + c), in1=Bz[:, 0:TN],
                                                                         op0=ALU.add, op1=ALU.mult),
                                 reads=[r_psb, r_vecs, rB], writes=[rC])
                            S.op("dve", lambda e: e.scalar_tensor_tensor(out=ta[:, c, 0:TN], in0=A[:, 0:TN],
                                                                         scalar=vcol(l, V_BC, c), in1=Cg[:, 0:TN],
                                                                         op0=ALU.add, op1=ALU.mult),
                                 reads=[rA, rC, r_vecs], writes=[r_ta] if c == 0 else (), partial=[r_ta] if c else ())
                        S.dma(tT[b, 0].rearrange("(k p) n -> p k n", p=128)[:, :, soff + t0:soff + t0 + TN], ta[:, :, 0:TN],
                              f"d_ta{i}", reads=[r_ta], writes=[hres("t", b, 0, soff + t0)])

            for b in range(NB):
                mcols = {"ctx": 2, "lat": b}
                act_seqs = seqs
                S.barrier()
                ps_pool["gen"] = [0, 1, 2, 3, 4]
                car = Carver(arena, ARN)
                wkv, _ = car.take(8 * 1024, "wkv"); r_wkv = WSet(8, "wkv"); wkv3 = wkv.rearrange("p (k n) -> p k n", k=8)
                wqz3, r_wqz = wkv3, r_wkv
                KT = {}
                VV = {}
                for (sname, soff, L, TN, _) in act_seqs:
                    v, r = car.take(4 * L, "KT" + sname)
                    KT[sname] = (v.rearrange("p (c n) -> p c n", c=4), r)
                    v, r = car.take((L // 128) * NH * 65, "V" + sname)
                    VV[sname] = (v.rearrange("p (t h d) -> p t h d", t=L // 128, h=NH), r)
                key0 = geo["order"][0]
                nres = len(geo["pats"][key0])
                bres, r_bres = car.take(nres * NH * 128, "bres"); bres3 = bres.rearrange("p (t k) -> p t k", k=128)
                bdyn, r_bdyn = car.take(5 * NH * 128, "bdyn"); bdyn3 = bdyn.rearrange("p (t k) -> p t k", k=128)
                wcar = Carver(work, WORKN)
                qTb, r_qT = wcar.take(4 * 512, "qT"); qT3 = qTb.rearrange("p (c n) -> p c n", c=4)
                PTs = []
                for i in range(2):
                    v, r = wcar.take(7 * 128, f"PT{i}")
                    PTs.append((v, r))
                Ons = []
                for i in range(2):
                    v, r = wcar.take(512, f"On{i}")
                    Ons.append((v, r))
                tcb = []
                for i in range(2):
                    v, r = wcar.take(4 * 512, f"tc{i}")
                    tcb.append((v.rearrange("p (g n) -> p g n", g=4), r))
                f32a, r_f32a = carve_f32a(wcar, 5)
                scz = [f32a[0], f32a[1], f32a[2], f32a[3]]; r_scz = r_f32a[0:4]
                rec, r_rec = f32a[4], r_f32a[4]
                load_cast(w_in[l], 8, 1024, wkv3, r_wkv, col0=C_K)

                def load_bias(tab0, ntabs, dst3, r_dst):
                    tot = ntabs * NH
                    c = 0
                    while c < tot:
                        w = min(16, tot - c)
                        i = rr["wst"] % 4
                        rr["wst"] += 1
                        stg = wst[i][:, 0:w * 128].rearrange("p (t k) -> p t k", k=128)
                        S.dma(stg, bias_in[l, tab0 * NH + c:tab0 * NH + c + w].rearrange("t q k -> q t k"),
                              f"d_wst{i}", writes=[r_wst[i]])
                        S.op("act", lambda e: e.activation(out=dst3[:, c:c + w, :], in_=stg, func=AF.Exp),
                             reads=[r_wst[i]], partial=[r_dst])
                        c += w
                load_bias(geo["tab_base"][key0], nres, bres3, r_bres)
                for (sname, soff, L, TN, _) in act_seqs:
                    kt3, r_kt = KT[sname]
                    v4, r_v = VV[sname]
                    S.op("pool", lambda e: e.memset(v4[:, :, :, 64:65], 1.0), partial=[r_v])
                    ntile = L // TN
                    load_h(b, soff, 0, TN, 0)
                    for tix in range(ntile):
                        t0 = tix * TN
                        i = tix % 2
                        if tix + 1 < ntile:
                            load_h(b, soff, t0 + TN, TN, (tix + 1) % 2)
                        for c in range(4):
                            ps, r_ps = inproj(ht[i], r_ht[i], TN, wkv3, r_wkv, c)
                            S.op("act", lambda e: e.activation(out=kt3[:, c, t0:t0 + TN], in_=ps[:, 0:TN], func=AF.Identity,
                                                               bias=vcol(l, V_BIN, C_K // 128 + c), scale=1.0),
                                 reads=[r_ps, r_vecs], partial=[r_kt])
                        for st in range(TN // 128):
                            ps, r_ps = psum()
                            for kc in range(8):
                                S.op("pe", lambda e: e.matmul(ps[:, :], ht[i][:, kc, st * 128:(st + 1) * 128], wkv3[:, kc, 512:1024],
                                                              start=(kc == 0), stop=(kc == 7)),
                                     reads=r_wkv.chunks[4:8] + [r_ht[i]], writes=[r_ps] if kc == 0 else (), partial=[r_ps] if kc else (), inc=(kc == 7))
                            S.op("dve", lambda e: e.tensor_copy(out=v4[:, t0 // 128 + st, :, 0:64],
                                                                in_=ps[:, :].rearrange("p (h d) -> p h d", h=NH)),
                                 reads=[r_ps], partial=[r_v])
                load_cast(w_in[l], 8, 512, wqz3, r_wqz, col0=C_Q, dcol0=0)
                load_cast(w_in[l], 8, 512, wqz3, r_wqz, col0=C_CZ, dcol0=512)
                cur_dyn = [None]
                for (sname, soff, L, TN, _) in (act_seqs if not last else act_seqs[1:]):
                    ntile = L // TN
                    kc3, r_kc = KT["ctx"]
                    vc4, r_vc = VV["ctx"]
                    kl3, r_kl = KT[sname]
                    vl4, r_vl = VV[sname]
                    load_h(b, soff, 0, TN, 0)
                    sti0 = [0]
                    jobn = [0]
                    for tix in range(ntile):
                        t0 = tix * TN
                        i = tix % 2
                        if tix + 1 < ntile:
                            load_h(b, soff, t0 + TN, TN, (tix + 1) % 2)
                        for c in range(4):
                            ps, r_ps = inproj(ht[i], r_ht[i], TN, wqz3, r_wqz, c)
                            S.op("dve", lambda e: e.tensor_scalar(out=qT3[:, c, 0:TN], in0=ps[:, 0:TN],
                                                                  scalar1=vcol(l, V_BIN, C_Q // 128 + c), scalar2=HD ** -0.5,
                                                                  op0=ALU.add, op1=ALU.mult),
                                 reads=[r_ps, r_vecs], writes=[r_qT] if c == 0 else (), partial=[r_qT] if c else ())
                        for c in range(4):
                            ps, r_ps = inproj(ht[i], r_ht[i], TN, wqz3, r_wqz, 4 + c)
                            S.op("act", lambda e: e.activation(out=scz[c][:, 0:TN], in_=ps[:, 0:TN], func=AF.Silu,
                                                               bias=vcol(l, V_BIN, C_CZ // 128 + c), scale=1.0),
                                 reads=[r_ps, r_vecs], writes=[r_scz[c]])
                        tc, r_tc = tcb[i]
                        sub_state = {}

                        def sub_get(st, gsub):
                            if st in sub_state:
                                return sub_state[st]
                            chunks = []
                            if sname == "lat":
                                key, chl = geo["subs"][gsub]
                                if key == key0:
                                    bt3, r_bt = bres3, r_bres
                                else:
                                    if cur_dyn[0] != (key,):
                                        load_bias(geo["tab_base"][key], len(chl), bdyn3, r_bdyn)
                                        cur_dyn[0] = (key,)
                                    bt3, r_bt = bdyn3, r_bdyn
                                for ci, ch in enumerate(chl):
                                    chunks.append((kl3, r_kl, vl4, r_vl, ch, (bt3, r_bt, ci)))
                            for ch in range(CTX // 128):
                                chunks.append((kc3, r_kc, vc4, r_vc, ch, None))
                            on, r_on = Ons[(sti0[0] + st) % 2]
                            sub_state[st] = dict(chunks=chunks, on=on, r_on=r_on, psO=None)
                            return sub_state[st]

                        def emit_scores(job):
                            st, h = job["st"], job["h"]
                            ss = sub_get(st, job["gsub"])
                            chunks = ss["chunks"]
                            nch = len(chunks)
                            c = h // 2
                            pb = (h % 2) * 64
                            banks = [psum() for _ in range((nch + 3) // 4)]
                            job["banks"] = banks
                            job["pt"] = PTs[jobn[0] % 2]
                            jobn[0] += 1
                            for ci, (k3, r_k, v4_, r_v_, ch, bt) in enumerate(chunks):
                                psS, r_psS = banks[ci // 4]
                                o = psS[:, (ci % 4) * 128:(ci % 4 + 1) * 128]
                                firstb = (ci % 4 == 0)
                                lastb = (ci % 4 == 3) or (ci == nch - 1)
                                S.op("pe", lambda e: e.matmul(o, k3[pb:pb + 64, c, ch * 128:(ch + 1) * 128],
                                                              qT3[pb:pb + 64, c, st * 128:(st + 1) * 128],
                                                              start=True, stop=True),
                                     reads=[r_k, r_qT], writes=[r_psS] if firstb else (), partial=() if firstb else [r_psS],
                                     inc=lastb)

                        def emit_exp(job):
                            ss = sub_get(job["st"], job["gsub"])
                            nch = len(ss["chunks"])
                            pt, r_pt = job["pt"]
                            for bi, (psS, r_psS) in enumerate(job["banks"]):
                                n = min(4, nch - bi * 4) * 128
                                S.op("act", lambda e: e.activation(out=pt[:, bi * 512:bi * 512 + n], in_=psS[:, 0:n], func=AF.Exp),
                                     reads=[r_psS], writes=[r_pt] if bi == 0 else (), partial=[r_pt] if bi else ())
                            loc = [c_ for c_ in ss["chunks"] if c_[5] is not None]
                            if loc:
                                nl = len(loc)
                                bt3, r_bt = loc[0][5][0], loc[0][5][1]
                                h_ = job["h"]
                                S.op("dve", lambda e: e.tensor_tensor(out=pt[:, 0:nl * 128].rearrange("p (t q) -> p t q", t=nl),
                                                                      in0=pt[:, 0:nl * 128].rearrange("p (t q) -> p t q", t=nl),
                                                                      in1=bt3[:, h_ * nl:(h_ + 1) * nl, :], op=ALU.mult),
                                     reads=[r_pt, r_bt], writes=[r_pt])

                        def emit_pv(job):
                            st, h = job["st"], job["h"]
                            ss = sub_get(st, job["gsub"])
                            chunks = ss["chunks"]
                            nch = len(chunks)
                            hl, hg = h % 4, h // 4
                            pt, r_pt = job["pt"]
                            on, r_on = ss["on"], ss["r_on"]
                            if hl == 0:
                                ss["psO"] = psum("o")
                            psO, r_psO = ss["psO"]
                            for ci, (k3, r_k, v4_, r_v_, ch, bt) in enumerate(chunks):
                                S.op("pe", lambda e: e.matmul(psO[:, hl * 65:hl * 65 + 65], pt[:, ci * 128:(ci + 1) * 128],
                                                              v4_[:, ch, h, :], start=(ci == 0), stop=(ci == nch - 1)),
                                     reads=[r_pt, r_v_], writes=[r_psO] if (hl == 0 and ci == 0) else (),
                                     partial=() if (hl == 0 and ci == 0) else [r_psO], inc=(ci == nch - 1))
                            if hl == 3:
                                pso4 = psO[:, 0:260].rearrange("p (h d) -> p h d", h=4)
                                S.op("dve", lambda e: e.reciprocal(out=rec[:, hg * 4:hg * 4 + 4], in_=pso4[:, :, 64]),
                                     reads=[r_psO], writes=[r_rec] if hg == 0 else (), partial=[r_rec] if hg else ())
                                for hl2 in range(4):
                                    h2 = hg * 4 + hl2
                                    S.op("dve", lambda e: e.tensor_scalar(out=on[:, h2 * 64:(h2 + 1) * 64], in0=psO[:, hl2 * 65:hl2 * 65 + 64],
                                                                          scalar1=rec[:, h2:h2 + 1], scalar2=None, op0=ALU.mult),
                                         reads=[r_psO, r_rec], writes=[r_on] if h2 == 0 else (), partial=[r_on] if h2 else ())
                            if h == NH - 1:
                                for c in range(4):
                                    S.op("pe", lambda e: e.transpose(pst[:, c * 128:(c + 1) * 128], on[:, c * 128:(c + 1) * 128], ident),
                                         reads=[r_on, r_consts], writes=[r_pst] if c == 0 else (), partial=[r_pst] if c else (), inc=(c == 3))
                                for c in range(4):
                                    S.op("dve", lambda e: e.scalar_tensor_tensor(out=tc[:, c, st * 128:(st + 1) * 128],
                                                                                 in0=pst[:, c * 128:(c + 1) * 128],
                                                                                 scalar=vcol(l, V_BIN, C_V // 128 + c),
                                                                                 in1=scz[c][:, st * 128:(st + 1) * 128],
                                                                                 op0=ALU.add, op1=ALU.mult),
                                         reads=[r_pst, r_vecs, r_scz[c]],
                                         writes=[r_tc] if (st == 0 and c == 0) else (), partial=() if (st == 0 and c == 0) else [r_tc])

                        jobs = [dict(st=st, h=h, gsub=t0 // 128 + st) for st in range(TN // 128) for h in range(NH)]
                        emit_scores(jobs[0])
                        for ji, job in enumerate(jobs):
                            if ji + 1 < len(jobs):
                                emit_scores(jobs[ji + 1])
                            emit_exp(job)
                            emit_pv(job)
                        sti0[0] += TN // 128
                        S.dma(tT[b, 2].rearrange("(k p) n -> p k n", p=128)[:, :, soff + t0:soff + t0 + TN], tc[:, :, 0:TN],
                              f"d_tc{i}", reads=[r_tc], writes=[hres("t", b, 2, soff + t0)])

            for b in range(NB):
                mcols = {"ctx": 2, "lat": b}
                act_seqs = seqs
                if b == 0:
                    S.barrier()
                    ps_pool["gen"] = list(range(7))
                    car = Carver(arena, ARN)
                    wg, _ = car.take(8 * 3072, "wg"); r_wg = WSet(24, "wg"); wg3 = wg.rearrange("p (k n) -> p k n", k=8)
                    wbr = []
                    for j in range(3):
                        v, r = car.take(4 * 1024, f"wbr{j}")
                        wbr.append((v.rearrange("p (k n) -> p k n", k=4), WSet(8, f"wbr{j}")))
                    wo, _ = car.take(8 * 1024, "wo"); r_wo = WSet(8, "wo"); wo3 = wo.rearrange("p (k n) -> p k n", k=8)
                    tin = []
                    for i in range(2):
                        row = []
                        for j in range(3):
                            v, r = car.take(4 * 512, f"tin{i}{j}")
                            row.append((v.rearrange("p (k n) -> p k n", k=4), r))
                        tin.append(row)
                    wcar = Carver(work, WORKN)
                    Gb, r_G = wcar.take(8 * 512, "G"); G3 = Gb.rearrange("p (k n) -> p k n", k=8)
                    f32a, r_f32a = carve_f32a(wcar, 3)
                    xq, r_xq = take_f32(wcar, 8 * 512, "xq")
                    xq3 = xq.rearrange("p (k n) -> p k n", k=8)
                    load_cast(w_in[l], 8, 3072, wg3, r_wg, col0=C_GA)
                    for j in range(3):
                        load_cast(w_br[l, j], 4, 1024, wbr[j][0], wbr[j][1])
                    load_cast(w_out[l], 8, 1024, wo3, r_wo)
                xi = 0
                for (sname, soff, L, TN, _) in (act_seqs if not last else act_seqs[1:]):
                    mc = mcols[sname]
                    ntile = L // TN

                    def load_tile(tix):
                        t0_ = tix * TN
                        i_ = tix % 2
                        load_h(b, soff, t0_, TN, i_)
                        for j in range(3):
                            S.dma(tin[i_][j][0][:, :, 0:TN],
                                  tT[b, j].rearrange("(k p) n -> p k n", p=128)[:, :, soff + t0_:soff + t0_ + TN],
                                  f"d_tin{i_}{j}", reads=[hres("t", b, j, soff + t0_)], writes=[tin[i_][j][1]])
                    load_tile(0)
                    for tix in range(ntile):
                        t0 = tix * TN
                        i = tix % 2
                        if tix + 1 < ntile:
                            load_tile(tix + 1)
                        S.dma(xq3[:, :, 0:TN], xsrc[b].rearrange("(k p) n -> p k n", p=128)[:, :, soff + t0:soff + t0 + TN],
                              "d_xq", reads=[hres("x", b, soff + t0)], writes=[r_xq])
                        for m in range(8):
                            for j in range(3):
                                tj, r_tj = tin[i][j]
                                wj, r_wj = wbr[j]
                                psy, r_psy = psum()
                                for kc in range(4):
                                    S.op("pe", lambda e: e.matmul(psy[:, 0:TN], wj[:, kc, m * 128:(m + 1) * 128], tj[:, kc, 0:TN],
                                                                  start=(kc == 0), stop=(kc == 3)),
                                         reads=[r_wj.chunks[m], r_tj], writes=[r_psy] if kc == 0 else (), partial=[r_psy] if kc else (), inc=(kc == 3))
                                psg, r_psg = inproj(ht[i], r_ht[i], TN, wg3, r_wg, j * 8 + m)
                                sg, r_sg = f32a[j], r_f32a[j]
                                S.op("act", lambda e: e.activation(out=sg[:, 0:TN], in_=psg[:, 0:TN], func=AF.Sigmoid,
                                                                   bias=vcol(l, V_BIN, C_GA // 128 + j * 8 + m), scale=1.0),
                                     reads=[r_psg, r_vecs], writes=[r_sg])
                                S.op("dve", lambda e: e.tensor_tensor(out=sg[:, 0:TN], in0=psy[:, 0:TN], in1=sg[:, 0:TN], op=ALU.mult),
                                     reads=[r_psy, r_sg], writes=[r_sg])
                            S.op("pool", lambda e: e.tensor_tensor(out=f32a[0][:, 0:TN], in0=f32a[0][:, 0:TN], in1=f32a[1][:, 0:TN], op=ALU.add),
                                 reads=[r_f32a[0], r_f32a[1]], writes=[r_f32a[0]])
                            S.op("pool", lambda e: e.tensor_tensor(out=G3[:, m, 0:TN], in0=f32a[0][:, 0:TN], in1=f32a[2][:, 0:TN], op=ALU.add),
                                 reads=[r_f32a[0], r_f32a[2]], writes=[r_G] if m == 0 else (), partial=[r_G] if m else ())
                        for mo in range(8):
                            xc = xq3[:, mo, 0:TN]
                            pso, r_pso = psum()
                            for kc in range(8):
                                S.op("pe", lambda e: e.matmul(pso[:, 0:TN], wo3[:, kc, mo * 128:(mo + 1) * 128], G3[:, kc, 0:TN],
                                                              start=(kc == 0), stop=(kc == 7)),
                                     reads=[r_wo.chunks[mo], r_G], writes=[r_pso] if kc == 0 else (), partial=[r_pso] if kc else (), inc=(kc == 7))
                            S.op("dve", lambda e: e.scalar_tensor_tensor(out=xc, in0=pso[:, 0:TN], scalar=mod[:, 16 + mo, mc:mc + 1],
                                                                         in1=xc, op0=ALU.mult, op1=ALU.add),
                                 reads=[r_pso, r_mod, r_xq], partial=[r_xq])
                        S.dma(xs[b].rearrange("(k p) n -> p k n", p=128)[:, :, soff + t0:soff + t0 + TN], xq3[:, :, 0:TN],
                              "d_xq", reads=[r_xq], writes=[hres("xn", b, soff + t0)])
                for (sname, soff, L, TN, _) in (act_seqs if not last else act_seqs[1:]):
                    for t0 in range(0, L, TN):
                        hbm_res[("x", b, soff + t0)] = hres("xn", b, soff + t0)
                        del hbm_res[("xn", b, soff + t0)]

            for b in range(NB):
                if last:
                    S.barrier()
                    ps_pool["gen"] = list(range(7))
                    car = Carver(arena, ARN)
                    sq, r_sq = car.take(8 * 512, "sq")
                    sq3 = sq.rearrange("p (k n) -> p k n", k=8)
                    xt, r_xt = carve_xt(car)
                    (sname, soff, L, TN, _) = seqs[1]
                    fgc = NVL * depth
                    for tix in range(L // TN):
                        t0 = tix * TN
                        i = tix % 2
                        S.dma(xt[i][:, :, 0:TN], xs[b].rearrange("(k p) n -> p k n", p=128)[:, :, soff + t0:soff + t0 + TN],
                              f"d_xt{i}", reads=[hres("x", b, soff + t0)], writes=[r_xt[i]])
                        S.op("act", lambda e: e.activation(out=sq3[:, :, 0:TN], in_=xt[i][:, :, 0:TN], func=AF.Square),
                             reads=[r_xt[i]], writes=[r_sq])
                        ps, r_ps = psum()
                        for kc in range(8):
                            S.op("pe", lambda e: e.matmul(ps[:, 0:TN], ones, sq3[:, kc, 0:TN], start=(kc == 0), stop=(kc == 7)),
                                 reads=[r_sq, r_consts], writes=[r_ps] if kc == 0 else (), partial=[r_ps] if kc else (), inc=(kc == 7))
                        S.op("act", lambda e: e.activation(out=rstd[i][:, 0:TN], in_=ps[:, 0:TN], func=AF.Sqrt, bias=epsc[:, 0:1], scale=1.0 / D),
                             reads=[r_ps, r_consts], writes=[r_rstd[i]])
                        S.op("dve", lambda e: e.reciprocal(out=rstd[i][:, 0:TN], in_=rstd[i][:, 0:TN]),
                             reads=[r_rstd[i]], writes=[r_rstd[i]])
                        for kc in range(8):
                            S.op("dve",
                                 lambda e: e.scalar_tensor_tensor(out=xt[i][:, kc, 0:TN], in0=xt[i][:, kc, 0:TN],
                                                                  scalar=vecs[:, fgc + kc:fgc + kc + 1], in1=rstd[i][:, 0:TN],
                                                                  op0=ALU.mult, op1=ALU.mult),
                                 reads=[r_vecs, r_rstd[i], r_xt[i]], writes=[r_xt[i]])
                        S.dma(y_out[b].rearrange("(k p) n -> p k n", p=128)[:, :, t0:t0 + TN], xt[i][:, :, 0:TN],
                              f"d_xt{i}", reads=[r_xt[i]], writes=[hres("y", b, t0)])
        S.barrier(engines=("sp",))
        print("program built: ninst", S.ninst, "nwaits", S.nwaits, "nsems", len(S.sems))
    return nc


_CACHE = {}


def _consts(seq):
    key = ("c", seq)
    if key not in _CACHE:
        cst = np.zeros((128, 512), dtype=NPBF)
        cst[:, 0:128] = np.eye(128).astype(NPBF)
        cst[:, 128:256] = np.ones((128, 128)).astype(NPBF)
        cst[:, 256:512] = _chan_table().astype(NPBF)
        dftl = _dft_pos_table(seq, 512, 4)
        dftc = _dft_pos_table(CTX, 256, 2)
        _CACHE[key] = (cst, dftl, dftc)
    return _CACHE[key]


def kernel(x, c, ctx, c_ctx, norm_g, w_ada, b_ada, w_in, b_in, w_conv, b_conv,
           rpb, w_br_a, w_br_f, w_br_c, w_out, final_g):
    x = np.asarray(x, dtype=np.float32)
    depth = int(np.asarray(w_in).shape[0])
    seq = int(x.shape[1])
    B = int(x.shape[0])
    assert B == NB * NCORES
    geo = _attn_geometry(seq)
    cst, dftl, dftc = _consts(seq)
    f = lambda a: np.ascontiguousarray(np.asarray(a, dtype=np.float32))
    c, ctx, c_ctx = f(c), f(ctx), f(c_ctx)
    norm_g, b_ada, b_in, w_conv, b_conv, final_g = f(norm_g), f(b_ada), f(b_in), f(w_conv), f(b_conv), f(final_g)
    w_ada, w_in, w_out = f(w_ada), f(w_in), f(w_out)
    w_br = np.ascontiguousarray(np.stack([f(w_br_a), f(w_br_f), f(w_br_c)], axis=1))
    rpb = f(rpb)
    NV = NVL * depth + 8
    vecs = np.zeros((128, NV), dtype=np.float32)
    for l in range(depth):
        o = l * NVL
        vecs[:, o + V_NG:o + V_NG + 8] = norm_g[l].reshape(8, 128).T
        vecs[:, o + V_BADA:o + V_BADA + 24] = b_ada[l].reshape(24, 128).T
        vecs[:, o + V_BIN:o + V_BIN + 64] = b_in[l].reshape(64, 128).T
        vecs[:, o + V_WC:o + V_WC + 12] = w_conv[l].reshape(3, 4, 128).transpose(2, 1, 0).reshape(128, 12)
        vecs[:, o + V_BC:o + V_BC + 4] = b_conv[l].reshape(4, 128).T
    vecs[:, NVL * depth:NVL * depth + 8] = final_g.reshape(8, 128).T
    bias_tab = np.stack([_bias_tables(rpb[l], geo) for l in range(depth)], axis=0)
    xall = np.concatenate([ctx.transpose(0, 2, 1), x.transpose(0, 2, 1)], axis=2)
    key = ("nc", seq, depth)
    if key not in _CACHE:
        _CACHE[key] = build_program(seq, depth)
    nc = _CACHE[key]
    in_maps = []
    for core in range(NCORES):
        b0 = core * NB
        cT = np.stack([c[b0], c[b0 + 1], c_ctx], axis=1)
        cT = np.ascontiguousarray(cT.reshape(8, 128, 3).transpose(1, 0, 2))
        in_maps.append({
            "x_in": np.ascontiguousarray(xall[b0:b0 + NB]),
            "cT": cT, "vecs": vecs, "w_ada": w_ada, "w_in": w_in, "w_br": w_br, "w_out": w_out,
            "bias_tab": bias_tab, "consts": cst, "dftl": dftl, "dftc": dftc,
        })
    res = run_bass_kernel_spmd(nc, in_maps, core_ids=list(range(NCORES)))
    ys = [np.asarray(r["y"], dtype=np.float32) for r in res.results]
    y = np.concatenate(ys, axis=0)
    return np.ascontiguousarray(y.transpose(0, 2, 1))
```

```python
import contextlib
import numpy as np
import ml_dtypes
import concourse.bass as bass
import concourse.mybir as mybir
from concourse.bass_utils import run_bass_kernel_spmd

F32 = mybir.dt.float32
BF16 = mybir.dt.bfloat16
AF = mybir.ActivationFunctionType
ALU = mybir.AluOpType
NPBF = ml_dtypes.bfloat16

D = 1024
SEQ = 4096
DEPTH = 4
NCORES = 8
NB = 2
CTX = 256
GW = 64
NH = 8
HD = 64
WIN_H = 8
WIN_W = 16
DIN = 8192
EPS = 1e-6
NEG = -30000.0
C_AX, C_AB, C_AC, C_AZ, C_FU, C_FZ, C_Q, C_K, C_V, C_CZ, C_GA, C_GF, C_GC = (
    0, 512, 1024, 1536, 2048, 2560, 3072, 3584, 4096, 4608, 5120, 6144, 7168)
V_NG, V_BADA, V_BIN, V_WC, V_BC, NVL = 0, 8, 32, 96, 108, 112


class Res:
    __slots__ = ("name", "lw", "rd")

    def __init__(self, name):
        self.name = name
        self.lw = {}
        self.rd = {}


class Sched:
    def __init__(self, nc, es):
        self.nc = nc
        self.es = es
        self.eng = {"pe": nc.tensor, "act": nc.scalar, "dve": nc.vector, "pool": nc.gpsimd, "sp": nc.sync}
        self.sems = {}
        self.cnt = {}
        self.isdma = {}
        self.waited = {e: {} for e in self.eng}
        for e in ("pe", "act", "dve", "pool"):
            self._sem(e, False)
        self.ninst = 0
        self.nwaits = 0

    def _sem(self, key, isdma):
        if key not in self.sems:
            self.sems[key] = self.es.enter_context(self.nc.semaphore("s_" + str(key)))
            self.cnt[key] = 0
            self.isdma[key] = isdma
        return self.sems[key]

    def _wait(self, e, k, v):
        if self.isdma[k]:
            v = self.cnt[k]
        w = self.waited[e]
        if w.get(k, 0) >= v:
            return
        if e == "pe" and k == "pe":
            return
        self.eng[e].wait_ge(self.sems[k], v)
        w[k] = v
        self.nwaits += 1

    def deps(self, e, reads, writes, partial):
        for r in reads:
            for k, v in r.lw.items():
                self._wait(e, k, v)
        for w_ in writes:
            for k, v in w_.lw.items():
                self._wait(e, k, v)
            for k, v in w_.rd.items():
                self._wait(e, k, v)
        for w_ in partial:
            for k, v in w_.rd.items():
                self._wait(e, k, v)

    def mark(self, ev, reads, writes, partial):
        k, v = ev
        for r in reads:
            if r.rd.get(k, 0) < v:
                r.rd[k] = v
        for w_ in writes:
            w_.lw = {k: v}
            w_.rd = {}
        for w_ in partial:
            if w_.lw.get(k, 0) < v:
                w_.lw[k] = v

    def op(self, e, fn, reads=(), writes=(), partial=(), inc=True):
        self.deps(e, reads, writes, partial)
        ins = fn(self.eng[e])
        self.ninst += 1
        if inc:
            self.cnt[e] += 1
            ins.then_inc(self.sems[e], 1)
            ev = (e, self.cnt[e])
        else:
            ev = (e, self.cnt[e] + 1)
        self.mark(ev, reads, writes, partial)
        return ins

    def dma(self, out, in_, semkey, reads=(), writes=(), partial=(), e="sp"):
        self._sem(semkey, True)
        self.deps(e, reads, writes, partial)
        ins = self.eng[e].dma_start(out=out, in_=in_)
        self.ninst += 1
        self.cnt[semkey] += 16
        ins.then_inc(self.sems[semkey], 16)
        self.mark((semkey, self.cnt[semkey]), reads, writes, partial)
        return ins

    def barrier(self, engines=("pe", "act", "dve", "pool", "sp")):
        for e in engines:
            for k in list(self.sems.keys()):
                if self.cnt[k] > 0:
                    self._wait(e, k, self.cnt[k])


class Buf:
    def __init__(self, ap_fn, name):
        self.t = ap_fn
        self.r = Res(name)


def _dft_pos_table(L, TN, LG):
    m = np.arange(L, dtype=np.float64)
    cosv = np.cos(2 * np.pi * m / L) / np.sqrt(L)
    sinv = -np.sin(2 * np.pi * m / L) / np.sqrt(L)
    l = np.arange(L, dtype=np.int64)
    idx = (l[:, None] * l[None, :]) % L
    ntile = L // TN
    nlc = L // 128
    nlg = nlc // LG
    out = np.empty((ntile, nlg, 128, LG, 2, TN), dtype=NPBF)
    for cs, tab in ((0, cosv), (1, sinv)):
        full = tab[idx].astype(NPBF)
        full = full.reshape(nlg, LG, 128, ntile, TN)
        out[:, :, :, :, cs, :] = full.transpose(3, 0, 2, 1, 4)
    return out


def _chan_table():
    c = np.arange(128, dtype=np.int64)
    idx = (c[:, None] * c[None, :]) % 128
    ang = 2 * np.pi * idx / 128.0
    return np.concatenate([np.cos(ang), np.sin(ang)], axis=1) / np.sqrt(128.0)


def _attn_geometry(seq):
    R = seq // GW
    kh = min(WIN_H, R)

    def rs(r):
        return int(np.clip(r - kh // 2, 0, R - kh))
    subs = []
    pats = {}
    for j in range(R // 2):
        r0 = 2 * j
        a0, a1 = rs(r0) - r0, rs(r0 + 1) - r0
        lo, hi = rs(r0), rs(r0 + 1) + kh - 1
        chunks = list(range(lo // 2, hi // 2 + 1))
        key = (a0, a1)
        deltas = tuple(2 * c - r0 for c in chunks)
        if key not in pats:
            pats[key] = deltas
        assert pats[key] == deltas
        subs.append((key, chunks))
    cnts = {}
    for key, _ in subs:
        cnts[key] = cnts.get(key, 0) + 1
    order = sorted(pats.keys(), key=lambda k: -cnts[k])
    tab_base = {}
    n = 0
    for key in order:
        tab_base[key] = n
        n += len(pats[key])
    return dict(R=R, kh=kh, subs=subs, pats=pats, order=order, tab_base=tab_base, ntab=n)


def _bias_tables(rpb_l, geo):
    kh = geo["kh"]
    out = np.full((geo["ntab"], NH, 128, 128), NEG, dtype=np.float32)
    qc = np.arange(GW)
    kc = np.arange(GW)
    cs = np.clip(qc - WIN_W // 2, 0, GW - WIN_W)
    col_ok = (kc[None, :] >= cs[:, None]) & (kc[None, :] < cs[:, None] + WIN_W)
    dc = np.clip(kc[None, :] - qc[:, None] + WIN_W - 1, 0, 2 * WIN_W - 2)
    for key in geo["order"]:
        a0, a1 = key
        for ci, dl in enumerate(geo["pats"][key]):
            t = geo["tab_base"][key] + ci
            for qr in range(2):
                a = a0 if qr == 0 else a1
                for krl in range(2):
                    rel = dl + krl
                    if not (a <= rel < a + kh):
                        continue
                    dr = rel - qr + WIN_H - 1
                    blk = np.where(col_ok[None], rpb_l[:, dr][:, dc], np.float32(NEG))
                    out[t, :, qr * 64:(qr + 1) * 64, krl * 64:(krl + 1) * 64] = blk
    res = np.empty((geo["ntab"] * NH, 128, 128), dtype=np.float32)
    for key in geo["order"]:
        tb, n = geo["tab_base"][key], len(geo["pats"][key])
        sub = out[tb:tb + n]
        res[tb * NH:(tb + n) * NH] = sub.transpose(1, 0, 3, 2).reshape(n * NH, 128, 128)
    return res


def build_program(seq, depth):
    geo = _attn_geometry(seq)
    LT = CTX + seq
    nc = bass.Bass("TRN2", target_bir_lowering=False)
    NV = NVL * depth + 8
    x_in = nc.dram_tensor("x_in", [NB, D, LT], F32, kind="ExternalInput").ap()
    cT_in = nc.dram_tensor("cT", [128, 8, 3], F32, kind="ExternalInput").ap()
    vecs_in = nc.dram_tensor("vecs", [128, NV], F32, kind="ExternalInput").ap()
    w_ada = nc.dram_tensor("w_ada", [depth, D, 3 * D], F32, kind="ExternalInput").ap()
    w_in = nc.dram_tensor("w_in", [depth, D, DIN], F32, kind="ExternalInput").ap()
    w_br = nc.dram_tensor("w_br", [depth, 3, 512, D], F32, kind="ExternalInput").ap()
    w_out = nc.dram_tensor("w_out", [depth, D, D], F32, kind="ExternalInput").ap()
    bias_in = nc.dram_tensor("bias_tab", [depth, geo["ntab"] * NH, 128, 128], F32, kind="ExternalInput").ap()
    consts_in = nc.dram_tensor("consts", [128, 512], BF16, kind="ExternalInput").ap()
    LGL = 4
    dftl_in = nc.dram_tensor("dftl", [seq // 512, seq // 128 // LGL, 128, LGL, 2, 512], BF16, kind="ExternalInput").ap()
    dftc_in = nc.dram_tensor("dftc", [1, 1, 128, 2, 2, 256], BF16, kind="ExternalInput").ap()
    y_out = nc.dram_tensor("y", [NB, D, seq], F32, kind="ExternalOutput").ap()
    xs = nc.dram_tensor("xs", [NB, D, LT], F32, kind="Internal").ap()
    hT = nc.dram_tensor("hT", [NB, D, LT], BF16, kind="Internal").ap()
    tT = nc.dram_tensor("tT", [NB, 3, 512, LT], BF16, kind="Internal").ap()

    es = contextlib.ExitStack()
    with es:
        S = Sched(nc, es)

        def sb(name, shape, dt):
            return es.enter_context(nc.sbuf_tensor("sb_" + name, shape, dt))

        vecs = sb("vecs", [128, NV], F32); r_vecs = Res("vecs")
        consts = sb("consts", [128, 512], BF16); r_consts = Res("consts")
        ident = consts[:, 0:128]
        ones = consts[:, 128:256]
        csc = consts[:, 256:512]
        epsc = sb("epsc", [128, 1], F32)
        cts = sb("cts", [128, 8, 3], F32); r_cts = Res("cts")
        mod = sb("mod", [128, 24, 3], F32); r_mod = Res("mod")
        gm = sb("gm", [128, 8, 3], F32); r_gm = Res("gm")
        ht = [sb(f"ht{i}", [128, 8, 512], BF16) for i in range(2)]; r_ht = [Res(f"ht{i}") for i in range(2)]
        wst = [sb(f"wst{i}", [128, 2048], F32) for i in range(4)]; r_wst = [Res(f"wst{i}") for i in range(4)]
        rstd = [sb(f"rstd{i}", [128, 512], F32) for i in range(2)]; r_rstd = [Res(f"rstd{i}") for i in range(2)]
        WORKN = 16384
        work = sb("work", [128, WORKN], BF16)
        ARN = 57344
        arena = sb("arena", [128, ARN], BF16)
        psf = [es.enter_context(nc.psum_tensor(f"psf{i}", [128, 512], F32)) for i in range(7)]
        r_psf = [Res(f"psf{i}") for i in range(7)]
        pst = es.enter_context(nc.psum_tensor("pst", [128, 1024], BF16)); r_pst = Res("pst")
        ps_pool = {"gen": list(range(7)), "acc": [3, 4, 5, 6], "o": [5, 6]}
        ps_rr = {"gen": 0, "acc": 0, "o": 0}

        def psum(pool="gen"):
            lst = ps_pool[pool]
            i = lst[ps_rr[pool] % len(lst)]
            ps_rr[pool] += 1
            return psf[i], r_psf[i]

        hbm_res = {}

        def hres(*key):
            if key not in hbm_res:
                hbm_res[key] = Res(str(key))
            return hbm_res[key]

        class Carver:
            def __init__(self, base, total):
                self.base = base
                self.total = total
                self.off = 0

            def take(self, n, name):
                assert self.off + n <= self.total, (name, self.off, n, self.total)
                v = self.base[:, self.off:self.off + n]
                self.off += n
                return v, Res(name)

        def take_f32(carver, n, name):
            v, r = carver.take(2 * n, name)
            return v.bitcast(F32), r

        def carve_xt(carver):
            xs_, rs_ = [], []
            for i in range(2):
                v, r = take_f32(carver, 8 * 512, f"xt{i}")
                xs_.append(v.rearrange("p (k n) -> p k n", k=8))
                rs_.append(r)
            return xs_, rs_

        def carve_f32a(carver, n):
            fs_, rs_ = [], []
            for i in range(n):
                v, r = take_f32(carver, 512, f"f32a{i}")
                fs_.append(v)
                rs_.append(r)
            return fs_, rs_

        rr = {"evac": 0, "wst": 0, "cast": 0}
        di_glob = [0]

        class WSet:
            def __init__(self, n, name):
                self.chunks = [Res(f"{name}{i}") for i in range(n)]

        def cast(out, in_, reads, partial):
            if rr["cast"] % 2 == 0:
                S.op("act", lambda e: e.activation(out=out, in_=in_, func=AF.Copy), reads=reads, partial=partial)
            else:
                S.op("dve", lambda e: e.tensor_copy(out=out, in_=in_), reads=reads, partial=partial)
            rr["cast"] += 1

        def load_cast(src2d, kcn, ncols, dst3, wset, col0=0, dcol0=0):
            src = src2d.rearrange("(kc p) n -> p kc n", p=128)
            piece = max(1, 2048 // kcn)
            piece = min(piece, ncols)
            c = 0
            while c < ncols:
                w = min(piece, ncols - c)
                i = rr["wst"] % 4
                rr["wst"] += 1
                stg = wst[i][:, 0:kcn * w].rearrange("p (k n) -> p k n", k=kcn)
                S.dma(stg, src[:, :, col0 + c:col0 + c + w], f"d_wst{i}", writes=[r_wst[i]])
                cks = wset.chunks[(dcol0 + c) // 128:(dcol0 + c + w + 127) // 128]
                cast(dst3[:, :, dcol0 + c:dcol0 + c + w], stg, [r_wst[i]], cks)
                c += w

        def inproj(htb, r_htb, N, wv, r_w, mchunk):
            ps, r_ps = psum()
            for kc in range(8):
                S.op("pe", lambda e: e.matmul(ps[:, 0:N], wv[:, kc, mchunk * 128:(mchunk + 1) * 128], htb[:, kc, 0:N],
                                              start=(kc == 0), stop=(kc == 7)),
                     reads=[r_w.chunks[mchunk], r_htb], writes=[r_ps] if kc == 0 else (), partial=[r_ps] if kc else (), inc=(kc == 7))
            return ps, r_ps

        def vcol(l, off, j=0):
            c = l * NVL + off + j
            return vecs[:, c:c + 1]

        def load_h(b, seqoff, t0, N, slot):
            S.dma(ht[slot][:, :, 0:N],
                  hT[b].rearrange("(k p) n -> p k n", p=128)[:, :, seqoff + t0:seqoff + t0 + N],
                  f"d_ht{slot}", reads=[hres("h", b, seqoff + t0)], writes=[r_ht[slot]])

        seqs = [("ctx", 0, CTX, 256, 2), ("lat", CTX, seq, 512, 3)]

        S.dma(vecs[:, :], vecs_in[:, :], "d_misc", writes=[r_vecs])
        S.dma(consts[:, :], consts_in[:, :], "d_misc", writes=[r_consts])
        S.dma(cts[:, :, :], cT_in[:, :, :], "d_misc", writes=[r_cts])
        S.op("pool", lambda e: e.memset(epsc[:, :], EPS), partial=[r_consts])
        S.op("act", lambda e: e.activation(out=cts[:, :, :], in_=cts[:, :, :], func=AF.Silu), reads=[r_cts], writes=[r_cts])

        for l in range(depth):
            last = (l == depth - 1)
            xsrc = x_in if l == 0 else xs
            S.barrier()
            ps_pool["gen"] = list(range(7))
            car = Carver(arena, ARN)
            xt, r_xt = carve_xt(car)
            for pc in range(8):
                i = pc % 2
                stg = xt[i][:, :, 0:384]
                S.dma(stg, w_ada[l].rearrange("(kc p) n -> p kc n", p=128)[:, :, pc * 384:(pc + 1) * 384],
                      f"d_xt{i}", writes=[r_xt[i]])
                ps, r_ps = psum()
                for mm in range(3):
                    for kc in range(8):
                        S.op("pe", lambda e: e.matmul(ps[:, mm * 4:mm * 4 + 3], stg[:, kc, mm * 128:(mm + 1) * 128],
                                                      cts[:, kc, :], start=(kc == 0), stop=(kc == 7)),
                             reads=[r_xt[i], r_cts], writes=[r_ps] if (mm == 0 and kc == 0) else (),
                             partial=() if (mm == 0 and kc == 0) else [r_ps], inc=(mm == 2 and kc == 7))
                for mm in range(3):
                    m = pc * 3 + mm
                    S.op("dve", lambda e: e.tensor_scalar(out=mod[:, m, :], in0=ps[:, mm * 4:mm * 4 + 3],
                                                          scalar1=vcol(l, V_BADA, m), scalar2=None, op0=ALU.add),
                         reads=[r_ps, r_vecs], partial=[r_mod])
            for j in range(3):
                S.op("dve", lambda e: e.scalar_tensor_tensor(out=gm[:, :, j], in0=mod[:, 8:16, j], scalar=1.0,
                                                             in1=vecs[:, l * NVL + V_NG:l * NVL + V_NG + 8],
                                                             op0=ALU.add, op1=ALU.mult),
                     reads=[r_mod, r_vecs], partial=[r_gm])

            for b in range(NB):
                mcols = {"ctx": 2, "lat": b}
                act_seqs = seqs
                S.barrier()
                ps_pool["gen"] = list(range(7))
                car = Carver(arena, ARN)
                sq, r_sq = car.take(8 * 512, "sq")
                r_sqh = [r_sq, Res("sq_hi")]
                sq3 = sq.rearrange("p (k n) -> p k n", k=8)
                xt, r_xt = carve_xt(car)
                wcar = Carver(work, WORKN)
                f32a, r_f32a = carve_f32a(wcar, 2)
                p1tiles = [(sname, soff, t0, TN, mcols[sname]) for (sname, soff, L, TN, _) in act_seqs for t0 in range(0, L, TN)]

                def p1load(n):
                    (sname_, soff_, t0_, TN_, mc_) = p1tiles[n]
                    S.dma(xt[n % 2][:, :, 0:TN_],
                          xsrc[b].rearrange("(k p) n -> p k n", p=128)[:, :, soff_ + t0_:soff_ + t0_ + TN_],
                          f"d_xt{n % 2}", reads=[hres("x", b, soff_ + t0_)], writes=[r_xt[n % 2]])
                p1load(0)
                for ti, (sname, soff, t0, TN, mc) in enumerate(p1tiles):
                    if True:
                        i = ti % 2
                        if ti + 1 < len(p1tiles):
                            p1load(ti + 1)
                        S.op("act", lambda e: e.activation(out=sq3[:, 0:4, 0:TN], in_=xt[i][:, 0:4, 0:TN], func=AF.Square),
                             reads=[r_xt[i]], writes=[r_sqh[0]])
                        S.op("dve", lambda e: e.tensor_tensor(out=sq3[:, 4:8, 0:TN], in0=xt[i][:, 4:8, 0:TN], in1=xt[i][:, 4:8, 0:TN], op=ALU.mult),
                             reads=[r_xt[i]], writes=[r_sqh[1]])
                        ps, r_ps = psum()
                        for kc in range(8):
                            S.op("pe", lambda e: e.matmul(ps[:, 0:TN], ones, sq3[:, kc, 0:TN], start=(kc == 0), stop=(kc == 7)),
                                 reads=[r_sqh[kc // 4], r_consts], writes=[r_ps] if kc == 0 else (), partial=[r_ps] if kc else (),
                                 inc=(kc == 7))
                        S.op("act", lambda e: e.activation(out=rstd[i][:, 0:TN], in_=ps[:, 0:TN], func=AF.Sqrt, bias=epsc[:, 0:1], scale=1.0 / D),
                             reads=[r_ps, r_consts], writes=[r_rstd[i]])
                        S.op("dve", lambda e: e.reciprocal(out=rstd[i][:, 0:TN], in_=rstd[i][:, 0:TN]),
                             reads=[r_rstd[i]], writes=[r_rstd[i]])
                        for kc in range(8):
                            fi = kc % 2
                            S.op("dve", lambda e: e.scalar_tensor_tensor(out=f32a[fi][:, 0:TN], in0=xt[i][:, kc, 0:TN],
                                                                         scalar=gm[:, kc, mc:mc + 1], in1=rstd[i][:, 0:TN],
                                                                         op0=ALU.mult, op1=ALU.mult),
                                 reads=[r_xt[i], r_gm, r_rstd[i]], writes=[r_f32a[fi]])
                            S.op("act", lambda e: e.activation(out=ht[i][:, kc, 0:TN], in_=f32a[fi][:, 0:TN], func=AF.Identity,
                                                               bias=mod[:, kc, mc:mc + 1], scale=1.0),
                                 reads=[r_f32a[fi], r_mod], writes=[r_ht[i]] if kc == 0 else (), partial=[r_ht[i]] if kc else ())
                        S.dma(hT[b].rearrange("(k p) n -> p k n", p=128)[:, :, soff + t0:soff + t0 + TN], ht[i][:, :, 0:TN],
                              f"d_ht{i}", reads=[r_ht[i]], writes=[hres("h", b, soff + t0)])

            for b in range(NB):
                mcols = {"ctx": 2, "lat": b}
                act_seqs = seqs
                for si_, (sname, soff, L, TN, _) in enumerate(act_seqs if not last else act_seqs[1:]):
                    if b == 0 and si_ == 0:
                        S.barrier()
                        ps_pool["gen"] = [0, 1, 2]
                        car = Carver(arena, ARN)
                        wfu, _ = car.take(8 * 512, "wfu"); r_wfu = WSet(4, "wfu"); wfu3 = wfu.rearrange("p (k n) -> p k n", k=8)
                        wfz, _ = car.take(8 * 512, "wfz"); r_wfz = WSet(4, "wfz"); wfz3 = wfz.rearrange("p (k n) -> p k n", k=8)
                        ABfull, r_AB = car.take((seq // 128) * 1024, "AB")
                        dslfull = [car.take(LGL * 2 * 512, f"dft{i}") for i in range(4)]
                        wcar = Carver(work, WORKN)
                        UT = []
                        for i in range(2):
                            v, r = wcar.take(4 * 512, f"UT{i}")
                            UT.append((v.rearrange("p (g n) -> p g n", g=4), r))
                        szb, r_sz = wcar.take(4 * 512, "sz"); sz3 = szb.rearrange("p (g n) -> p g n", g=4)
                        tfb = []
                        for i in range(2):
                            v, r = wcar.take(4 * 512, f"tf{i}")
                            tfb.append((v.rearrange("p (g n) -> p g n", g=4), r))
                        load_cast(w_in[l], 8, 512, wfu3, r_wfu, col0=C_FU)
                        load_cast(w_in[l], 8, 512, wfz3, r_wfz, col0=C_FZ)
                    LC = L // 128
                    AB = ABfull[:, 0:LC * 1024]
                    AB5 = AB.rearrange("p (lc g cs c) -> p lc g cs c", lc=LC, g=4, cs=2)
                    AB3 = AB.rearrange("p (lc x) -> p lc x", lc=LC)
                    LG = LGL if sname == "lat" else 2
                    dsl = [(v_[:, 0:LG * 2 * TN].rearrange("p (a cs n) -> p a cs n", a=LG, cs=2), r_) for (v_, r_) in dslfull]
                    ntile = L // TN
                    load_h(b, soff, 0, TN, 0)
                    for tix in range(ntile):
                        t0 = tix * TN
                        i = tix % 2
                        if tix + 1 < ntile:
                            load_h(b, soff, t0 + TN, TN, (tix + 1) % 2)
                        ut, r_ut = UT[i]
                        for g in range(4):
                            ps, r_ps = inproj(ht[i], r_ht[i], TN, wfu3, r_wfu, g)
                            S.op("act", lambda e: e.activation(out=ut[:, g, 0:TN], in_=ps[:, 0:TN], func=AF.Identity,
                                                               bias=vcol(l, V_BIN, C_FU // 128 + g), scale=1.0),
                                 reads=[r_ps, r_vecs], writes=[r_ut] if g == 0 else (), partial=[r_ut] if g else ())
                        for st in range(TN // 128):
                            lc = (t0 // 128) + st
                            for half in range(2):
                                ps, r_ps = psum()
                                for gg in range(2):
                                    g = half * 2 + gg
                                    S.op("pe", lambda e: e.matmul(ps[:, gg * 256:(gg + 1) * 256], ut[:, g, st * 128:(st + 1) * 128], csc,
                                                                  start=True, stop=True),
                                         reads=[r_ut, r_consts], writes=[r_ps] if gg == 0 else (), partial=[r_ps] if gg else (),
                                         inc=(gg == 1))
                                dst = AB3[:, lc, half * 512:(half + 1) * 512]
                                if rr["evac"] % 2 == 0:
                                    S.op("act", lambda e: e.activation(out=dst, in_=ps[:, :], func=AF.Copy), reads=[r_ps], partial=[r_AB])
                                else:
                                    S.op("dve", lambda e: e.tensor_copy(out=dst, in_=ps[:, :]), reads=[r_ps], partial=[r_AB])
                                rr["evac"] += 1
                    dsrc = dftl_in if sname == "lat" else dftc_in
                    nlg = LC // LG
                    pieces = [(tix_, lg_) for tix_ in range(ntile) for lg_ in range(nlg)]
                    dbase = di_glob[0]

                    def dft_load(n_):
                        tix_, lg_ = pieces[n_]
                        sl_ = (dbase + n_) % 4
                        S.dma(dsl[sl_][0], dsrc[tix_, lg_], f"d_dft{sl_}", writes=[dsl[sl_][1]])
                    for n_ in range(min(3, len(pieces))):
                        dft_load(n_)
                    load_h(b, soff, 0, TN, 0)
                    for tix in range(ntile):
                        t0 = tix * TN
                        i = tix % 2
                        if tix + 1 < ntile:
                            load_h(b, soff, t0 + TN, TN, (tix + 1) % 2)
                        accs = [psum("acc") for _ in range(4)]
                        for lg in range(nlg):
                            n_ = tix * nlg + lg
                            if n_ + 3 < len(pieces):
                                dft_load(n_ + 3)
                            dv, r_dv = dsl[(dbase + n_) % 4]
                            for a in range(LG):
                                lc = lg * LG + a
                                for cs in range(2):
                                    first = (lc == 0 and cs == 0)
                                    lastm = (lc == LC - 1 and cs == 1)
                                    for g in range(4):
                                        ps, r_ps = accs[g]
                                        S.op("pe", lambda e: e.matmul(ps[:, 0:TN], AB5[:, lc, g, cs, :], dv[:, a, cs, 0:TN],
                                                                      start=first, stop=lastm),
                                             reads=[r_AB, r_dv], writes=[r_ps] if first else (), partial=() if first else [r_ps],
                                             inc=(lastm or (a == LG - 1 and cs == 1 and g == 3)))
                        tf, r_tf = tfb[i]
                        for g in range(4):
                            ps, r_ps = inproj(ht[i], r_ht[i], TN, wfz3, r_wfz, g)
                            S.op("act", lambda e: e.activation(out=sz3[:, g, 0:TN], in_=ps[:, 0:TN], func=AF.Silu,
                                                               bias=vcol(l, V_BIN, C_FZ // 128 + g), scale=1.0),
                                 reads=[r_ps, r_vecs], writes=[r_sz] if g == 0 else (), partial=[r_sz] if g else ())
                            psY, r_psY = accs[g]
                            S.op("dve", lambda e: e.tensor_tensor(out=tf[:, g, 0:TN], in0=psY[:, 0:TN], in1=sz3[:, g, 0:TN], op=ALU.mult),
                                 reads=[r_psY, r_sz], writes=[r_tf] if g == 0 else (), partial=[r_tf] if g else ())
                        S.dma(tT[b, 1].rearrange("(k p) n -> p k n", p=128)[:, :, soff + t0:soff + t0 + TN], tf[:, :, 0:TN],
                              f"d_tf{i}", reads=[r_tf], writes=[hres("t", b, 1, soff + t0)])
                    di_glob[0] += len(pieces)

            for b in range(NB):
                mcols = {"ctx": 2, "lat": b}
                act_seqs = seqs
                for si_, (sname, soff, L, TN, _) in enumerate(act_seqs if not last else act_seqs[1:]):
                    if b == 0 and si_ == 0:
                        S.barrier()
                        ps_pool["gen"] = list(range(7))
                        car = Carver(arena, ARN)
                        wc, _ = car.take(8 * 2048, "wconv"); r_wc = WSet(16, "wconv"); wc3 = wc.rearrange("p (k n) -> p k n", k=8)
                        ubfull, r_u = car.take(4 * (seq + 2), "u")
                        wcar = Carver(work, WORKN)
                        tab = []
                        for i in range(2):
                            v, r = wcar.take(4 * 512, f"ta{i}")
                            tab.append((v.rearrange("p (g n) -> p g n", g=4), r))
                        f32a, r_f32a = carve_f32a(wcar, 6)
                        load_cast(w_in[l], 8, 2048, wc3, r_wc, col0=0)
                    u3 = ubfull[:, 0:4 * (L + 2)].rearrange("p (c n) -> p c n", c=4)
                    S.op("pool", lambda e: e.memset(u3[:, :, 0:1], 0.0), partial=[r_u])
                    S.op("pool", lambda e: e.memset(u3[:, :, L + 1:L + 2], 0.0), partial=[r_u])
                    ntile = L // TN
                    load_h(b, soff, 0, TN, 0)
                    for tix in range(ntile):
                        t0 = tix * TN
                        i = tix % 2
                        if tix + 1 < ntile:
                            load_h(b, soff, t0 + TN, TN, (tix + 1) % 2)
                        for c in range(4):
                            fi = c % 2
                            ps, r_ps = inproj(ht[i], r_ht[i], TN, wc3, r_wc, C_AX // 128 + c)
                            S.op("act", lambda e: e.activation(out=f32a[fi][:, 0:TN], in_=ps[:, 0:TN], func=AF.Identity,
                                                               bias=vcol(l, V_BIN, C_AX // 128 + c), scale=1.0),
                                 reads=[r_ps, r_vecs], writes=[r_f32a[fi]])
                            ps2, r_ps2 = inproj(ht[i], r_ht[i], TN, wc3, r_wc, C_AC // 128 + c)
                            S.op("dve", lambda e: e.scalar_tensor_tensor(out=u3[:, c, 1 + t0:1 + t0 + TN], in0=ps2[:, 0:TN],
                                                                         scalar=vcol(l, V_BIN, C_AC // 128 + c), in1=f32a[fi][:, 0:TN],
                                                                         op0=ALU.add, op1=ALU.mult),
                                 reads=[r_ps2, r_vecs, r_f32a[fi]], partial=[r_u])
                    load_h(b, soff, 0, TN, 0)
                    for tix in range(ntile):
                        t0 = tix * TN
                        i = tix % 2
                        if tix + 1 < ntile:
                            load_h(b, soff, t0 + TN, TN, (tix + 1) % 2)
                        ta, r_ta = tab[i]
                        for c in range(4):
                            A, rA = f32a[0 + 3 * (c % 2)], r_f32a[0 + 3 * (c % 2)]
                            Bz, rB = f32a[1 + 3 * (c % 2)], r_f32a[1 + 3 * (c % 2)]
                            Cg, rC = f32a[2 + 3 * (c % 2)], r_f32a[2 + 3 * (c % 2)]
                            S.op("dve", lambda e: e.tensor_scalar(out=A[:, 0:TN], in0=u3[:, c, t0:t0 + TN],
                                                                   scalar1=vcol(l, V_WC, c * 3 + 0), scalar2=None, op0=ALU.mult),
                                 reads=[r_u, r_vecs], writes=[rA])
                            S.op("dve", lambda e: e.scalar_tensor_tensor(out=A[:, 0:TN], in0=u3[:, c, t0 + 1:t0 + 1 + TN],
                                                                          scalar=vcol(l, V_WC, c * 3 + 1), in1=A[:, 0:TN],
                                                                          op0=ALU.mult, op1=ALU.add),
                                 reads=[r_u, r_vecs, rA], writes=[rA])
                            S.op("dve", lambda e: e.scalar_tensor_tensor(out=A[:, 0:TN], in0=u3[:, c, t0 + 2:t0 + 2 + TN],
                                                                          scalar=vcol(l, V_WC, c * 3 + 2), in1=A[:, 0:TN],
                                                                          op0=ALU.mult, op1=ALU.add),
                                 reads=[r_u, r_vecs, rA], writes=[rA])
                            psz, r_psz = inproj(ht[i], r_ht[i], TN, wc3, r_wc, C_AZ // 128 + c)
                            S.op("act", lambda e: e.activation(out=Bz[:, 0:TN], in_=psz[:, 0:TN], func=AF.Silu,
                                                               bias=vcol(l, V_BIN, C_AZ // 128 + c), scale=1.0),
                                 reads=[r_psz, r_vecs], writes=[rB])
                            psb, r_psb = inproj(ht[i], r_ht[i], TN, wc3, r_wc, C_AB // 128 + c)
                            S.op("dve", lambda e: e.scalar_tensor_tensor(out=Cg[:, 0:TN], in0=psb[:, 0:TN],
                                                                         scalar=vcol(l, V_BIN, C_AB // 128 + c), in1=Bz[:, 0:TN],
                                                                         op0=ALU.add, op1=ALU.mult),
                                 reads=[r_psb, r_vecs, rB], writes=[rC])
                            S.op("dve", lambda e: e.scalar_tensor_tensor(out=ta[:, c, 0:TN], in0=A[:, 0:TN],
                                                                         scalar=vcol(l, V_BC, c), in1=Cg[:, 0:TN],
                                                                         op0=ALU.add, op1=ALU.mult),
                                 reads=[rA, rC, r_vecs], writes=[r_ta] if c == 0 else (), partial=[r_ta] if c else ())
                        S.dma(tT[b, 0].rearrange("(k p) n -> p k n", p=128)[:, :, soff + t0:soff + t0 + TN], ta[:, :, 0:TN],
                              f"d_ta{i}", reads=[r_ta], writes=[hres("t", b, 0, soff + t0)])

            for b in range(NB):
                mcols = {"ctx": 2, "lat": b}
                act_seqs = seqs
                S.barrier()
                ps_pool["gen"] = [0, 1, 2, 3, 4]
                car = Carver(arena, ARN)
                wkv, _ = car.take(8 * 1024, "wkv"); r_wkv = WSet(8, "wkv"); wkv3 = wkv.rearrange("p (k n) -> p k n", k=8)
                wqz3, r_wqz = wkv3, r_wkv
                KT = {}
                VV = {}
                for (sname, soff, L, TN, _) in act_seqs:
                    v, r = car.take(4 * L, "KT" + sname)
                    KT[sname] = (v.rearrange("p (c n) -> p c n", c=4), r)
                    v, r = car.take((L // 128) * NH * 65, "V" + sname)
                    VV[sname] = (v.rearrange("p (t h d) -> p t h d", t=L // 128, h=NH), r)
                key0 = geo["order"][0]
                nres = len(geo["pats"][key0])
                bres, r_bres = car.take(nres * NH * 128, "bres"); bres3 = bres.rearrange("p (t k) -> p t k", k=128)
                bdyn, r_bdyn = car.take(5 * NH * 128, "bdyn"); bdyn3 = bdyn.rearrange("p (t k) -> p t k", k=128)
                wcar = Carver(work, WORKN)
                qTb, r_qT = wcar.take(4 * 512, "qT"); qT3 = qTb.rearrange("p (c n) -> p c n", c=4)
                PTs = []
                for i in range(3):
                    v, r = wcar.take(7 * 128, f"PT{i}")
                    PTs.append((v, r))
                Ons = []
                for i in range(2):
                    v, r = wcar.take(512, f"On{i}")
                    Ons.append((v, r))
                tcb = []
                for i in range(2):
                    v, r = wcar.take(4 * 512, f"tc{i}")
                    tcb.append((v.rearrange("p (g n) -> p g n", g=4), r))
                f32a, r_f32a = carve_f32a(wcar, 5)
                scz = [f32a[0], f32a[1], f32a[2], f32a[3]]; r_scz = r_f32a[0:4]
                rec, r_rec = f32a[4], r_f32a[4]
                load_cast(w_in[l], 8, 1024, wkv3, r_wkv, col0=C_K)

                def load_bias(tab0, ntabs, dst3, r_dst):
                    tot = ntabs * NH
                    c = 0
                    while c < tot:
                        w = min(16, tot - c)
                        i = rr["wst"] % 4
                        rr["wst"] += 1
                        stg = wst[i][:, 0:w * 128].rearrange("p (t k) -> p t k", k=128)
                        S.dma(stg, bias_in[l, tab0 * NH + c:tab0 * NH + c + w].rearrange("t q k -> q t k"),
                              f"d_wst{i}", writes=[r_wst[i]])
                        S.op("act", lambda e: e.activation(out=dst3[:, c:c + w, :], in_=stg, func=AF.Exp),
                             reads=[r_wst[i]], partial=[r_dst])
                        c += w
                load_bias(geo["tab_base"][key0], nres, bres3, r_bres)
                for (sname, soff, L, TN, _) in act_seqs:
                    kt3, r_kt = KT[sname]
                    v4, r_v = VV[sname]
                    S.op("pool", lambda e: e.memset(v4[:, :, :, 64:65], 1.0), partial=[r_v])
                    ntile = L // TN
                    load_h(b, soff, 0, TN, 0)
                    for tix in range(ntile):
                        t0 = tix * TN
                        i = tix % 2
                        if tix + 1 < ntile:
                            load_h(b, soff, t0 + TN, TN, (tix + 1) % 2)
                        for c in range(4):
                            ps, r_ps = inproj(ht[i], r_ht[i], TN, wkv3, r_wkv, c)
                            S.op("act", lambda e: e.activation(out=kt3[:, c, t0:t0 + TN], in_=ps[:, 0:TN], func=AF.Identity,
                                                               bias=vcol(l, V_BIN, C_K // 128 + c), scale=1.0),
                                 reads=[r_ps, r_vecs], partial=[r_kt])
                        for st in range(TN // 128):
                            ps, r_ps = psum()
                            for kc in range(8):
                                S.op("pe", lambda e: e.matmul(ps[:, :], ht[i][:, kc, st * 128:(st + 1) * 128], wkv3[:, kc, 512:1024],
                                                              start=(kc == 0), stop=(kc == 7)),
                                     reads=r_wkv.chunks[4:8] + [r_ht[i]], writes=[r_ps] if kc == 0 else (), partial=[r_ps] if kc else (), inc=(kc == 7))
                            S.op("dve", lambda e: e.tensor_copy(out=v4[:, t0 // 128 + st, :, 0:64],
                                                                in_=ps[:, :].rearrange("p (h d) -> p h d", h=NH)),
                                 reads=[r_ps], partial=[r_v])
                load_cast(w_in[l], 8, 512, wqz3, r_wqz, col0=C_Q, dcol0=0)
                load_cast(w_in[l], 8, 512, wqz3, r_wqz, col0=C_CZ, dcol0=512)
                cur_dyn = [None]
                for (sname, soff, L, TN, _) in (act_seqs if not last else act_seqs[1:]):
                    ntile = L // TN
                    kc3, r_kc = KT["ctx"]
                    vc4, r_vc = VV["ctx"]
                    kl3, r_kl = KT[sname]
                    vl4, r_vl = VV[sname]
                    load_h(b, soff, 0, TN, 0)
                    sti0 = [0]
                    jobn = [0]
                    for tix in range(ntile):
                        t0 = tix * TN
                        i = tix % 2
                        if tix + 1 < ntile:
                            load_h(b, soff, t0 + TN, TN, (tix + 1) % 2)
                        for c in range(4):
                            ps, r_ps = inproj(ht[i], r_ht[i], TN, wqz3, r_wqz, c)
                            S.op("dve", lambda e: e.tensor_scalar(out=qT3[:, c, 0:TN], in0=ps[:, 0:TN],
                                                                  scalar1=vcol(l, V_BIN, C_Q // 128 + c), scalar2=HD ** -0.5,
                                                                  op0=ALU.add, op1=ALU.mult),
                                 reads=[r_ps, r_vecs], writes=[r_qT] if c == 0 else (), partial=[r_qT] if c else ())
                        for c in range(4):
                            ps, r_ps = inproj(ht[i], r_ht[i], TN, wqz3, r_wqz, 4 + c)
                            S.op("act", lambda e: e.activation(out=scz[c][:, 0:TN], in_=ps[:, 0:TN], func=AF.Silu,
                                                               bias=vcol(l, V_BIN, C_CZ // 128 + c), scale=1.0),
                                 reads=[r_ps, r_vecs], writes=[r_scz[c]])
                        tc, r_tc = tcb[i]
                        sub_state = {}

                        def sub_get(st, gsub):
                            if st in sub_state:
                                return sub_state[st]
                            chunks = []
                            if sname == "lat":
                                key, chl = geo["subs"][gsub]
                                if key == key0:
                                    bt3, r_bt = bres3, r_bres
                                else:
                                    bt3, r_bt = bdyn3, r_bdyn
                                for ci, ch in enumerate(chl):
                                    chunks.append((kl3, r_kl, vl4, r_vl, ch, (bt3, r_bt, ci)))
                            for ch in range(CTX // 128):
                                chunks.append((kc3, r_kc, vc4, r_vc, ch, None))
                            on, r_on = Ons[(sti0[0] + st) % 2]
                            sub_state[st] = dict(chunks=chunks, on=on, r_on=r_on, psO=None)
                            return sub_state[st]

                        def emit_scores(job):
                            st, h = job["st"], job["h"]
                            ss = sub_get(st, job["gsub"])
                            chunks = ss["chunks"]
                            nch = len(chunks)
                            c = h // 2
                            pb = (h % 2) * 64
                            banks = [psum() for _ in range((nch + 3) // 4)]
                            job["banks"] = banks
                            job["pt"] = PTs[jobn[0] % 3]
                            jobn[0] += 1
                            for ci, (k3, r_k, v4_, r_v_, ch, bt) in enumerate(chunks):
                                psS, r_psS = banks[ci // 4]
                                o = psS[:, (ci % 4) * 128:(ci % 4 + 1) * 128]
                                firstb = (ci % 4 == 0)
                                lastb = (ci % 4 == 3) or (ci == nch - 1)
                                S.op("pe", lambda e: e.matmul(o, k3[pb:pb + 64, c, ch * 128:(ch + 1) * 128],
                                                              qT3[pb:pb + 64, c, st * 128:(st + 1) * 128],
                                                              start=True, stop=True),
                                     reads=[r_k, r_qT], writes=[r_psS] if firstb else (), partial=() if firstb else [r_psS],
                                     inc=lastb)

                        def emit_exp(job):
                            ss = sub_get(job["st"], job["gsub"])
                            if sname == "lat":
                                key_, chl_ = geo["subs"][job["gsub"]]
                                if key_ != key0 and cur_dyn[0] != (key_,):
                                    load_bias(geo["tab_base"][key_], len(chl_), bdyn3, r_bdyn)
                                    cur_dyn[0] = (key_,)
                            nch = len(ss["chunks"])
                            pt, r_pt = job["pt"]
                            for bi, (psS, r_psS) in enumerate(job["banks"]):
                                n = min(4, nch - bi * 4) * 128
                                S.op("act", lambda e: e.activation(out=pt[:, bi * 512:bi * 512 + n], in_=psS[:, 0:n], func=AF.Exp),
                                     reads=[r_psS], writes=[r_pt] if bi == 0 else (), partial=[r_pt] if bi else ())
                            loc = [c_ for c_ in ss["chunks"] if c_[5] is not None]
                            if loc:
                                nl = len(loc)
                                bt3, r_bt = loc[0][5][0], loc[0][5][1]
                                h_ = job["h"]
                                S.op("dve", lambda e: e.tensor_tensor(out=pt[:, 0:nl * 128].rearrange("p (t q) -> p t q", t=nl),
                                                                      in0=pt[:, 0:nl * 128].rearrange("p (t q) -> p t q", t=nl),
                                                                      in1=bt3[:, h_ * nl:(h_ + 1) * nl, :], op=ALU.mult),
                                     reads=[r_pt, r_bt], writes=[r_pt])

                        def emit_pv(job):
                            st, h = job["st"], job["h"]
                            ss = sub_get(st, job["gsub"])
                            chunks = ss["chunks"]
                            nch = len(chunks)
                            hl, hg = h % 4, h // 4
                            pt, r_pt = job["pt"]
                            on, r_on = ss["on"], ss["r_on"]
                            if hl == 0:
                                ss["psO"] = psum("o")
                            psO, r_psO = ss["psO"]
                            for ci, (k3, r_k, v4_, r_v_, ch, bt) in enumerate(chunks):
                                S.op("pe", lambda e: e.matmul(psO[:, hl * 65:hl * 65 + 65], pt[:, ci * 128:(ci + 1) * 128],
                                                              v4_[:, ch, h, :], start=(ci == 0), stop=(ci == nch - 1)),
                                     reads=[r_pt, r_v_], writes=[r_psO] if (hl == 0 and ci == 0) else (),
                                     partial=() if (hl == 0 and ci == 0) else [r_psO], inc=(ci == nch - 1))
                            if hl == 3:
                                pso4 = psO[:, 0:260].rearrange("p (h d) -> p h d", h=4)
                                S.op("dve", lambda e: e.reciprocal(out=rec[:, hg * 4:hg * 4 + 4], in_=pso4[:, :, 64]),
                                     reads=[r_psO], writes=[r_rec] if hg == 0 else (), partial=[r_rec] if hg else ())
                                for hl2 in range(4):
                                    h2 = hg * 4 + hl2
                                    S.op("dve", lambda e: e.tensor_scalar(out=on[:, h2 * 64:(h2 + 1) * 64], in0=psO[:, hl2 * 65:hl2 * 65 + 64],
                                                                          scalar1=rec[:, h2:h2 + 1], scalar2=None, op0=ALU.mult),
                                         reads=[r_psO, r_rec], writes=[r_on] if h2 == 0 else (), partial=[r_on] if h2 else ())
                            if h == NH - 1:
                                for c in range(4):
                                    S.op("pe", lambda e: e.transpose(pst[:, c * 128:(c + 1) * 128], on[:, c * 128:(c + 1) * 128], ident),
                                         reads=[r_on, r_consts], writes=[r_pst] if c == 0 else (), partial=[r_pst] if c else (), inc=(c == 3))
                                for c in range(4):
                                    S.op("dve", lambda e: e.scalar_tensor_tensor(out=tc[:, c, st * 128:(st + 1) * 128],
                                                                                 in0=pst[:, c * 128:(c + 1) * 128],
                                                                                 scalar=vcol(l, V_BIN, C_V // 128 + c),
                                                                                 in1=scz[c][:, st * 128:(st + 1) * 128],
                                                                                 op0=ALU.add, op1=ALU.mult),
                                         reads=[r_pst, r_vecs, r_scz[c]],
                                         writes=[r_tc] if (st == 0 and c == 0) else (), partial=() if (st == 0 and c == 0) else [r_tc])

                        jobs = [dict(st=st, h=h, gsub=t0 // 128 + st) for st in range(TN // 128) for h in range(NH)]
                        emit_scores(jobs[0])
                        for ji, job in enumerate(jobs):
                            if ji + 1 < len(jobs):
                                emit_scores(jobs[ji + 1])
                            emit_exp(job)
                            if ji >= 1:
                                emit_pv(jobs[ji - 1])
                        emit_pv(jobs[-1])
                        sti0[0] += TN // 128
                        S.dma(tT[b, 2].rearrange("(k p) n -> p k n", p=128)[:, :, soff + t0:soff + t0 + TN], tc[:, :, 0:TN],
                              f"d_tc{i}", reads=[r_tc], writes=[hres("t", b, 2, soff + t0)])

            for b in range(NB):
                mcols = {"ctx": 2, "lat": b}
                act_seqs = seqs
                if b == 0:
                    S.barrier()
                    ps_pool["gen"] = list(range(7))
                    car = Carver(arena, ARN)
                    wg, _ = car.take(8 * 3072, "wg"); r_wg = WSet(24, "wg"); wg3 = wg.rearrange("p (k n) -> p k n", k=8)
                    wbr = []
                    for j in range(3):
                        v, r = car.take(4 * 1024, f"wbr{j}")
                        wbr.append((v.rearrange("p (k n) -> p k n", k=4), WSet(8, f"wbr{j}")))
                    wo, _ = car.take(8 * 1024, "wo"); r_wo = WSet(8, "wo"); wo3 = wo.rearrange("p (k n) -> p k n", k=8)
                    tin = []
                    for i in range(2):
                        row = []
                        for j in range(3):
                            v, r = car.take(4 * 512, f"tin{i}{j}")
                            row.append((v.rearrange("p (k n) -> p k n", k=4), r))
                        tin.append(row)
                    wcar = Carver(work, WORKN)
                    Gb, r_G = wcar.take(8 * 512, "G"); G3 = Gb.rearrange("p (k n) -> p k n", k=8)
                    f32a, r_f32a = carve_f32a(wcar, 3)
                    xq, r_xq = take_f32(wcar, 8 * 512, "xq")
                    xq3 = xq.rearrange("p (k n) -> p k n", k=8)
                    load_cast(w_in[l], 8, 3072, wg3, r_wg, col0=C_GA)
                    for j in range(3):
                        load_cast(w_br[l, j], 4, 1024, wbr[j][0], wbr[j][1])
                    load_cast(w_out[l], 8, 1024, wo3, r_wo)
                xi = 0
                for (sname, soff, L, TN, _) in (act_seqs if not last else act_seqs[1:]):
                    mc = mcols[sname]
                    ntile = L // TN

                    def load_tile(tix):
                        t0_ = tix * TN
                        i_ = tix % 2
                        load_h(b, soff, t0_, TN, i_)
                        for j in range(3):
                            S.dma(tin[i_][j][0][:, :, 0:TN],
                                  tT[b, j].rearrange("(k p) n -> p k n", p=128)[:, :, soff + t0_:soff + t0_ + TN],
                                  f"d_tin{i_}{j}", reads=[hres("t", b, j, soff + t0_)], writes=[tin[i_][j][1]])
                    load_tile(0)
                    for tix in range(ntile):
                        t0 = tix * TN
                        i = tix % 2
                        if tix + 1 < ntile:
                            load_tile(tix + 1)
                        S.dma(xq3[:, :, 0:TN], xsrc[b].rearrange("(k p) n -> p k n", p=128)[:, :, soff + t0:soff + t0 + TN],
                              "d_xq", reads=[hres("x", b, soff + t0)], writes=[r_xq])
                        for m in range(8):
                            for j in range(3):
                                tj, r_tj = tin[i][j]
                                wj, r_wj = wbr[j]
                                psy, r_psy = psum()
                                for kc in range(4):
                                    S.op("pe", lambda e: e.matmul(psy[:, 0:TN], wj[:, kc, m * 128:(m + 1) * 128], tj[:, kc, 0:TN],
                                                                  start=(kc == 0), stop=(kc == 3)),
                                         reads=[r_wj.chunks[m], r_tj], writes=[r_psy] if kc == 0 else (), partial=[r_psy] if kc else (), inc=(kc == 3))
                                psg, r_psg = inproj(ht[i], r_ht[i], TN, wg3, r_wg, j * 8 + m)
                                sg, r_sg = f32a[j], r_f32a[j]
                                S.op("act", lambda e: e.activation(out=sg[:, 0:TN], in_=psg[:, 0:TN], func=AF.Sigmoid,
                                                                   bias=vcol(l, V_BIN, C_GA // 128 + j * 8 + m), scale=1.0),
                                     reads=[r_psg, r_vecs], writes=[r_sg])
                                S.op("dve", lambda e: e.tensor_tensor(out=sg[:, 0:TN], in0=psy[:, 0:TN], in1=sg[:, 0:TN], op=ALU.mult),
                                     reads=[r_psy, r_sg], writes=[r_sg])
                            S.op("pool", lambda e: e.tensor_tensor(out=f32a[0][:, 0:TN], in0=f32a[0][:, 0:TN], in1=f32a[1][:, 0:TN], op=ALU.add),
                                 reads=[r_f32a[0], r_f32a[1]], writes=[r_f32a[0]])
                            S.op("pool", lambda e: e.tensor_tensor(out=G3[:, m, 0:TN], in0=f32a[0][:, 0:TN], in1=f32a[2][:, 0:TN], op=ALU.add),
                                 reads=[r_f32a[0], r_f32a[2]], writes=[r_G] if m == 0 else (), partial=[r_G] if m else ())
                        for mo in range(8):
                            xc = xq3[:, mo, 0:TN]
                            pso, r_pso = psum()
                            for kc in range(8):
                                S.op("pe", lambda e: e.matmul(pso[:, 0:TN], wo3[:, kc, mo * 128:(mo + 1) * 128], G3[:, kc, 0:TN],
                                                              start=(kc == 0), stop=(kc == 7)),
                                     reads=[r_wo.chunks[mo], r_G], writes=[r_pso] if kc == 0 else (), partial=[r_pso] if kc else (), inc=(kc == 7))
                            S.op("dve", lambda e: e.scalar_tensor_tensor(out=xc, in0=pso[:, 0:TN], scalar=mod[:, 16 + mo, mc:mc + 1],
                                                                         in1=xc, op0=ALU.mult, op1=ALU.add),
                                 reads=[r_pso, r_mod, r_xq], partial=[r_xq])
                        S.dma(xs[b].rearrange("(k p) n -> p k n", p=128)[:, :, soff + t0:soff + t0 + TN], xq3[:, :, 0:TN],
                              "d_xq", reads=[r_xq], writes=[hres("xn", b, soff + t0)])
                for (sname, soff, L, TN, _) in (act_seqs if not last else act_seqs[1:]):
                    for t0 in range(0, L, TN):
                        hbm_res[("x", b, soff + t0)] = hres("xn", b, soff + t0)
                        del hbm_res[("xn", b, soff + t0)]

            for b in range(NB):
                if last:
                    S.barrier()
                    ps_pool["gen"] = list(range(7))
                    car = Carver(arena, ARN)
                    sq, r_sq = car.take(8 * 512, "sq")
                    sq3 = sq.rearrange("p (k n) -> p k n", k=8)
                    xt, r_xt = carve_xt(car)
                    (sname, soff, L, TN, _) = seqs[1]
                    fgc = NVL * depth
                    for tix in range(L // TN):
                        t0 = tix * TN
                        i = tix % 2
                        S.dma(xt[i][:, :, 0:TN], xs[b].rearrange("(k p) n -> p k n", p=128)[:, :, soff + t0:soff + t0 + TN],
                              f"d_xt{i}", reads=[hres("x", b, soff + t0)], writes=[r_xt[i]])
                        S.op("act", lambda e: e.activation(out=sq3[:, :, 0:TN], in_=xt[i][:, :, 0:TN], func=AF.Square),
                             reads=[r_xt[i]], writes=[r_sq])
                        ps, r_ps = psum()
                        for kc in range(8):
                            S.op("pe", lambda e: e.matmul(ps[:, 0:TN], ones, sq3[:, kc, 0:TN], start=(kc == 0), stop=(kc == 7)),
                                 reads=[r_sq, r_consts], writes=[r_ps] if kc == 0 else (), partial=[r_ps] if kc else (), inc=(kc == 7))
                        S.op("act", lambda e: e.activation(out=rstd[i][:, 0:TN], in_=ps[:, 0:TN], func=AF.Sqrt, bias=epsc[:, 0:1], scale=1.0 / D),
                             reads=[r_ps, r_consts], writes=[r_rstd[i]])
                        S.op("dve", lambda e: e.reciprocal(out=rstd[i][:, 0:TN], in_=rstd[i][:, 0:TN]),
                             reads=[r_rstd[i]], writes=[r_rstd[i]])
                        for kc in range(8):
                            S.op("dve",
                                 lambda e: e.scalar_tensor_tensor(out=xt[i][:, kc, 0:TN], in0=xt[i][:, kc, 0:TN],
                                                                  scalar=vecs[:, fgc + kc:fgc + kc + 1], in1=rstd[i][:, 0:TN],
                                                                  op0=ALU.mult, op1=ALU.mult),
                                 reads=[r_vecs, r_rstd[i], r_xt[i]], writes=[r_xt[i]])
                        S.dma(y_out[b].rearrange("(k p) n -> p k n", p=128)[:, :, t0:t0 + TN], xt[i][:, :, 0:TN],
                              f"d_xt{i}", reads=[r_xt[i]], writes=[hres("y", b, t0)])
        S.barrier(engines=("sp",))
        print("program built: ninst", S.ninst, "nwaits", S.nwaits, "nsems", len(S.sems))
    return nc


_CACHE = {}


def _consts(seq):
    key = ("c", seq)
    if key not in _CACHE:
        cst = np.zeros((128, 512), dtype=NPBF)
        cst[:, 0:128] = np.eye(128).astype(NPBF)
        cst[:, 128:256] = np.ones((128, 128)).astype(NPBF)
        cst[:, 256:512] = _chan_table().astype(NPBF)
        dftl = _dft_pos_table(seq, 512, 4)
        dftc = _dft_pos_table(CTX, 256, 2)
        _CACHE[key] = (cst, dftl, dftc)
    return _CACHE[key]


def kernel(x, c, ctx, c_ctx, norm_g, w_ada, b_ada, w_in, b_in, w_conv, b_conv,
           rpb, w_br_a, w_br_f, w_br_c, w_out, final_g):
    x = np.asarray(x, dtype=np.float32)
    depth = int(np.asarray(w_in).shape[0])
    seq = int(x.shape[1])
    B = int(x.shape[0])
    assert B == NB * NCORES
    geo = _attn_geometry(seq)
    cst, dftl, dftc = _consts(seq)
    f = lambda a: np.ascontiguousarray(np.asarray(a, dtype=np.float32))
    c, ctx, c_ctx = f(c), f(ctx), f(c_ctx)
    norm_g, b_ada, b_in, w_conv, b_conv, final_g = f(norm_g), f(b_ada), f(b_in), f(w_conv), f(b_conv), f(final_g)
    w_ada, w_in, w_out = f(w_ada), f(w_in), f(w_out)
    w_br = np.ascontiguousarray(np.stack([f(w_br_a), f(w_br_f), f(w_br_c)], axis=1))
    rpb = f(rpb)
    NV = NVL * depth + 8
    vecs = np.zeros((128, NV), dtype=np.float32)
    for l in range(depth):
        o = l * NVL
        vecs[:, o + V_NG:o + V_NG + 8] = norm_g[l].reshape(8, 128).T
        vecs[:, o + V_BADA:o + V_BADA + 24] = b_ada[l].reshape(24, 128).T
        vecs[:, o + V_BIN:o + V_BIN + 64] = b_in[l].reshape(64, 128).T
        vecs[:, o + V_WC:o + V_WC + 12] = w_conv[l].reshape(3, 4, 128).transpose(2, 1, 0).reshape(128, 12)
        vecs[:, o + V_BC:o + V_BC + 4] = b_conv[l].reshape(4, 128).T
    vecs[:, NVL * depth:NVL * depth + 8] = final_g.reshape(8, 128).T
    bias_tab = np.stack([_bias_tables(rpb[l], geo) for l in range(depth)], axis=0)
    xall = np.concatenate([ctx.transpose(0, 2, 1), x.transpose(0, 2, 1)], axis=2)
    key = ("nc", seq, depth)
    if key not in _CACHE:
        _CACHE[key] = build_program(seq, depth)
    nc = _CACHE[key]
    in_maps = []
    for core in range(NCORES):
        b0 = core * NB
        cT = np.stack([c[b0], c[b0 + 1], c_ctx], axis=1)
        cT = np.ascontiguousarray(cT.reshape(8, 128, 3).transpose(1, 0, 2))
        in_maps.append({
            "x_in": np.ascontiguousarray(xall[b0:b0 + NB]),
            "cT": cT, "vecs": vecs, "w_ada": w_ada, "w_in": w_in, "w_br": w_br, "w_out": w_out,
            "bias_tab": bias_tab, "consts": cst, "dftl": dftl, "dftc": dftc,
        })
    res = run_bass_kernel_spmd(nc, in_maps, core_ids=list(range(NCORES)))
    ys = [np.asarray(r["y"], dtype=np.float32) for r in res.results]
    y = np.concatenate(ys, axis=0)
    return np.ascontiguousarray(y.transpose(0, 2, 1))
```

```python
import contextlib
import numpy as np
import ml_dtypes
import concourse.bass as bass
import concourse.mybir as mybir
from concourse.bass_utils import run_bass_kernel_spmd

F32 = mybir.dt.float32
BF16 = mybir.dt.bfloat16
AF = mybir.ActivationFunctionType
ALU = mybir.AluOpType
NPBF = ml_dtypes.bfloat16

D = 1024
SEQ = 4096
DEPTH = 4
NCORES = 8
NB = 2
CTX = 256
GW = 64
NH = 8
HD = 64
WIN_H = 8
WIN_W = 16
DIN = 8192
EPS = 1e-6
NEG = -30000.0
C_AX, C_AB, C_AC, C_AZ, C_FU, C_FZ, C_Q, C_K, C_V, C_CZ, C_GA, C_GF, C_GC = (
    0, 512, 1024, 1536, 2048, 2560, 3072, 3584, 4096, 4608, 5120, 6144, 7168)
V_NG, V_BADA, V_BIN, V_WC, V_BC, NVL = 0, 8, 32, 96, 108, 112


class Res:
    __slots__ = ("name", "lw", "rd")

    def __init__(self, name):
        self.name = name
        self.lw = {}
        self.rd = {}


class Sched:
    def __init__(self, nc, es):
        self.nc = nc
        self.es = es
        self.eng = {"pe": nc.tensor, "act": nc.scalar, "dve": nc.vector, "pool": nc.gpsimd, "sp": nc.sync}
        self.sems = {}
        self.cnt = {}
        self.isdma = {}
        self.waited = {e: {} for e in self.eng}
        for e in ("pe", "act", "dve", "pool"):
            self._sem(e, False)
        self.ninst = 0
        self.nwaits = 0

    def _sem(self, key, isdma):
        if key not in self.sems:
            self.sems[key] = self.es.enter_context(self.nc.semaphore("s_" + str(key)))
            self.cnt[key] = 0
            self.isdma[key] = isdma
        return self.sems[key]

    def _wait(self, e, k, v):
        if self.isdma[k]:
            v = self.cnt[k]
        w = self.waited[e]
        if w.get(k, 0) >= v:
            return
        if e == "pe" and k == "pe":
            return
        self.eng[e].wait_ge(self.sems[k], v)
        w[k] = v
        self.nwaits += 1

    def deps(self, e, reads, writes, partial):
        for r in reads:
            for k, v in r.lw.items():
                self._wait(e, k, v)
        for w_ in writes:
            for k, v in w_.lw.items():
                self._wait(e, k, v)
            for k, v in w_.rd.items():
                self._wait(e, k, v)
        for w_ in partial:
            for k, v in w_.rd.items():
                self._wait(e, k, v)

    def mark(self, ev, reads, writes, partial):
        k, v = ev
        for r in reads:
            if r.rd.get(k, 0) < v:
                r.rd[k] = v
        for w_ in writes:
            w_.lw = {k: v}
            w_.rd = {}
        for w_ in partial:
            if w_.lw.get(k, 0) < v:
                w_.lw[k] = v

    def op(self, e, fn, reads=(), writes=(), partial=(), inc=True):
        self.deps(e, reads, writes, partial)
        ins = fn(self.eng[e])
        self.ninst += 1
        if inc:
            self.cnt[e] += 1
            ins.then_inc(self.sems[e], 1)
            ev = (e, self.cnt[e])
        else:
            ev = (e, self.cnt[e] + 1)
        self.mark(ev, reads, writes, partial)
        return ins

    def dma(self, out, in_, semkey, reads=(), writes=(), partial=(), e="sp"):
        self._sem(semkey, True)
        self.deps(e, reads, writes, partial)
        ins = self.eng[e].dma_start(out=out, in_=in_)
        self.ninst += 1
        self.cnt[semkey] += 16
        ins.then_inc(self.sems[semkey], 16)
        self.mark((semkey, self.cnt[semkey]), reads, writes, partial)
        return ins

    def barrier(self, engines=("pe", "act", "dve", "pool", "sp")):
        for e in engines:
            for k in list(self.sems.keys()):
                if self.cnt[k] > 0:
                    self._wait(e, k, self.cnt[k])


class Buf:
    def __init__(self, ap_fn, name):
        self.t = ap_fn
        self.r = Res(name)


def _dft_pos_table(L, TN, LG):
    m = np.arange(L, dtype=np.float64)
    cosv = np.cos(2 * np.pi * m / L) / np.sqrt(L)
    sinv = -np.sin(2 * np.pi * m / L) / np.sqrt(L)
    l = np.arange(L, dtype=np.int64)
    idx = (l[:, None] * l[None, :]) % L
    ntile = L // TN
    nlc = L // 128
    nlg = nlc // LG
    out = np.empty((ntile, nlg, 128, LG, 2, TN), dtype=NPBF)
    for cs, tab in ((0, cosv), (1, sinv)):
        full = tab[idx].astype(NPBF)
        full = full.reshape(nlg, LG, 128, ntile, TN)
        out[:, :, :, :, cs, :] = full.transpose(3, 0, 2, 1, 4)
    return out


def _chan_table():
    c = np.arange(128, dtype=np.int64)
    idx = (c[:, None] * c[None, :]) % 128
    ang = 2 * np.pi * idx / 128.0
    return np.concatenate([np.cos(ang), np.sin(ang)], axis=1) / np.sqrt(128.0)


def _attn_geometry(seq):
    R = seq // GW
    kh = min(WIN_H, R)

    def rs(r):
        return int(np.clip(r - kh // 2, 0, R - kh))
    subs = []
    pats = {}
    for j in range(R // 2):
        r0 = 2 * j
        a0, a1 = rs(r0) - r0, rs(r0 + 1) - r0
        lo, hi = rs(r0), rs(r0 + 1) + kh - 1
        chunks = list(range(lo // 2, hi // 2 + 1))
        key = (a0, a1)
        deltas = tuple(2 * c - r0 for c in chunks)
        if key not in pats:
            pats[key] = deltas
        assert pats[key] == deltas
        subs.append((key, chunks))
    cnts = {}
    for key, _ in subs:
        cnts[key] = cnts.get(key, 0) + 1
    order = sorted(pats.keys(), key=lambda k: -cnts[k])
    tab_base = {}
    n = 0
    for key in order:
        tab_base[key] = n
        n += len(pats[key])
    return dict(R=R, kh=kh, subs=subs, pats=pats, order=order, tab_base=tab_base, ntab=n)


def _bias_tables(rpb_l, geo):
    kh = geo["kh"]
    out = np.full((geo["ntab"], NH, 128, 128), NEG, dtype=np.float32)
    qc = np.arange(GW)
    kc = np.arange(GW)
    cs = np.clip(qc - WIN_W // 2, 0, GW - WIN_W)
    col_ok = (kc[None, :] >= cs[:, None]) & (kc[None, :] < cs[:, None] + WIN_W)
    dc = np.clip(kc[None, :] - qc[:, None] + WIN_W - 1, 0, 2 * WIN_W - 2)
    for key in geo["order"]:
        a0, a1 = key
        for ci, dl in enumerate(geo["pats"][key]):
            t = geo["tab_base"][key] + ci
            for qr in range(2):
                a = a0 if qr == 0 else a1
                for krl in range(2):
                    rel = dl + krl
                    if not (a <= rel < a + kh):
                        continue
                    dr = rel - qr + WIN_H - 1
                    blk = np.where(col_ok[None], rpb_l[:, dr][:, dc], np.float32(NEG))
                    out[t, :, qr * 64:(qr + 1) * 64, krl * 64:(krl + 1) * 64] = blk
    res = np.empty((geo["ntab"] * NH, 128, 128), dtype=np.float32)
    for key in geo["order"]:
        tb, n = geo["tab_base"][key], len(geo["pats"][key])
        sub = out[tb:tb + n]
        res[tb * NH:(tb + n) * NH] = sub.transpose(1, 0, 3, 2).reshape(n * NH, 128, 128)
    return res


def build_program(seq, depth):
    geo = _attn_geometry(seq)
    LT = CTX + seq
    nc = bass.Bass("TRN2", target_bir_lowering=False)
    NV = NVL * depth + 8
    x_in = nc.dram_tensor("x_in", [NB, D, LT], F32, kind="ExternalInput").ap()
    cT_in = nc.dram_tensor("cT", [128, 8, 3], F32, kind="ExternalInput").ap()
    vecs_in = nc.dram_tensor("vecs", [128, NV], F32, kind="ExternalInput").ap()
    w_ada = nc.dram_tensor("w_ada", [depth, D, 3 * D], F32, kind="ExternalInput").ap()
    w_in = nc.dram_tensor("w_in", [depth, D, DIN], F32, kind="ExternalInput").ap()
    w_br = nc.dram_tensor("w_br", [depth, 3, 512, D], F32, kind="ExternalInput").ap()
    w_out = nc.dram_tensor("w_out", [depth, D, D], F32, kind="ExternalInput").ap()
    bias_in = nc.dram_tensor("bias_tab", [depth, geo["ntab"] * NH, 128, 128], F32, kind="ExternalInput").ap()
    consts_in = nc.dram_tensor("consts", [128, 512], BF16, kind="ExternalInput").ap()
    LGL = 4
    dftl_in = nc.dram_tensor("dftl", [seq // 512, seq // 128 // LGL, 128, LGL, 2, 512], BF16, kind="ExternalInput").ap()
    dftc_in = nc.dram_tensor("dftc", [1, 1, 128, 2, 2, 256], BF16, kind="ExternalInput").ap()
    y_out = nc.dram_tensor("y", [NB, D, seq], F32, kind="ExternalOutput").ap()
    xs = nc.dram_tensor("xs", [NB, D, LT], F32, kind="Internal").ap()
    hT = nc.dram_tensor("hT", [NB, D, LT], BF16, kind="Internal").ap()
    tT = nc.dram_tensor("tT", [NB, 3, 512, LT], BF16, kind="Internal").ap()

    es = contextlib.ExitStack()
    with es:
        S = Sched(nc, es)

        def sb(name, shape, dt):
            return es.enter_context(nc.sbuf_tensor("sb_" + name, shape, dt))

        vecs = sb("vecs", [128, NV], F32); r_vecs = Res("vecs")
        consts = sb("consts", [128, 512], BF16); r_consts = Res("consts")
        ident = consts[:, 0:128]
        ones = consts[:, 128:256]
        csc = consts[:, 256:512]
        epsc = sb("epsc", [128, 1], F32)
        cts = sb("cts", [128, 8, 3], F32); r_cts = Res("cts")
        mod = sb("mod", [128, 24, 3], F32); r_mod = Res("mod")
        gm = sb("gm", [128, 8, 3], F32); r_gm = Res("gm")
        ht = [sb(f"ht{i}", [128, 8, 512], BF16) for i in range(2)]; r_ht = [Res(f"ht{i}") for i in range(2)]
        wst = [sb(f"wst{i}", [128, 2048], F32) for i in range(4)]; r_wst = [Res(f"wst{i}") for i in range(4)]
        rstd = [sb(f"rstd{i}", [128, 512], F32) for i in range(2)]; r_rstd = [Res(f"rstd{i}") for i in range(2)]
        WORKN = 16384
        work = sb("work", [128, WORKN], BF16)
        ARN = 57344
        arena = sb("arena", [128, ARN], BF16)
        psf = [es.enter_context(nc.psum_tensor(f"psf{i}", [128, 512], F32)) for i in range(7)]
        r_psf = [Res(f"psf{i}") for i in range(7)]
        pst = es.enter_context(nc.psum_tensor("pst", [128, 1024], BF16)); r_pst = Res("pst")
        ps_pool = {"gen": list(range(7)), "acc": [3, 4, 5, 6], "o": [5, 6]}
        ps_rr = {"gen": 0, "acc": 0, "o": 0}

        def psum(pool="gen"):
            lst = ps_pool[pool]
            i = lst[ps_rr[pool] % len(lst)]
            ps_rr[pool] += 1
            return psf[i], r_psf[i]

        hbm_res = {}

        def hres(*key):
            if key not in hbm_res:
                hbm_res[key] = Res(str(key))
            return hbm_res[key]

        class Carver:
            def __init__(self, base, total):
                self.base = base
                self.total = total
                self.off = 0

            def take(self, n, name):
                assert self.off + n <= self.total, (name, self.off, n, self.total)
                v = self.base[:, self.off:self.off + n]
                self.off += n
                return v, Res(name)

        def take_f32(carver, n, name):
            v, r = carver.take(2 * n, name)
            return v.bitcast(F32), r

        def carve_xt(carver):
            xs_, rs_ = [], []
            for i in range(2):
                v, r = take_f32(carver, 8 * 512, f"xt{i}")
                xs_.append(v.rearrange("p (k n) -> p k n", k=8))
                rs_.append(r)
            return xs_, rs_

        def carve_f32a(carver, n):
            fs_, rs_ = [], []
            for i in range(n):
                v, r = take_f32(carver, 512, f"f32a{i}")
                fs_.append(v)
                rs_.append(r)
            return fs_, rs_

        rr = {"evac": 0, "wst": 0, "cast": 0}
        di_glob = [0]

        class WSet:
            def __init__(self, n, name):
                self.chunks = [Res(f"{name}{i}") for i in range(n)]

        def cast(out, in_, reads, partial):
            if rr["cast"] % 2 == 0:
                S.op("act", lambda e: e.activation(out=out, in_=in_, func=AF.Copy), reads=reads, partial=partial)
            else:
                S.op("dve", lambda e: e.tensor_copy(out=out, in_=in_), reads=reads, partial=partial)
            rr["cast"] += 1

        def load_cast(src2d, kcn, ncols, dst3, wset, col0=0, dcol0=0):
            src = src2d.rearrange("(kc p) n -> p kc n", p=128)
            piece = max(1, 2048 // kcn)
            piece = min(piece, ncols)
            c = 0
            while c < ncols:
                w = min(piece, ncols - c)
                i = rr["wst"] % 4
                rr["wst"] += 1
                stg = wst[i][:, 0:kcn * w].rearrange("p (k n) -> p k n", k=kcn)
                S.dma(stg, src[:, :, col0 + c:col0 + c + w], f"d_wst{i}", writes=[r_wst[i]])
                cks = wset.chunks[(dcol0 + c) // 128:(dcol0 + c + w + 127) // 128]
                cast(dst3[:, :, dcol0 + c:dcol0 + c + w], stg, [r_wst[i]], cks)
                c += w

        def inproj(htb, r_htb, N, wv, r_w, mchunk):
            ps, r_ps = psum()
            for kc in range(8):
                S.op("pe", lambda e: e.matmul(ps[:, 0:N], wv[:, kc, mchunk * 128:(mchunk + 1) * 128], htb[:, kc, 0:N],
                                              start=(kc == 0), stop=(kc == 7)),
                     reads=[r_w.chunks[mchunk], r_htb], writes=[r_ps] if kc == 0 else (), partial=[r_ps] if kc else (), inc=(kc == 7))
            return ps, r_ps

        def vcol(l, off, j=0):
            c = l * NVL + off + j
            return vecs[:, c:c + 1]

        def load_h(b, seqoff, t0, N, slot):
            S.dma(ht[slot][:, :, 0:N],
                  hT[b].rearrange("(k p) n -> p k n", p=128)[:, :, seqoff + t0:seqoff + t0 + N],
                  f"d_ht{slot}", reads=[hres("h", b, seqoff + t0)], writes=[r_ht[slot]])

        seqs = [("ctx", 0, CTX, 256, 2), ("lat", CTX, seq, 512, 3)]

        S.dma(vecs[:, :], vecs_in[:, :], "d_misc", writes=[r_vecs])
        S.dma(consts[:, :], consts_in[:, :], "d_misc", writes=[r_consts])
        S.dma(cts[:, :, :], cT_in[:, :, :], "d_misc", writes=[r_cts])
        S.op("pool", lambda e: e.memset(epsc[:, :], EPS), partial=[r_consts])
        S.op("act", lambda e: e.activation(out=cts[:, :, :], in_=cts[:, :, :], func=AF.Silu), reads=[r_cts], writes=[r_cts])

        for l in range(depth):
            last = (l == depth - 1)
            xsrc = x_in if l == 0 else xs
            S.barrier()
            ps_pool["gen"] = list(range(7))
            car = Carver(arena, ARN)
            xt, r_xt = carve_xt(car)
            for pc in range(8):
                i = pc % 2
                stg = xt[i][:, :, 0:384]
                S.dma(stg, w_ada[l].rearrange("(kc p) n -> p kc n", p=128)[:, :, pc * 384:(pc + 1) * 384],
                      f"d_xt{i}", writes=[r_xt[i]])
                ps, r_ps = psum()
                for mm in range(3):
                    for kc in range(8):
                        S.op("pe", lambda e: e.matmul(ps[:, mm * 4:mm * 4 + 3], stg[:, kc, mm * 128:(mm + 1) * 128],
                                                      cts[:, kc, :], start=(kc == 0), stop=(kc == 7)),
                             reads=[r_xt[i], r_cts], writes=[r_ps] if (mm == 0 and kc == 0) else (),
                             partial=() if (mm == 0 and kc == 0) else [r_ps], inc=(mm == 2 and kc == 7))
                for mm in range(3):
                    m = pc * 3 + mm
                    S.op("dve", lambda e: e.tensor_scalar(out=mod[:, m, :], in0=ps[:, mm * 4:mm * 4 + 3],
                                                          scalar1=vcol(l, V_BADA, m), scalar2=None, op0=ALU.add),
                         reads=[r_ps, r_vecs], partial=[r_mod])
            for j in range(3):
                S.op("dve", lambda e: e.scalar_tensor_tensor(out=gm[:, :, j], in0=mod[:, 8:16, j], scalar=1.0,
                                                             in1=vecs[:, l * NVL + V_NG:l * NVL + V_NG + 8],
                                                             op0=ALU.add, op1=ALU.mult),
                     reads=[r_mod, r_vecs], partial=[r_gm])

            for b in range(NB):
                mcols = {"ctx": 2, "lat": b}
                act_seqs = seqs
                S.barrier()
                ps_pool["gen"] = list(range(7))
                car = Carver(arena, ARN)
                sq, r_sq = car.take(8 * 512, "sq")
                r_sqh = [r_sq, Res("sq_hi")]
                sq3 = sq.rearrange("p (k n) -> p k n", k=8)
                xt, r_xt = carve_xt(car)
                wcar = Carver(work, WORKN)
                f32a, r_f32a = carve_f32a(wcar, 2)
                p1tiles = [(sname, soff, t0, TN, mcols[sname]) for (sname, soff, L, TN, _) in act_seqs for t0 in range(0, L, TN)]

                def p1load(n):
                    (sname_, soff_, t0_, TN_, mc_) = p1tiles[n]
                    S.dma(xt[n % 2][:, :, 0:TN_],
                          xsrc[b].rearrange("(k p) n -> p k n", p=128)[:, :, soff_ + t0_:soff_ + t0_ + TN_],
                          f"d_xt{n % 2}", reads=[hres("x", b, soff_ + t0_)], writes=[r_xt[n % 2]])
                p1load(0)
                for ti, (sname, soff, t0, TN, mc) in enumerate(p1tiles):
                    if True:
                        i = ti % 2
                        if ti + 1 < len(p1tiles):
                            p1load(ti + 1)
                        S.op("act", lambda e: e.activation(out=sq3[:, 0:4, 0:TN], in_=xt[i][:, 0:4, 0:TN], func=AF.Square),
                             reads=[r_xt[i]], writes=[r_sqh[0]])
                        S.op("dve", lambda e: e.tensor_tensor(out=sq3[:, 4:8, 0:TN], in0=xt[i][:, 4:8, 0:TN], in1=xt[i][:, 4:8, 0:TN], op=ALU.mult),
                             reads=[r_xt[i]], writes=[r_sqh[1]])
                        ps, r_ps = psum()
                        for kc in range(8):
                            S.op("pe", lambda e: e.matmul(ps[:, 0:TN], ones, sq3[:, kc, 0:TN], start=(kc == 0), stop=(kc == 7)),
                                 reads=[r_sqh[kc // 4], r_consts], writes=[r_ps] if kc == 0 else (), partial=[r_ps] if kc else (),
                                 inc=(kc == 7))
                        S.op("act", lambda e: e.activation(out=rstd[i][:, 0:TN], in_=ps[:, 0:TN], func=AF.Sqrt, bias=epsc[:, 0:1], scale=1.0 / D),
                             reads=[r_ps, r_consts], writes=[r_rstd[i]])
                        S.op("dve", lambda e: e.reciprocal(out=rstd[i][:, 0:TN], in_=rstd[i][:, 0:TN]),
                             reads=[r_rstd[i]], writes=[r_rstd[i]])
                        for kc in range(8):
                            fi = kc % 2
                            S.op("dve", lambda e: e.scalar_tensor_tensor(out=f32a[fi][:, 0:TN], in0=xt[i][:, kc, 0:TN],
                                                                         scalar=gm[:, kc, mc:mc + 1], in1=rstd[i][:, 0:TN],
                                                                         op0=ALU.mult, op1=ALU.mult),
                                 reads=[r_xt[i], r_gm, r_rstd[i]], writes=[r_f32a[fi]])
                            S.op("act", lambda e: e.activation(out=ht[i][:, kc, 0:TN], in_=f32a[fi][:, 0:TN], func=AF.Identity,
                                                               bias=mod[:, kc, mc:mc + 1], scale=1.0),
                                 reads=[r_f32a[fi], r_mod], writes=[r_ht[i]] if kc == 0 else (), partial=[r_ht[i]] if kc else ())
                        S.dma(hT[b].rearrange("(k p) n -> p k n", p=128)[:, :, soff + t0:soff + t0 + TN], ht[i][:, :, 0:TN],
                              f"d_ht{i}", reads=[r_ht[i]], writes=[hres("h", b, soff + t0)])

            for b in range(NB):
                mcols = {"ctx": 2, "lat": b}
                act_seqs = seqs
                for si_, (sname, soff, L, TN, _) in enumerate(act_seqs if not last else act_seqs[1:]):
                    if b == 0 and si_ == 0:
                        S.barrier()
                        ps_pool["gen"] = [0, 1, 2]
                        car = Carver(arena, ARN)
                        wfu, _ = car.take(8 * 512, "wfu"); r_wfu = WSet(4, "wfu"); wfu3 = wfu.rearrange("p (k n) -> p k n", k=8)
                        wfz, _ = car.take(8 * 512, "wfz"); r_wfz = WSet(4, "wfz"); wfz3 = wfz.rearrange("p (k n) -> p k n", k=8)
                        ABfull, r_AB = car.take((seq // 128) * 1024, "AB")
                        dslfull = [car.take(LGL * 2 * 512, f"dft{i}") for i in range(4)]
                        wcar = Carver(work, WORKN)
                        UT = []
                        for i in range(2):
                            v, r = wcar.take(4 * 512, f"UT{i}")
                            UT.append((v.rearrange("p (g n) -> p g n", g=4), r))
                        szb, r_sz = wcar.take(4 * 512, "sz"); sz3 = szb.rearrange("p (g n) -> p g n", g=4)
                        tfb = []
                        for i in range(2):
                            v, r = wcar.take(4 * 512, f"tf{i}")
                            tfb.append((v.rearrange("p (g n) -> p g n", g=4), r))
                        load_cast(w_in[l], 8, 512, wfu3, r_wfu, col0=C_FU)
                        load_cast(w_in[l], 8, 512, wfz3, r_wfz, col0=C_FZ)
                    LC = L // 128
                    AB = ABfull[:, 0:LC * 1024]
                    AB5 = AB.rearrange("p (lc g cs c) -> p lc g cs c", lc=LC, g=4, cs=2)
                    AB3 = AB.rearrange("p (lc x) -> p lc x", lc=LC)
                    LG = LGL if sname == "lat" else 2
                    dsl = [(v_[:, 0:LG * 2 * TN].rearrange("p (a cs n) -> p a cs n", a=LG, cs=2), r_) for (v_, r_) in dslfull]
                    ntile = L // TN
                    load_h(b, soff, 0, TN, 0)
                    for tix in range(ntile):
                        t0 = tix * TN
                        i = tix % 2
                        if tix + 1 < ntile:
                            load_h(b, soff, t0 + TN, TN, (tix + 1) % 2)
                        ut, r_ut = UT[i]
                        for g in range(4):
                            ps, r_ps = inproj(ht[i], r_ht[i], TN, wfu3, r_wfu, g)
                            S.op("act", lambda e: e.activation(out=ut[:, g, 0:TN], in_=ps[:, 0:TN], func=AF.Identity,
                                                               bias=vcol(l, V_BIN, C_FU // 128 + g), scale=1.0),
                                 reads=[r_ps, r_vecs], writes=[r_ut] if g == 0 else (), partial=[r_ut] if g else ())
                        for st in range(TN // 128):
                            lc = (t0 // 128) + st
                            for half in range(2):
                                ps, r_ps = psum()
                                for gg in range(2):
                                    g = half * 2 + gg
                                    S.op("pe", lambda e: e.matmul(ps[:, gg * 256:(gg + 1) * 256], ut[:, g, st * 128:(st + 1) * 128], csc,
                                                                  start=True, stop=True),
                                         reads=[r_ut, r_consts], writes=[r_ps] if gg == 0 else (), partial=[r_ps] if gg else (),
                                         inc=(gg == 1))
                                dst = AB3[:, lc, half * 512:(half + 1) * 512]
                                if rr["evac"] % 2 == 0:
                                    S.op("act", lambda e: e.activation(out=dst, in_=ps[:, :], func=AF.Copy), reads=[r_ps], partial=[r_AB])
                                else:
                                    S.op("dve", lambda e: e.tensor_copy(out=dst, in_=ps[:, :]), reads=[r_ps], partial=[r_AB])
                                rr["evac"] += 1
                    dsrc = dftl_in if sname == "lat" else dftc_in
                    nlg = LC // LG
                    pieces = [(tix_, lg_) for tix_ in range(ntile) for lg_ in range(nlg)]
                    dbase = di_glob[0]

                    def dft_load(n_):
                        tix_, lg_ = pieces[n_]
                        sl_ = (dbase + n_) % 4
                        S.dma(dsl[sl_][0], dsrc[tix_, lg_], f"d_dft{sl_}", writes=[dsl[sl_][1]])
                    for n_ in range(min(3, len(pieces))):
                        dft_load(n_)
                    load_h(b, soff, 0, TN, 0)
                    for tix in range(ntile):
                        t0 = tix * TN
                        i = tix % 2
                        if tix + 1 < ntile:
                            load_h(b, soff, t0 + TN, TN, (tix + 1) % 2)
                        accs = [psum("acc") for _ in range(4)]
                        for lg in range(nlg):
                            n_ = tix * nlg + lg
                            if n_ + 3 < len(pieces):
                                dft_load(n_ + 3)
                            dv, r_dv = dsl[(dbase + n_) % 4]
                            for a in range(LG):
                                lc = lg * LG + a
                                for cs in range(2):
                                    first = (lc == 0 and cs == 0)
                                    lastm = (lc == LC - 1 and cs == 1)
                                    for g in range(4):
                                        ps, r_ps = accs[g]
                                        S.op("pe", lambda e: e.matmul(ps[:, 0:TN], AB5[:, lc, g, cs, :], dv[:, a, cs, 0:TN],
                                                                      start=first, stop=lastm),
                                             reads=[r_AB, r_dv], writes=[r_ps] if first else (), partial=() if first else [r_ps],
                                             inc=(lastm or (a == LG - 1 and cs == 1 and g == 3)))
                        tf, r_tf = tfb[i]
                        for g in range(4):
                            ps, r_ps = inproj(ht[i], r_ht[i], TN, wfz3, r_wfz, g)
                            S.op("act", lambda e: e.activation(out=sz3[:, g, 0:TN], in_=ps[:, 0:TN], func=AF.Silu,
                                                               bias=vcol(l, V_BIN, C_FZ // 128 + g), scale=1.0),
                                 reads=[r_ps, r_vecs], writes=[r_sz] if g == 0 else (), partial=[r_sz] if g else ())
                            psY, r_psY = accs[g]
                            S.op("dve", lambda e: e.tensor_tensor(out=tf[:, g, 0:TN], in0=psY[:, 0:TN], in1=sz3[:, g, 0:TN], op=ALU.mult),
                                 reads=[r_psY, r_sz], writes=[r_tf] if g == 0 else (), partial=[r_tf] if g else ())
                        S.dma(tT[b, 1].rearrange("(k p) n -> p k n", p=128)[:, :, soff + t0:soff + t0 + TN], tf[:, :, 0:TN],
                              f"d_tf{i}", reads=[r_tf], writes=[hres("t", b, 1, soff + t0)])
                    di_glob[0] += len(pieces)

            for b in range(NB):
                mcols = {"ctx": 2, "lat": b}
                act_seqs = seqs
                for si_, (sname, soff, L, TN, _) in enumerate(act_seqs if not last else act_seqs[1:]):
                    if b == 0 and si_ == 0:
                        S.barrier()
                        ps_pool["gen"] = list(range(7))
                        car = Carver(arena, ARN)
                        wc, _ = car.take(8 * 2048, "wconv"); r_wc = WSet(16, "wconv"); wc3 = wc.rearrange("p (k n) -> p k n", k=8)
                        ubfull, r_u = car.take(4 * (seq + 2), "u")
                        wcar = Carver(work, WORKN)
                        tab = []
                        for i in range(2):
                            v, r = wcar.take(4 * 512, f"ta{i}")
                            tab.append((v.rearrange("p (g n) -> p g n", g=4), r))
                        f32a, r_f32a = carve_f32a(wcar, 6)
                        for p_ in range(2):
                            load_cast(w_in[l], 8, 256, wc3, r_wc, col0=C_AX + p_ * 256, dcol0=C_AX + p_ * 256)
                            load_cast(w_in[l], 8, 256, wc3, r_wc, col0=C_AC + p_ * 256, dcol0=C_AC + p_ * 256)
                        load_cast(w_in[l], 8, 512, wc3, r_wc, col0=C_AZ, dcol0=C_AZ)
                        load_cast(w_in[l], 8, 512, wc3, r_wc, col0=C_AB, dcol0=C_AB)
                    u3 = ubfull[:, 0:4 * (L + 2)].rearrange("p (c n) -> p c n", c=4)
                    S.op("pool", lambda e: e.memset(u3[:, :, 0:1], 0.0), partial=[r_u])
                    S.op("pool", lambda e: e.memset(u3[:, :, L + 1:L + 2], 0.0), partial=[r_u])
                    ntile = L // TN
                    load_h(b, soff, 0, TN, 0)
                    for tix in range(ntile):
                        t0 = tix * TN
                        i = tix % 2
                        if tix + 1 < ntile:
                            load_h(b, soff, t0 + TN, TN, (tix + 1) % 2)
                        for c in range(4):
                            fi = c % 2
                            ps, r_ps = inproj(ht[i], r_ht[i], TN, wc3, r_wc, C_AX // 128 + c)
                            S.op("act", lambda e: e.activation(out=f32a[fi][:, 0:TN], in_=ps[:, 0:TN], func=AF.Identity,
                                                               bias=vcol(l, V_BIN, C_AX // 128 + c), scale=1.0),
                                 reads=[r_ps, r_vecs], writes=[r_f32a[fi]])
                            ps2, r_ps2 = inproj(ht[i], r_ht[i], TN, wc3, r_wc, C_AC // 128 + c)
                            S.op("dve", lambda e: e.scalar_tensor_tensor(out=u3[:, c, 1 + t0:1 + t0 + TN], in0=ps2[:, 0:TN],
                                                                         scalar=vcol(l, V_BIN, C_AC // 128 + c), in1=f32a[fi][:, 0:TN],
                                                                         op0=ALU.add, op1=ALU.mult),
                                 reads=[r_ps2, r_vecs, r_f32a[fi]], partial=[r_u])
                    load_h(b, soff, 0, TN, 0)
                    for tix in range(ntile):
                        t0 = tix * TN
                        i = tix % 2
                        if tix + 1 < ntile:
                            load_h(b, soff, t0 + TN, TN, (tix + 1) % 2)
                        ta, r_ta = tab[i]
                        for c in range(4):
                            A, rA = f32a[0 + 3 * (c % 2)], r_f32a[0 + 3 * (c % 2)]
                            Bz, rB = f32a[1 + 3 * (c % 2)], r_f32a[1 + 3 * (c % 2)]
                            Cg, rC = f32a[2 + 3 * (c % 2)], r_f32a[2 + 3 * (c % 2)]
                            S.op("dve", lambda e: e.tensor_scalar(out=A[:, 0:TN], in0=u3[:, c, t0:t0 + TN],
                                                                   scalar1=vcol(l, V_WC, c * 3 + 0), scalar2=None, op0=ALU.mult),
                                 reads=[r_u, r_vecs], writes=[rA])
                            S.op("dve", lambda e: e.scalar_tensor_tensor(out=A[:, 0:TN], in0=u3[:, c, t0 + 1:t0 + 1 + TN],
                                                                          scalar=vcol(l, V_WC, c * 3 + 1), in1=A[:, 0:TN],
                                                                          op0=ALU.mult, op1=ALU.add),
                                 reads=[r_u, r_vecs, rA], writes=[rA])
                            S.op("dve", lambda e: e.scalar_tensor_tensor(out=A[:, 0:TN], in0=u3[:, c, t0 + 2:t0 + 2 + TN],
                                                                          scalar=vcol(l, V_WC, c * 3 + 2), in1=A[:, 0:TN],
                                                                          op0=ALU.mult, op1=ALU.add),
                                 reads=[r_u, r_vecs, rA], writes=[rA])
                            psz, r_psz = inproj(ht[i], r_ht[i], TN, wc3, r_wc, C_AZ // 128 + c)
                            S.op("act", lambda e: e.activation(out=Bz[:, 0:TN], in_=psz[:, 0:TN], func=AF.Silu,
                                                               bias=vcol(l, V_BIN, C_AZ // 128 + c), scale=1.0),
                                 reads=[r_psz, r_vecs], writes=[rB])
                            psb, r_psb = inproj(ht[i], r_ht[i], TN, wc3, r_wc, C_AB // 128 + c)
                            S.op("dve", lambda e: e.scalar_tensor_tensor(out=Cg[:, 0:TN], in0=psb[:, 0:TN],
                                                                         scalar=vcol(l, V_BIN, C_AB // 128 + c), in1=Bz[:, 0:TN],
                                                                         op0=ALU.add, op1=ALU.mult),
                                 reads=[r_psb, r_vecs, rB], writes=[rC])
                            S.op("dve", lambda e: e.scalar_tensor_tensor(out=ta[:, c, 0:TN], in0=A[:, 0:TN],
                                                                         scalar=vcol(l, V_BC, c), in1=Cg[:, 0:TN],
                                                                         op0=ALU.add, op1=ALU.mult),
                                 reads=[rA, rC, r_vecs], writes=[r_ta] if c == 0 else (), partial=[r_ta] if c else ())
                        S.dma(tT[b, 0].rearrange("(k p) n -> p k n", p=128)[:, :, soff + t0:soff + t0 + TN], ta[:, :, 0:TN],
                              f"d_ta{i}", reads=[r_ta], writes=[hres("t", b, 0, soff + t0)])

            for b in range(NB):
                mcols = {"ctx": 2, "lat": b}
                act_seqs = seqs
                S.barrier()
                ps_pool["gen"] = [0, 1, 2, 3, 4]
                car = Carver(arena, ARN)
                wkv, _ = car.take(8 * 1024, "wkv"); r_wkv = WSet(8, "wkv"); wkv3 = wkv.rearrange("p (k n) -> p k n", k=8)
                wqz3, r_wqz = wkv3, r_wkv
                KT = {}
                VV = {}
                for (sname, soff, L, TN, _) in act_seqs:
                    v, r = car.take(4 * L, "KT" + sname)
                    KT[sname] = (v.rearrange("p (c n) -> p c n", c=4), r)
                    v, r = car.take((L // 128) * NH * 65, "V" + sname)
                    VV[sname] = (v.rearrange("p (t h d) -> p t h d", t=L // 128, h=NH), r)
                key0 = geo["order"][0]
                nres = len(geo["pats"][key0])
                bres, r_bres = car.take(nres * NH * 128, "bres"); bres3 = bres.rearrange("p (t k) -> p t k", k=128)
                bdyn, r_bdyn = car.take(5 * NH * 128, "bdyn"); bdyn3 = bdyn.rearrange("p (t k) -> p t k", k=128)
                wcar = Carver(work, WORKN)
                qTb, r_qT = wcar.take(4 * 512, "qT"); qT3 = qTb.rearrange("p (c n) -> p c n", c=4)
                PTs = []
                for i in range(3):
                    v, r = wcar.take(7 * 128, f"PT{i}")
                    PTs.append((v, r))
                Ons = []
                for i in range(2):
                    v, r = wcar.take(512, f"On{i}")
                    Ons.append((v, r))
                tcb = []
                for i in range(2):
                    v, r = wcar.take(4 * 512, f"tc{i}")
                    tcb.append((v.rearrange("p (g n) -> p g n", g=4), r))
                f32a, r_f32a = carve_f32a(wcar, 5)
                scz = [f32a[0], f32a[1], f32a[2], f32a[3]]; r_scz = r_f32a[0:4]
                rec, r_rec = f32a[4], r_f32a[4]
                load_cast(w_in[l], 8, 1024, wkv3, r_wkv, col0=C_K)

                def load_bias(tab0, ntabs, dst3, r_dst):
                    tot = ntabs * NH
                    c = 0
                    while c < tot:
                        w = min(16, tot - c)
                        i = rr["wst"] % 4
                        rr["wst"] += 1
                        stg = wst[i][:, 0:w * 128].rearrange("p (t k) -> p t k", k=128)
                        S.dma(stg, bias_in[l, tab0 * NH + c:tab0 * NH + c + w].rearrange("t q k -> q t k"),
                              f"d_wst{i}", writes=[r_wst[i]])
                        S.op("act", lambda e: e.activation(out=dst3[:, c:c + w, :], in_=stg, func=AF.Exp),
                             reads=[r_wst[i]], partial=[r_dst])
                        c += w
                load_bias(geo["tab_base"][key0], nres, bres3, r_bres)
                for (sname, soff, L, TN, _) in act_seqs:
                    kt3, r_kt = KT[sname]
                    v4, r_v = VV[sname]
                    S.op("pool", lambda e: e.memset(v4[:, :, :, 64:65], 1.0), partial=[r_v])
                    ntile = L // TN
                    load_h(b, soff, 0, TN, 0)
                    for tix in range(ntile):
                        t0 = tix * TN
                        i = tix % 2
                        if tix + 1 < ntile:
                            load_h(b, soff, t0 + TN, TN, (tix + 1) % 2)
                        for c in range(4):
                            ps, r_ps = inproj(ht[i], r_ht[i], TN, wkv3, r_wkv, c)
                            S.op("act", lambda e: e.activation(out=kt3[:, c, t0:t0 + TN], in_=ps[:, 0:TN], func=AF.Identity,
                                                               bias=vcol(l, V_BIN, C_K // 128 + c), scale=1.0),
                                 reads=[r_ps, r_vecs], partial=[r_kt])
                        for st in range(TN // 128):
                            ps, r_ps = psum()
                            for kc in range(8):
                                S.op("pe", lambda e: e.matmul(ps[:, :], ht[i][:, kc, st * 128:(st + 1) * 128], wkv3[:, kc, 512:1024],
                                                              start=(kc == 0), stop=(kc == 7)),
                                     reads=r_wkv.chunks[4:8] + [r_ht[i]], writes=[r_ps] if kc == 0 else (), partial=[r_ps] if kc else (), inc=(kc == 7))
                            S.op("dve", lambda e: e.tensor_copy(out=v4[:, t0 // 128 + st, :, 0:64],
                                                                in_=ps[:, :].rearrange("p (h d) -> p h d", h=NH)),
                                 reads=[r_ps], partial=[r_v])
                load_cast(w_in[l], 8, 512, wqz3, r_wqz, col0=C_Q, dcol0=0)
                load_cast(w_in[l], 8, 512, wqz3, r_wqz, col0=C_CZ, dcol0=512)
                cur_dyn = [None]
                for (sname, soff, L, TN, _) in (act_seqs if not last else act_seqs[1:]):
                    ntile = L // TN
                    kc3, r_kc = KT["ctx"]
                    vc4, r_vc = VV["ctx"]
                    kl3, r_kl = KT[sname]
                    vl4, r_vl = VV[sname]
                    load_h(b, soff, 0, TN, 0)
                    sti0 = [0]
                    jobn = [0]
                    for tix in range(ntile):
                        t0 = tix * TN
                        i = tix % 2
                        if tix + 1 < ntile:
                            load_h(b, soff, t0 + TN, TN, (tix + 1) % 2)
                        for c in range(4):
                            ps, r_ps = inproj(ht[i], r_ht[i], TN, wqz3, r_wqz, c)
                            S.op("dve", lambda e: e.tensor_scalar(out=qT3[:, c, 0:TN], in0=ps[:, 0:TN],
                                                                  scalar1=vcol(l, V_BIN, C_Q // 128 + c), scalar2=HD ** -0.5,
                                                                  op0=ALU.add, op1=ALU.mult),
                                 reads=[r_ps, r_vecs], writes=[r_qT] if c == 0 else (), partial=[r_qT] if c else ())
                        for c in range(4):
                            ps, r_ps = inproj(ht[i], r_ht[i], TN, wqz3, r_wqz, 4 + c)
                            S.op("act", lambda e: e.activation(out=scz[c][:, 0:TN], in_=ps[:, 0:TN], func=AF.Silu,
                                                               bias=vcol(l, V_BIN, C_CZ // 128 + c), scale=1.0),
                                 reads=[r_ps, r_vecs], writes=[r_scz[c]])
                        tc, r_tc = tcb[i]
                        sub_state = {}

                        def sub_get(st, gsub):
                            if st in sub_state:
                                return sub_state[st]
                            chunks = []
                            if sname == "lat":
                                key, chl = geo["subs"][gsub]
                                if key == key0:
                                    bt3, r_bt = bres3, r_bres
                                else:
                                    bt3, r_bt = bdyn3, r_bdyn
                                for ci, ch in enumerate(chl):
                                    chunks.append((kl3, r_kl, vl4, r_vl, ch, (bt3, r_bt, ci)))
                            for ch in range(CTX // 128):
                                chunks.append((kc3, r_kc, vc4, r_vc, ch, None))
                            on, r_on = Ons[(sti0[0] + st) % 2]
                            sub_state[st] = dict(chunks=chunks, on=on, r_on=r_on, psO=None)
                            return sub_state[st]

                        def emit_scores(job):
                            st, h = job["st"], job["h"]
                            ss = sub_get(st, job["gsub"])
                            chunks = ss["chunks"]
                            nch = len(chunks)
                            c = h // 2
                            pb = (h % 2) * 64
                            banks = [psum() for _ in range((nch + 3) // 4)]
                            job["banks"] = banks
                            job["pt"] = PTs[jobn[0] % 3]
                            jobn[0] += 1
                            for ci, (k3, r_k, v4_, r_v_, ch, bt) in enumerate(chunks):
                                psS, r_psS = banks[ci // 4]
                                o = psS[:, (ci % 4) * 128:(ci % 4 + 1) * 128]
                                firstb = (ci % 4 == 0)
                                lastb = (ci % 4 == 3) or (ci == nch - 1)
                                S.op("pe", lambda e: e.matmul(o, k3[pb:pb + 64, c, ch * 128:(ch + 1) * 128],
                                                              qT3[pb:pb + 64, c, st * 128:(st + 1) * 128],
                                                              start=True, stop=True),
                                     reads=[r_k, r_qT], writes=[r_psS] if firstb else (), partial=() if firstb else [r_psS],
                                     inc=lastb)

                        def emit_exp(job):
                            ss = sub_get(job["st"], job["gsub"])
                            if sname == "lat":
                                key_, chl_ = geo["subs"][job["gsub"]]
                                if key_ != key0 and cur_dyn[0] != (key_,):
                                    load_bias(geo["tab_base"][key_], len(chl_), bdyn3, r_bdyn)
                                    cur_dyn[0] = (key_,)
                            nch = len(ss["chunks"])
                            pt, r_pt = job["pt"]
                            for bi, (psS, r_psS) in enumerate(job["banks"]):
                                n = min(4, nch - bi * 4) * 128
                                S.op("act", lambda e: e.activation(out=pt[:, bi * 512:bi * 512 + n], in_=psS[:, 0:n], func=AF.Exp),
                                     reads=[r_psS], writes=[r_pt] if bi == 0 else (), partial=[r_pt] if bi else ())
                            loc = [c_ for c_ in ss["chunks"] if c_[5] is not None]
                            if loc:
                                nl = len(loc)
                                bt3, r_bt = loc[0][5][0], loc[0][5][1]
                                h_ = job["h"]
                                S.op("dve", lambda e: e.tensor_tensor(out=pt[:, 0:nl * 128].rearrange("p (t q) -> p t q", t=nl),
                                                                      in0=pt[:, 0:nl * 128].rearrange("p (t q) -> p t q", t=nl),
                                                                      in1=bt3[:, h_ * nl:(h_ + 1) * nl, :], op=ALU.mult),
                                     reads=[r_pt, r_bt], writes=[r_pt])

                        def emit_pv(job):
                            st, h = job["st"], job["h"]
                            ss = sub_get(st, job["gsub"])
                            chunks = ss["chunks"]
                            nch = len(chunks)
                            hl, hg = h % 4, h // 4
                            pt, r_pt = job["pt"]
                            on, r_on = ss["on"], ss["r_on"]
                            if hl == 0:
                                ss["psO"] = psum("o")
                            psO, r_psO = ss["psO"]
                            for ci, (k3, r_k, v4_, r_v_, ch, bt) in enumerate(chunks):
                                S.op("pe", lambda e: e.matmul(psO[:, hl * 65:hl * 65 + 65], pt[:, ci * 128:(ci + 1) * 128],
                                                              v4_[:, ch, h, :], start=(ci == 0), stop=(ci == nch - 1)),
                                     reads=[r_pt, r_v_], writes=[r_psO] if (hl == 0 and ci == 0) else (),
                                     partial=() if (hl == 0 and ci == 0) else [r_psO], inc=(ci == nch - 1))
                            if hl == 3:
                                pso4 = psO[:, 0:260].rearrange("p (h d) -> p h d", h=4)
                                S.op("dve", lambda e: e.reciprocal(out=rec[:, hg * 4:hg * 4 + 4], in_=pso4[:, :, 64]),
                                     reads=[r_psO], writes=[r_rec] if hg == 0 else (), partial=[r_rec] if hg else ())
                                for hl2 in range(4):
                                    h2 = hg * 4 + hl2
                                    S.op("dve", lambda e: e.tensor_scalar(out=on[:, h2 * 64:(h2 + 1) * 64], in0=psO[:, hl2 * 65:hl2 * 65 + 64],
                                                                          scalar1=rec[:, h2:h2 + 1], scalar2=None, op0=ALU.mult),
                                         reads=[r_psO, r_rec], writes=[r_on] if h2 == 0 else (), partial=[r_on] if h2 else ())
                            if h == NH - 1:
                                for c in range(4):
                                    S.op("pe", lambda e: e.transpose(pst[:, c * 128:(c + 1) * 128], on[:, c * 128:(c + 1) * 128], ident),
                                         reads=[r_on, r_consts], writes=[r_pst] if c == 0 else (), partial=[r_pst] if c else (), inc=(c == 3))
                                for c in range(4):
                                    S.op("dve", lambda e: e.scalar_tensor_tensor(out=tc[:, c, st * 128:(st + 1) * 128],
                                                                                 in0=pst[:, c * 128:(c + 1) * 128],
                                                                                 scalar=vcol(l, V_BIN, C_V // 128 + c),
                                                                                 in1=scz[c][:, st * 128:(st + 1) * 128],
                                                                                 op0=ALU.add, op1=ALU.mult),
                                         reads=[r_pst, r_vecs, r_scz[c]],
                                         writes=[r_tc] if (st == 0 and c == 0) else (), partial=() if (st == 0 and c == 0) else [r_tc])

                        jobs = [dict(st=st, h=h, gsub=t0 // 128 + st) for st in range(TN // 128) for h in range(NH)]
                        emit_scores(jobs[0])
                        for ji, job in enumerate(jobs):
                            if ji + 1 < len(jobs):
                                emit_scores(jobs[ji + 1])
                            emit_exp(job)
                            if ji >= 1:
                                emit_pv(jobs[ji - 1])
                        emit_pv(jobs[-1])
                        sti0[0] += TN // 128
                        S.dma(tT[b, 2].rearrange("(k p) n -> p k n", p=128)[:, :, soff + t0:soff + t0 + TN], tc[:, :, 0:TN],
                              f"d_tc{i}", reads=[r_tc], writes=[hres("t", b, 2, soff + t0)])

            for b in range(NB):
                mcols = {"ctx": 2, "lat": b}
                act_seqs = seqs
                if b == 0:
                    S.barrier()
                    ps_pool["gen"] = list(range(7))
                    car = Carver(arena, ARN)
                    wg, _ = car.take(8 * 3072, "wg"); r_wg = WSet(24, "wg"); wg3 = wg.rearrange("p (k n) -> p k n", k=8)
                    wbr = []
                    for j in range(3):
                        v, r = car.take(4 * 1024, f"wbr{j}")
                        wbr.append((v.rearrange("p (k n) -> p k n", k=4), WSet(8, f"wbr{j}")))
                    wo, _ = car.take(8 * 1024, "wo"); r_wo = WSet(8, "wo"); wo3 = wo.rearrange("p (k n) -> p k n", k=8)
                    tin = []
                    for i in range(2):
                        row = []
                        for j in range(3):
                            v, r = car.take(4 * 512, f"tin{i}{j}")
                            row.append((v.rearrange("p (k n) -> p k n", k=4), r))
                        tin.append(row)
                    wcar = Carver(work, WORKN)
                    Gb, r_G = wcar.take(8 * 512, "G"); G3 = Gb.rearrange("p (k n) -> p k n", k=8)
                    f32a, r_f32a = carve_f32a(wcar, 3)
                    xq, r_xq = take_f32(wcar, 8 * 512, "xq")
                    xq3 = xq.rearrange("p (k n) -> p k n", k=8)
                    for j in range(3):
                        load_cast(w_br[l, j], 4, 512, wbr[j][0], wbr[j][1], col0=0, dcol0=0)
                    for mp in range(4):
                        for j in range(3):
                            load_cast(w_in[l], 8, 256, wg3, r_wg, col0=C_GA + j * 1024 + mp * 256, dcol0=j * 1024 + mp * 256)
                        if mp == 1:
                            for j in range(3):
                                load_cast(w_br[l, j], 4, 512, wbr[j][0], wbr[j][1], col0=512, dcol0=512)
                    load_cast(w_out[l], 8, 1024, wo3, r_wo)
                xi = 0
                for (sname, soff, L, TN, _) in (act_seqs if not last else act_seqs[1:]):
                    mc = mcols[sname]
                    ntile = L // TN

                    def load_tile(tix):
                        t0_ = tix * TN
                        i_ = tix % 2
                        load_h(b, soff, t0_, TN, i_)
                        for j in range(3):
                            S.dma(tin[i_][j][0][:, :, 0:TN],
                                  tT[b, j].rearrange("(k p) n -> p k n", p=128)[:, :, soff + t0_:soff + t0_ + TN],
                                  f"d_tin{i_}{j}", reads=[hres("t", b, j, soff + t0_)], writes=[tin[i_][j][1]])
                    load_tile(0)
                    for tix in range(ntile):
                        t0 = tix * TN
                        i = tix % 2
                        if tix + 1 < ntile:
                            load_tile(tix + 1)
                        S.dma(xq3[:, :, 0:TN], xsrc[b].rearrange("(k p) n -> p k n", p=128)[:, :, soff + t0:soff + t0 + TN],
                              "d_xq", reads=[hres("x", b, soff + t0)], writes=[r_xq])
                        for m in range(8):
                            for j in range(3):
                                tj, r_tj = tin[i][j]
                                wj, r_wj = wbr[j]
                                psy, r_psy = psum()
                                for kc in range(4):
                                    S.op("pe", lambda e: e.matmul(psy[:, 0:TN], wj[:, kc, m * 128:(m + 1) * 128], tj[:, kc, 0:TN],
                                                                  start=(kc == 0), stop=(kc == 3)),
                                         reads=[r_wj.chunks[m], r_tj], writes=[r_psy] if kc == 0 else (), partial=[r_psy] if kc else (), inc=(kc == 3))
                                psg, r_psg = inproj(ht[i], r_ht[i], TN, wg3, r_wg, j * 8 + m)
                                sg, r_sg = f32a[j], r_f32a[j]
                                S.op("act", lambda e: e.activation(out=sg[:, 0:TN], in_=psg[:, 0:TN], func=AF.Sigmoid,
                                                                   bias=vcol(l, V_BIN, C_GA // 128 + j * 8 + m), scale=1.0),
                                     reads=[r_psg, r_vecs], writes=[r_sg])
                                S.op("dve", lambda e: e.tensor_tensor(out=sg[:, 0:TN], in0=psy[:, 0:TN], in1=sg[:, 0:TN], op=ALU.mult),
                                     reads=[r_psy, r_sg], writes=[r_sg])
                            S.op("pool", lambda e: e.tensor_tensor(out=f32a[0][:, 0:TN], in0=f32a[0][:, 0:TN], in1=f32a[1][:, 0:TN], op=ALU.add),
                                 reads=[r_f32a[0], r_f32a[1]], writes=[r_f32a[0]])
                            S.op("pool", lambda e: e.tensor_tensor(out=G3[:, m, 0:TN], in0=f32a[0][:, 0:TN], in1=f32a[2][:, 0:TN], op=ALU.add),
                                 reads=[r_f32a[0], r_f32a[2]], writes=[r_G] if m == 0 else (), partial=[r_G] if m else ())
                        for mo in range(8):
                            xc = xq3[:, mo, 0:TN]
                            pso, r_pso = psum()
                            for kc in range(8):
                                S.op("pe", lambda e: e.matmul(pso[:, 0:TN], wo3[:, kc, mo * 128:(mo + 1) * 128], G3[:, kc, 0:TN],
                                                              start=(kc == 0), stop=(kc == 7)),
                                     reads=[r_wo.chunks[mo], r_G], writes=[r_pso] if kc == 0 else (), partial=[r_pso] if kc else (), inc=(kc == 7))
                            S.op("dve", lambda e: e.scalar_tensor_tensor(out=xc, in0=pso[:, 0:TN], scalar=mod[:, 16 + mo, mc:mc + 1],
                                                                         in1=xc, op0=ALU.mult, op1=ALU.add),
                                 reads=[r_pso, r_mod, r_xq], partial=[r_xq])
                        S.dma(xs[b].rearrange("(k p) n -> p k n", p=128)[:, :, soff + t0:soff + t0 + TN], xq3[:, :, 0:TN],
                              "d_xq", reads=[r_xq], writes=[hres("xn", b, soff + t0)])
                for (sname, soff, L, TN, _) in (act_seqs if not last else act_seqs[1:]):
                    for t0 in range(0, L, TN):
                        hbm_res[("x", b, soff + t0)] = hres("xn", b, soff + t0)
                        del hbm_res[("xn", b, soff + t0)]

            for b in range(NB):
                if last:
                    S.barrier()
                    ps_pool["gen"] = list(range(7))
                    car = Carver(arena, ARN)
                    sq, r_sq = car.take(8 * 512, "sq")
                    sq3 = sq.rearrange("p (k n) -> p k n", k=8)
                    xt, r_xt = carve_xt(car)
                    (sname, soff, L, TN, _) = seqs[1]
                    fgc = NVL * depth
                    for tix in range(L // TN):
                        t0 = tix * TN
                        i = tix % 2
                        S.dma(xt[i][:, :, 0:TN], xs[b].rearrange("(k p) n -> p k n", p=128)[:, :, soff + t0:soff + t0 + TN],
                              f"d_xt{i}", reads=[hres("x", b, soff + t0)], writes=[r_xt[i]])
                        S.op("act", lambda e: e.activation(out=sq3[:, :, 0:TN], in_=xt[i][:, :, 0:TN], func=AF.Square),
                             reads=[r_xt[i]], writes=[r_sq])
                        ps, r_ps = psum()
                        for kc in range(8):
                            S.op("pe", lambda e: e.matmul(ps[:, 0:TN], ones, sq3[:, kc, 0:TN], start=(kc == 0), stop=(kc == 7)),
                                 reads=[r_sq, r_consts], writes=[r_ps] if kc == 0 else (), partial=[r_ps] if kc else (), inc=(kc == 7))
                        S.op("act", lambda e: e.activation(out=rstd[i][:, 0:TN], in_=ps[:, 0:TN], func=AF.Sqrt, bias=epsc[:, 0:1], scale=1.0 / D),
                             reads=[r_ps, r_consts], writes=[r_rstd[i]])
                        S.op("dve", lambda e: e.reciprocal(out=rstd[i][:, 0:TN], in_=rstd[i][:, 0:TN]),
                             reads=[r_rstd[i]], writes=[r_rstd[i]])
                        for kc in range(8):
                            S.op("dve",
                                 lambda e: e.scalar_tensor_tensor(out=xt[i][:, kc, 0:TN], in0=xt[i][:, kc, 0:TN],
                                                                  scalar=vecs[:, fgc + kc:fgc + kc + 1], in1=rstd[i][:, 0:TN],
                                                                  op0=ALU.mult, op1=ALU.mult),
                                 reads=[r_vecs, r_rstd[i], r_xt[i]], writes=[r_xt[i]])
                        S.dma(y_out[b].rearrange("(k p) n -> p k n", p=128)[:, :, t0:t0 + TN], xt[i][:, :, 0:TN],
                              f"d_xt{i}", reads=[r_xt[i]], writes=[hres("y", b, t0)])
        S.barrier(engines=("sp",))
        print("program built: ninst", S.ninst, "nwaits", S.nwaits, "nsems", len(S.sems))
    return nc


_CACHE = {}


def _consts(seq):
    key = ("c", seq)
    if key not in _CACHE:
        cst = np.zeros((128, 512), dtype=NPBF)
        cst[:, 0:128] = np.eye(128).astype(NPBF)
        cst[:, 128:256] = np.ones((128, 128)).astype(NPBF)
        cst[:, 256:512] = _chan_table().astype(NPBF)
        dftl = _dft_pos_table(seq, 512, 4)
        dftc = _dft_pos_table(CTX, 256, 2)
        _CACHE[key] = (cst, dftl, dftc)
    return _CACHE[key]


def kernel(x, c, ctx, c_ctx, norm_g, w_ada, b_ada, w_in, b_in, w_conv, b_conv,
           rpb, w_br_a, w_br_f, w_br_c, w_out, final_g):
    x = np.asarray(x, dtype=np.float32)
    depth = int(np.asarray(w_in).shape[0])
    seq = int(x.shape[1])
    B = int(x.shape[0])
    assert B == NB * NCORES
    geo = _attn_geometry(seq)
    cst, dftl, dftc = _consts(seq)
    f = lambda a: np.ascontiguousarray(np.asarray(a, dtype=np.float32))
    c, ctx, c_ctx = f(c), f(ctx), f(c_ctx)
    norm_g, b_ada, b_in, w_conv, b_conv, final_g = f(norm_g), f(b_ada), f(b_in), f(w_conv), f(b_conv), f(final_g)
    w_ada, w_in, w_out = f(w_ada), f(w_in), f(w_out)
    w_br = np.ascontiguousarray(np.stack([f(w_br_a), f(w_br_f), f(w_br_c)], axis=1))
    rpb = f(rpb)
    NV = NVL * depth + 8
    vecs = np.zeros((128, NV), dtype=np.float32)
    for l in range(depth):
        o = l * NVL
        vecs[:, o + V_NG:o + V_NG + 8] = norm_g[l].reshape(8, 128).T
        vecs[:, o + V_BADA:o + V_BADA + 24] = b_ada[l].reshape(24, 128).T
        vecs[:, o + V_BIN:o + V_BIN + 64] = b_in[l].reshape(64, 128).T
        vecs[:, o + V_WC:o + V_WC + 12] = w_conv[l].reshape(3, 4, 128).transpose(2, 1, 0).reshape(128, 12)
        vecs[:, o + V_BC:o + V_BC + 4] = b_conv[l].reshape(4, 128).T
    vecs[:, NVL * depth:NVL * depth + 8] = final_g.reshape(8, 128).T
    bias_tab = np.stack([_bias_tables(rpb[l], geo) for l in range(depth)], axis=0)
    xall = np.concatenate([ctx.transpose(0, 2, 1), x.transpose(0, 2, 1)], axis=2)
    key = ("nc", seq, depth)
    if key not in _CACHE:
        _CACHE[key] = build_program(seq, depth)
    nc = _CACHE[key]
    in_maps = []
    for core in range(NCORES):
        b0 = core * NB
        cT = np.stack([c[b0], c[b0 + 1], c_ctx], axis=1)
        cT = np.ascontiguousarray(cT.reshape(8, 128, 3).transpose(1, 0, 2))
        in_maps.append({
            "x_in": np.ascontiguousarray(xall[b0:b0 + NB]),
            "cT": cT, "vecs": vecs, "w_ada": w_ada, "w_in": w_in, "w_br": w_br, "w_out": w_out,
            "bias_tab": bias_tab, "consts": cst, "dftl": dftl, "dftc": dftc,
        })
    res = run_bass_kernel_spmd(nc, in_maps, core_ids=list(range(NCORES)))
    ys = [np.asarray(r["y"], dtype=np.float32) for r in res.results]
    y = np.concatenate(ys, axis=0)
    return np.ascontiguousarray(y.transpose(0, 2, 1))
```
